# Optimizing a Trainium2 kernel written in Bass

```python
import math
import jax, jax.numpy as jnp
from jax import lax
import numpy as np

D_MODEL = 1024
BATCH = 8
SEQ = 2048
DEPTH = 2
DEC_BATCH = 128
DEC_SEQ = 8
PAST_LEN = 16384
PAGE_SIZE = 128

HEAD_DIM = 64
ROPE_THETA = 10000.0
BLOCK = 128
EPS = 1e-6
NEG_INF = -1e30
A_HEADS = 4
A_GROUPS = ((128, 1), (512, 4), (2048, 16))
A_NG = len(A_GROUPS)
A_QKV = A_NG * A_HEADS * HEAD_DIM
B_Q_HEADS = 8
B_KV_HEADS = 2
B_GROUP = B_Q_HEADS // B_KV_HEADS
B_WINDOW = 128
C_WIDTH = 256
C_CONV = 3
D_GROUP = 16
D_NGROUPS = 16
D_WIDTH = D_GROUP * D_NGROUPS
D_STATE = 64
N_BRANCH = 4
D_FF = -(-(-(-8 * D_MODEL // 3)) // 256) * 256
PLE_DIM = 256
IN_SPLITS = (A_QKV, A_QKV, A_QKV, B_Q_HEADS * HEAD_DIM, B_KV_HEADS * HEAD_DIM, B_KV_HEADS * HEAD_DIM,
             C_WIDTH, C_WIDTH, C_WIDTH, D_WIDTH, N_BRANCH * D_MODEL)
IN_WIDTH = sum(IN_SPLITS)
SPLIT_POINTS = tuple(sum(IN_SPLITS[:i + 1]) for i in range(len(IN_SPLITS) - 1))
N_STATE = 11

kernel_name = 'hybrid_gated_dilated_swa_conv_s5_decode_step'


def rmsnorm(x, g):
    xf = x.astype(jnp.float32)
    y = xf * lax.rsqrt(jnp.mean(xf * xf, axis=-1, keepdims=True) + EPS)
    return (y * g.astype(jnp.float32)).astype(x.dtype)


def rope(x, pos):
    half = HEAD_DIM // 2
    inv = ROPE_THETA ** (-jnp.arange(half, dtype=jnp.float32) / half)
    ang = pos.astype(jnp.float32)[:, None] * inv[None, :]
    shape = (1, pos.shape[0]) + (1,) * (x.ndim - 3) + (half,)
    cos = jnp.cos(ang).reshape(shape)
    sin = jnp.sin(ang).reshape(shape)
    xf = x.astype(jnp.float32)
    x1, x2 = xf[..., :half], xf[..., half:]
    return jnp.concatenate([x1 * cos - x2 * sin, x2 * cos + x1 * sin], axis=-1).astype(x.dtype)


def last_rows(x, n):
    L = x.shape[1]
    if L >= n:
        return x[:, L - n:]
    return jnp.pad(x, ((0, 0), (n - L, 0)) + ((0, 0),) * (x.ndim - 2))


def attend(q, k, v, mask, sinks=None):
    s = jnp.einsum('...qhgd,...khd->...hgqk', q.astype(jnp.float32), k.astype(jnp.float32)) * (HEAD_DIM ** -0.5)
    s = jnp.where(mask, s, NEG_INF)
    m = jnp.max(s, axis=-1)
    if sinks is not None:
        sk = sinks.astype(jnp.float32)[:, :, None]
        m = jnp.maximum(m, sk)
    p = jnp.exp(s - m[..., None])
    den = jnp.sum(p, axis=-1)
    if sinks is not None:
        den = den + jnp.exp(sk - m)
    o = jnp.einsum('...hgqk,...khd->...qhgd', p, v.astype(jnp.float32))
    den_q = jnp.moveaxis(den, -1, -3)
    lse = jnp.moveaxis(m, -1, -3) + jnp.log(den_q)
    return o / den_q[..., None], lse


def banded_attention(q, k, v, window, sinks=None):
    bsz, L = q.shape[0], q.shape[1]
    blk = min(BLOCK, L)
    nb = -(-L // blk)
    Lp = nb * blk
    pad = -(-window // blk) * blk
    qp = jnp.pad(q, ((0, 0), (0, Lp - L)) + ((0, 0),) * (q.ndim - 2))
    kp = jnp.pad(k, ((0, 0), (pad, Lp - L), (0, 0), (0, 0)))
    vp = jnp.pad(v, ((0, 0), (pad, Lp - L), (0, 0), (0, 0)))
    kidx = (jnp.arange(nb) * blk)[:, None] + jnp.arange(pad + blk)[None, :]
    kb = kp[:, kidx]
    vb = vp[:, kidx]
    qb = qp.reshape((bsz, nb, blk) + q.shape[2:])
    dist = jnp.arange(blk)[:, None] + pad - jnp.arange(pad + blk)[None, :]
    mask = ((dist >= 0) & (dist <= window))[None] & ((kidx - pad) >= 0)[:, None, :]
    o, lse = attend(qb, kb, vb, mask[:, None, None], sinks)
    o = o.reshape((bsz, Lp) + q.shape[2:])[:, :L]
    lse = lse.reshape((bsz, Lp) + q.shape[2:-1])[:, :L]
    return o, lse


def dilated_prompt(q, k, v, w, d):
    bsz, L, H, dh = q.shape
    Ls = L // d

    def strided(t):
        return t.reshape(bsz, Ls, d, H, dh).transpose(0, 2, 1, 3, 4).reshape(bsz * d, Ls, H, dh)

    o, lse = banded_attention(strided(q)[:, :, :, None], strided(k), strided(v), w // d)
    o = o.reshape(bsz, d, Ls, H, dh).transpose(0, 2, 1, 3, 4).reshape(bsz, L, H, dh)
    lse = lse.reshape(bsz, d, Ls, H).transpose(0, 2, 1, 3).reshape(bsz, L, H)
    return o, lse


def dilated_sample(q, fk, fv, w, d):
    T = q.shape[1]
    lb = fk.shape[1] - T
    nk = w // d + 1
    idx = lb + jnp.arange(T)[:, None] - d * jnp.arange(nk)[None, :]
    valid = idx >= 0
    idx = jnp.maximum(idx, 0)
    kg = fk[:, idx]
    vg = fv[:, idx]
    o, lse = attend(q[:, :, None, :, None, :], kg, vg, valid[:, None, None, None, :])
    bsz, _, H, dh = q.shape
    return o.reshape(bsz, T, H, dh), lse.reshape(bsz, T, H)


def window_sample(q, fk, fv, window, sinks):
    T = q.shape[1]
    K = fk.shape[1]
    dist = (K - T) + jnp.arange(T)[:, None] - jnp.arange(K)[None, :]
    mask = (dist >= 0) & (dist <= window)
    o, _ = attend(q, fk, fv, mask, sinks)
    return o


def short_conv(hc, bc, cc, conv_w, prev):
    L = hc.shape[1]
    zc = cc * hc
    zp = jnp.concatenate([prev.astype(zc.dtype), zc], axis=1)
    y = conv_w[0] * zp[:, 0:L]
    for j in range(1, C_CONV):
        y = y + conv_w[j] * zp[:, j:j + L]
    return bc * y, zp[:, zp.shape[1] - (C_CONV - 1):]


def _ssm_combine(e1, e2):
    a1, b1 = e1
    a2, b2 = e2
    return a1 * a2, a2 * b1 + b2


def ssm_mixer(u, lp, x0_re, x0_im):
    f32 = jnp.float32
    bsz, L, _ = u.shape
    uf = u.astype(f32).reshape(bsz, L, D_NGROUPS, D_GROUP)
    lam = lax.complex(lp['ssm_lam_re'].astype(f32), lp['ssm_lam_im'].astype(f32))
    dt = jnp.exp(lp['ssm_log_dt'].astype(f32))[:, None]
    a_bar = jnp.exp(lam * dt)
    b_mat = lax.complex(lp['ssm_b_re'].astype(f32), lp['ssm_b_im'].astype(f32))
    c_mat = lax.complex(lp['ssm_c_re'].astype(f32), lp['ssm_c_im'].astype(f32))
    b_bar = ((a_bar - 1.0) / lam)[..., None] * b_mat
    bu = jnp.einsum('blgi,gni->blgn', uf.astype(jnp.complex64), b_bar)
    x0 = lax.complex(x0_re.astype(f32), x0_im.astype(f32))
    bu = bu.at[:, 0].add(a_bar * x0)
    a_seq = jnp.broadcast_to(a_bar, bu.shape)
    _, xs = lax.associative_scan(_ssm_combine, (a_seq, bu), axis=1)
    y = jnp.einsum('blgn,gin->blgi', xs, c_mat).real + lp['ssm_d'].astype(f32) * uf
    y = jax.nn.gelu(y.reshape(bsz, L, D_WIDTH)) @ lp['w_d_glu'].astype(f32)
    y = y[..., :D_WIDTH] * jax.nn.sigmoid(y[..., D_WIDTH:])
    return y.astype(u.dtype), jnp.real(xs[:, -1]), jnp.imag(xs[:, -1])


def token_mix(h, pos, lp, cache):
    bsz, L, _ = h.shape
    z = h @ lp['w_in']
    qa, ka, va, qb, kb, vb, hc, bc, cc, ud, gl = jnp.split(z, SPLIT_POINTS, axis=-1)
    a_shape = (bsz, L, A_NG, A_HEADS, HEAD_DIM)
    qa = rope(qa.reshape(a_shape), pos)
    ka = rope(ka.reshape(a_shape), pos)
    va = va.reshape(a_shape)
    qb = rope(qb.reshape(bsz, L, B_KV_HEADS, B_GROUP, HEAD_DIM), pos)
    kb = rope(kb.reshape(bsz, L, B_KV_HEADS, HEAD_DIM), pos)
    vb = vb.reshape(bsz, L, B_KV_HEADS, HEAD_DIM)
    new = []
    outs, lses = [], []
    for g, (w, d) in enumerate(A_GROUPS):
        lb = min(w, PAST_LEN)
        if cache is None:
            fk, fv = ka[:, :, g], va[:, :, g]
            o, lse = dilated_prompt(qa[:, :, g], fk, fv, w, d)
        else:
            fk = jnp.concatenate([cache[2 * g].astype(ka.dtype), ka[:, :, g]], axis=1)
            fv = jnp.concatenate([cache[2 * g + 1].astype(va.dtype), va[:, :, g]], axis=1)
            o, lse = dilated_sample(qa[:, :, g], fk, fv, w, d)
        outs.append(o)
        lses.append(lse)
        new += [last_rows(fk, lb), last_rows(fv, lb)]
    alpha = jax.nn.softmax(jnp.stack(lses), axis=0)
    o_a = jnp.sum(alpha[..., None] * jnp.stack(outs), axis=0).reshape(bsz, L, A_HEADS * HEAD_DIM).astype(h.dtype)
    lbb = min(B_WINDOW, PAST_LEN)
    if cache is None:
        fk, fv = kb, vb
        o_b, _ = banded_attention(qb, kb, vb, B_WINDOW, lp['attn_sinks'])
    else:
        fk = jnp.concatenate([cache[6].astype(kb.dtype), kb], axis=1)
        fv = jnp.concatenate([cache[7].astype(vb.dtype), vb], axis=1)
        o_b = window_sample(qb, fk, fv, B_WINDOW, lp['attn_sinks'])
    new += [last_rows(fk, lbb), last_rows(fv, lbb)]
    o_b = o_b.reshape(bsz, L, B_Q_HEADS * HEAD_DIM).astype(h.dtype)
    if cache is None:
        prev = jnp.zeros((bsz, C_CONV - 1, C_WIDTH), h.dtype)
    else:
        prev = cache[8]
    o_c, conv_state = short_conv(hc, bc, cc, lp['conv_c_w'], prev)
    new.append(conv_state)
    if cache is None:
        x0_re = jnp.zeros((bsz, D_NGROUPS, D_STATE), jnp.float32)
        x0_im = jnp.zeros((bsz, D_NGROUPS, D_STATE), jnp.float32)
    else:
        x0_re, x0_im = cache[9], cache[10]
    o_d, s_re, s_im = ssm_mixer(ud, lp, x0_re, x0_im)
    new += [s_re, s_im]
    gate = jax.nn.sigmoid(gl.reshape(bsz, L, N_BRANCH, D_MODEL))
    merged = (gate[:, :, 0] * (o_a @ lp['w_br_a']) + gate[:, :, 1] * (o_b @ lp['w_br_b'])
              + gate[:, :, 2] * (o_c @ lp['w_br_c']) + gate[:, :, 3] * (o_d @ lp['w_br_d']))
    return merged @ lp['w_out'], new


def layer(x, p, pos, lp, cache):
    h = rmsnorm(x, lp['norm_mix_pre'])
    mix, new = token_mix(h, pos, lp, cache)
    x = x + rmsnorm(mix, lp['norm_mix_post'])
    h = rmsnorm(x, lp['norm_ffn_pre'])
    f = (jax.nn.silu(h @ lp['w_ffn_gate']) * (h @ lp['w_ffn_up'])) @ lp['w_ffn_down']
    x = x + rmsnorm(f, lp['norm_ffn_post'])
    x = x + jax.nn.sigmoid(x @ lp['w_ple_gate']) * (p @ lp['w_ple'])
    return x, new


def setup_inputs(seed: int = 0) -> dict:
    key = jax.random.key(seed)
    ks = jax.random.split(key, 48)
    f32 = jnp.float32

    def nrm(i, shape, scale=1.0):
        return jax.random.normal(ks[i], shape, f32) * scale

    a_lb = [min(w, PAST_LEN) for (w, _) in A_GROUPS]
    lbb = min(B_WINDOW, PAST_LEN)
    n_idx = jnp.arange(D_STATE, dtype=f32)
    ssm_shape = (DEPTH, D_NGROUPS, D_STATE)
    return {
        'x_prompt': nrm(0, (BATCH, SEQ, D_MODEL)),
        'x_sample': nrm(1, (DEC_BATCH, DEC_SEQ, D_MODEL)),
        'p_prompt': nrm(2, (DEPTH, BATCH, SEQ, PLE_DIM)),
        'p_sample': nrm(3, (DEPTH, DEC_BATCH, DEC_SEQ, PLE_DIM)),
        'cache_a1_k': nrm(4, (DEPTH, DEC_BATCH, a_lb[0], A_HEADS, HEAD_DIM)),
        'cache_a1_v': nrm(5, (DEPTH, DEC_BATCH, a_lb[0], A_HEADS, HEAD_DIM)),
        'cache_a2_k': nrm(6, (DEPTH, DEC_BATCH, a_lb[1], A_HEADS, HEAD_DIM)),
        'cache_a2_v': nrm(7, (DEPTH, DEC_BATCH, a_lb[1], A_HEADS, HEAD_DIM)),
        'cache_a3_k': nrm(8, (DEPTH, DEC_BATCH, a_lb[2], A_HEADS, HEAD_DIM)),
        'cache_a3_v': nrm(9, (DEPTH, DEC_BATCH, a_lb[2], A_HEADS, HEAD_DIM)),
        'cache_b_k': nrm(10, (DEPTH, DEC_BATCH, lbb, B_KV_HEADS, HEAD_DIM)),
        'cache_b_v': nrm(11, (DEPTH, DEC_BATCH, lbb, B_KV_HEADS, HEAD_DIM)),
        'state_c_conv': nrm(12, (DEPTH, DEC_BATCH, C_CONV - 1, C_WIDTH)),
        'state_d_re': nrm(13, (DEPTH, DEC_BATCH, D_NGROUPS, D_STATE), 0.1),
        'state_d_im': nrm(14, (DEPTH, DEC_BATCH, D_NGROUPS, D_STATE), 0.1),
        'norm_mix_pre': 1.0 + nrm(15, (DEPTH, D_MODEL), 0.05),
        'norm_mix_post': 1.0 + nrm(16, (DEPTH, D_MODEL), 0.05),
        'norm_ffn_pre': 1.0 + nrm(17, (DEPTH, D_MODEL), 0.05),
        'norm_ffn_post': 1.0 + nrm(18, (DEPTH, D_MODEL), 0.05),
        'w_in': nrm(19, (DEPTH, D_MODEL, IN_WIDTH), D_MODEL ** -0.5),
        'attn_sinks': nrm(20, (DEPTH, B_KV_HEADS, B_GROUP), 0.5),
        'conv_c_w': nrm(21, (DEPTH, C_CONV, C_WIDTH), C_CONV ** -0.5),
        'ssm_lam_re': -0.5 + nrm(22, ssm_shape, 0.01),
        'ssm_lam_im': math.pi * n_idx + nrm(23, ssm_shape, 0.01),
        'ssm_log_dt': jax.random.uniform(ks[24], (DEPTH, D_NGROUPS), f32, math.log(0.001), math.log(0.1)),
        'ssm_b_re': nrm(25, (DEPTH, D_NGROUPS, D_STATE, D_GROUP), (2 * D_GROUP) ** -0.5),
        'ssm_b_im': nrm(26, (DEPTH, D_NGROUPS, D_STATE, D_GROUP), (2 * D_GROUP) ** -0.5),
        'ssm_c_re': nrm(27, (DEPTH, D_NGROUPS, D_GROUP, D_STATE), D_STATE ** -0.5),
        'ssm_c_im': nrm(28, (DEPTH, D_NGROUPS, D_GROUP, D_STATE), D_STATE ** -0.5),
        'ssm_d': nrm(29, (DEPTH, D_NGROUPS, D_GROUP)),
        'w_d_glu': nrm(30, (DEPTH, D_WIDTH, 2 * D_WIDTH), D_WIDTH ** -0.5),
        'w_br_a': nrm(31, (DEPTH, A_HEADS * HEAD_DIM, D_MODEL), (A_HEADS * HEAD_DIM) ** -0.5),
        'w_br_b': nrm(32, (DEPTH, B_Q_HEADS * HEAD_DIM, D_MODEL), (B_Q_HEADS * HEAD_DIM) ** -0.5),
        'w_br_c': nrm(33, (DEPTH, C_WIDTH, D_MODEL), C_WIDTH ** -0.5),
        'w_br_d': nrm(34, (DEPTH, D_WIDTH, D_MODEL), D_WIDTH ** -0.5),
        'w_out': nrm(35, (DEPTH, D_MODEL, D_MODEL), D_MODEL ** -0.5),
        'w_ffn_gate': nrm(36, (DEPTH, D_MODEL, D_FF), D_MODEL ** -0.5),
        'w_ffn_up': nrm(37, (DEPTH, D_MODEL, D_FF), D_MODEL ** -0.5),
        'w_ffn_down': nrm(38, (DEPTH, D_FF, D_MODEL), D_FF ** -0.5),
        'w_ple': nrm(39, (DEPTH, PLE_DIM, D_MODEL), PLE_DIM ** -0.5),
        'w_ple_gate': nrm(40, (DEPTH, D_MODEL, D_MODEL), D_MODEL ** -0.5),
    }


def reference(x_prompt, x_sample, p_prompt, p_sample,
              cache_a1_k, cache_a1_v, cache_a2_k, cache_a2_v, cache_a3_k, cache_a3_v,
              cache_b_k, cache_b_v, state_c_conv, state_d_re, state_d_im,
              norm_mix_pre, norm_mix_post, norm_ffn_pre, norm_ffn_post,
              w_in, attn_sinks, conv_c_w, ssm_lam_re, ssm_lam_im, ssm_log_dt,
              ssm_b_re, ssm_b_im, ssm_c_re, ssm_c_im, ssm_d, w_d_glu,
              w_br_a, w_br_b, w_br_c, w_br_d, w_out,
              w_ffn_gate, w_ffn_up, w_ffn_down, w_ple, w_ple_gate):
    pos_p = jnp.arange(SEQ, dtype=jnp.int32)
    pos_s = PAST_LEN + jnp.arange(DEC_SEQ, dtype=jnp.int32)
    yp, ys = x_prompt, x_sample
    new_p, new_s = [], []
    for i in range(DEPTH):
        lp = {
            'norm_mix_pre': norm_mix_pre[i], 'norm_mix_post': norm_mix_post[i],
            'norm_ffn_pre': norm_ffn_pre[i], 'norm_ffn_post': norm_ffn_post[i],
            'w_in': w_in[i], 'attn_sinks': attn_sinks[i], 'conv_c_w': conv_c_w[i],
            'ssm_lam_re': ssm_lam_re[i], 'ssm_lam_im': ssm_lam_im[i], 'ssm_log_dt': ssm_log_dt[i],
            'ssm_b_re': ssm_b_re[i], 'ssm_b_im': ssm_b_im[i], 'ssm_c_re': ssm_c_re[i], 'ssm_c_im': ssm_c_im[i],
            'ssm_d': ssm_d[i], 'w_d_glu': w_d_glu[i],
            'w_br_a': w_br_a[i], 'w_br_b': w_br_b[i], 'w_br_c': w_br_c[i], 'w_br_d': w_br_d[i],
            'w_out': w_out[i], 'w_ffn_gate': w_ffn_gate[i], 'w_ffn_up': w_ffn_up[i],
            'w_ffn_down': w_ffn_down[i], 'w_ple': w_ple[i], 'w_ple_gate': w_ple_gate[i],
        }
        cache = (cache_a1_k[i], cache_a1_v[i], cache_a2_k[i], cache_a2_v[i], cache_a3_k[i], cache_a3_v[i],
                 cache_b_k[i], cache_b_v[i], state_c_conv[i], state_d_re[i], state_d_im[i])
        yp, st_p = layer(yp, p_prompt[i], pos_p, lp, None)
        new_p.append(st_p)
        ys, st_s = layer(ys, p_sample[i], pos_s, lp, cache)
        new_s.append(st_s)
    (a1k_p, a1v_p, a2k_p, a2v_p, a3k_p, a3v_p, bk_p, bv_p, conv_p, dre_p, dim_p) = [
        jnp.stack([st[j] for st in new_p]) for j in range(N_STATE)]
    (a1k_s, a1v_s, a2k_s, a2v_s, a3k_s, a3v_s, bk_s, bv_s, conv_s, dre_s, dim_s) = [
        jnp.stack([st[j] for st in new_s]) for j in range(N_STATE)]
    return (yp, ys,
            a1k_p, a1k_s, a1v_p, a1v_s, a2k_p, a2k_s, a2v_p, a2v_s, a3k_p, a3k_s, a3v_p, a3v_s,
            bk_p, bk_s, bv_p, bv_s, conv_p, conv_s, dre_p, dre_s, dim_p, dim_s)
```

```python
import types
import numpy as np
from contextlib import ExitStack
import concourse.bass as bass
import concourse.mybir as mybir
from concourse.bass_utils import run_bass_kernel_spmd

F32 = mybir.dt.float32
BF16 = mybir.dt.bfloat16
ALU = mybir.AluOpType
AF = mybir.ActivationFunctionType
AX = mybir.AxisListType

NCORES = 8
D = 1024
SEQ = 2048
NT = 17
NTOK = NT * 128
DEPTH = 2
DFF = 2816
NFF = DFF // 128
EPS = 1e-6
PAST = 16384
A_W = (128, 512, 2048)
GATE_OFF = 4096

import os
DBG_A_GROUPS = [int(x) for x in os.environ.get("DBG_A_GROUPS", "0,1,2").split(",")]
DBG_A_NB = int(os.environ.get("DBG_A_NB", "16"))
DBG_A_NT = int(os.environ.get("DBG_A_NT", str(NT)))
DBG_B_NT = int(os.environ.get("DBG_B_NT", str(NT)))
DBG_M_BR = [int(x) for x in os.environ.get("DBG_M_BR", "0,1,2,3").split(",")]

M_C1, M_P1, M_C4, M_E4, M_P4, M_C16, M_E16, M_BD0, M_BD1, M_BD2 = range(10)
NMASK = 10


class _Rec:
    def __init__(self):
        self.call = None

    def __getattr__(self, name):
        def f(*a, **k):
            self.call = (name, a, k)
            return self
        return f

    @staticmethod
    def _rnd(n):
        return 32 if n <= 32 else (64 if n <= 64 else 128)

    def pe_class(self):
        name, a, k = self.call
        out = a[0] if a else k["out"]
        if name == "matmul":
            lhsT = k["lhsT"] if "lhsT" in k else a[1]
        elif name == "transpose":
            lhsT = a[1] if len(a) > 1 else k["in_"]
        else:
            raise ValueError(name)
        return (self._rnd(lhsT.partition_size()), lhsT.base_partition(), self._rnd(lhsT.free_size()), out.base_partition())


class Prog:
    ENG = ("pe", "act", "dve", "pool", "sp")

    def __init__(self, nc):
        self.nc = nc
        self.ops = []
        self.by_eng = {e: [] for e in self.ENG}
        self.writers = {}
        self.readers = {}
        self.self_sync = {"pool", "dve", "act"}
        self.serial = {"pool", "dve", "act"}
        self.eng_w = {e: {} for e in self.ENG}
        self.eng_r = {e: {} for e in self.ENG}
        self.pending_bar = {}
        self.dma_since_bar = []
        self.pe_last = {}
        self.log = None

    def _add(self, eng, fn, reads, writes, is_dma):
        oid = len(self.ops)
        cls = None
        if not is_dma:
            r = _Rec()
            fn(r)
            name_, a_, k_ = r.call
            if eng == "pe":
                cls = r.pe_class()

            def fn(e, name_=name_, a_=a_, k_=k_):
                return getattr(e, name_)(*a_, **k_)
            auto_out, auto_in = [], []
            if eng in self.self_sync:
                def _isap(x):
                    return hasattr(x, "tensor") and hasattr(x, "ap")
                items = [("arg%d" % i, v) for i, v in enumerate(a_)] + list(k_.items())
                has_out_kw = "out" in k_
                for nm, v in items:
                    if not _isap(v):
                        continue
                    is_out = (nm in ("out", "accum_out")) or (nm == "arg0" and not has_out_kw)
                    (auto_out if is_out else auto_in).append(v.name)
        sub = ("dma", oid) if is_dma else ((eng, cls) if eng == "pe" else (eng,))
        deps = set()
        for r in reads:
            deps.update(self.writers.get(r, {}).values())
        for w_ in writes:
            deps.update(self.writers.get(w_, {}).values())
            deps.update(self.readers.get(w_, ()))
        if not is_dma and eng in self.self_sync:
            ew, er = self.eng_w[eng], self.eng_r[eng]
            for t in auto_in:
                if t in ew:
                    deps.add(ew[t])
            for t in auto_out:
                if t in ew:
                    deps.add(ew[t])
                deps.update(er.get(t, ()))
            for t in auto_in:
                er.setdefault(t, []).append(oid)
            for t in auto_out:
                ew[t] = oid
                er[t] = []
        pb = self.pending_bar.pop(eng, None)
        if pb:
            deps |= pb
        deps.discard(oid)
        op = dict(id=oid, eng=eng, fn=fn, deps=deps, dma=is_dma, need_inc=False, cls=cls)
        self.ops.append(op)
        self.by_eng[eng].append(op)
        if is_dma:
            self.dma_since_bar.append(oid)
        if cls is not None:
            self.pe_last[cls] = oid
        for r in reads:
            self.readers.setdefault(r, []).append(oid)
        for w_ in writes:
            if self.readers.get(w_):
                self.writers[w_] = {sub: oid}
                self.readers[w_] = []
            else:
                self.writers.setdefault(w_, {})[sub] = oid
                self.readers.setdefault(w_, [])
        return op

    @property
    def last_w(self):
        return self.writers

    @staticmethod
    def _snap(fn):
        if getattr(fn, "__closure__", None) is None:
            return fn
        cells = []
        for c in fn.__closure__:
            try:
                cells.append(types.CellType(c.cell_contents))
            except ValueError:
                cells.append(c)
        return types.FunctionType(fn.__code__, fn.__globals__, fn.__name__, fn.__defaults__, tuple(cells))

    def op(self, eng, fn, reads=(), writes=()):
        return self._add(eng, fn, tuple(reads), tuple(writes), False)

    def barrier(self):
        d = set(self.dma_since_bar)
        for e in self.ENG:
            if e == "pe":
                d.update(self.pe_last.values())
                continue
            for op in reversed(self.by_eng[e]):
                if not op["dma"]:
                    d.add(op["id"])
                    break
        for e in self.ENG:
            self.pending_bar[e] = set(d) | self.pending_bar.get(e, set())
        self.dma_since_bar = []

    def dma(self, queue, out, in_, reads=(), writes=(), **kw):
        def fn(e, out=out, in_=in_, kw=kw):
            return e.dma_start(out=out, in_=in_, **kw)
        return self._add(queue, fn, tuple(reads), tuple(writes), True)

    def emit(self, es):
        nc = self.nc
        ops = self.ops
        for e in self.serial:
            prev = None
            for op in self.by_eng[e]:
                if prev is not None and not op["dma"]:
                    op["deps"].add(prev)
                if not op["dma"]:
                    prev = op["id"]
        for op in ops:
            kept = set()
            for d in op["deps"]:
                p = ops[d]
                same = (p["eng"] == op["eng"]) and (not p["dma"]) and (not op["dma"]) and (p["eng"] not in self.self_sync)
                if not same:
                    kept.add(d)
                    p["need_inc"] = True
            op["kept"] = kept
        NDS = 40
        esem = {e: es.enter_context(nc.semaphore("sem_" + e)) for e in self.ENG if e != "pe"}
        classes = sorted({op["cls"] for op in ops if op["cls"] is not None})
        pesem = {c: es.enter_context(nc.semaphore("sem_pe%d" % i)) for i, c in enumerate(classes)}
        self.pe_classes = classes
        dsem = [es.enter_context(nc.semaphore("dsem%d" % i)) for i in range(NDS)]
        dcount = [0] * NDS
        cnt = {}
        nd = 0
        for op in ops:
            if op["dma"]:
                s = nd % NDS
                nd += 1
                op["prev_ev"] = (dsem[s], dcount[s]) if dcount[s] > 0 else None
                dcount[s] += 16
                op["ev"] = (dsem[s], dcount[s])
            elif op["need_inc"]:
                sem = pesem[op["cls"]] if op["eng"] == "pe" else esem[op["eng"]]
                cnt[id(sem)] = cnt.get(id(sem), 0) + 1
                op["ev"] = (sem, cnt[id(sem)])
            else:
                op["ev"] = None
        final_waits = [(dsem[s], dcount[s]) for s in range(NDS) if dcount[s] > 0]
        block = es.enter_context(nc.Block())
        by_eng = self.by_eng
        self.n_instr = {e: len(by_eng[e]) for e in self.ENG}

        def run(e, eng):
            seen = {}

            def wait(ev):
                sem, val = ev
                k = id(sem)
                if seen.get(k, 0) >= val:
                    return
                eng.wait_ge(sem, val)
                seen[k] = val
            for op in by_eng[e]:
                wl = []
                for d in sorted(op["kept"]):
                    wait(ops[d]["ev"])
                    wl.append((ops[d]["ev"][0].name, ops[d]["ev"][1]))
                if op["dma"] and op["prev_ev"] is not None:
                    wait(op["prev_ev"])
                ins = op["fn"](eng)
                if op["ev"] is not None:
                    sem, val = op["ev"]
                    ins.then_inc(sem, 16 if op["dma"] else 1)
                if self.log is not None:
                    self.log.append((op["id"], e, wl, (op["ev"][0].name, op["ev"][1]) if op["ev"] else None, str(ins)[:150]))
            if e == "sp":
                for ev in final_waits:
                    wait(ev)

        @block.tensor
        def _(eng):
            run("pe", eng)

        @block.scalar
        def _(eng):
            run("act", eng)

        @block.vector
        def _(eng):
            run("dve", eng)

        @block.gpsimd
        def _(eng):
            run("pool", eng)

        @block.sync
        def _(eng):
            run("sp", eng)


class Ctx:
    pass


def _host_consts():
    kk = np.arange(128)[:, None]
    qq = np.arange(128)[None, :]
    m = np.zeros((128, NMASK, 128), np.float32)
    m[:, M_C1] = kk <= qq
    m[:, M_P1] = kk >= qq
    m[:, M_C4] = (kk <= qq) & ((qq - kk) % 4 == 0)
    m[:, M_E4] = ((qq - kk) % 4 == 0)
    m[:, M_P4] = (kk >= qq) & ((qq - kk) % 4 == 0)
    m[:, M_C16] = (kk <= qq) & ((qq - kk) % 16 == 0)
    m[:, M_E16] = ((qq - kk) % 16 == 0)
    bk, tk = kk // 8, kk % 8
    bq, tq = qq // 8, qq % 8
    m[:, M_BD0] = (bk == bq) & (tk <= tq)
    m[:, M_BD1] = (bk == bq) & (tk <= tq) & ((tq - tk) % 4 == 0)
    m[:, M_BD2] = (bk == bq) & (tk == tq)
    sm = np.zeros((128, 16), np.float32)
    r = np.arange(128)[:, None]
    sm[:, 0:8] = r >= np.arange(8)[None, :]
    sm[:, 8] = 1.0
    sm[:, 9] = (np.arange(128) >= 1)
    half = 32
    inv = (np.float32(10000.0) ** (-np.arange(half, dtype=np.float32) / np.float32(half))).astype(np.float32)
    pos = np.zeros((128, NT), np.float32)
    for j in range(16):
        pos[:, j] = 128 * j + np.arange(128)
    pos[:, 16] = PAST + (np.arange(128) % 8)
    ang = (pos[:, :, None] * inv[None, None, :]).astype(np.float32)
    cos = np.cos(ang).astype(np.float32)
    sin = np.sin(ang).astype(np.float32)
    cosF = np.concatenate([cos, cos], axis=-1)
    sinS = np.concatenate([-sin, sin], axis=-1)
    iota = np.tile(np.arange(256, dtype=np.float32)[None, :], (128, 1))
    gmask = np.zeros((128, 2), np.float32)
    gmask[0:64, 0] = 1.0
    gmask[64:128, 1] = 1.0
    return dict(c_ident=np.eye(128, dtype=np.float32), c_masks=np.ascontiguousarray(m),
                c_smask=sm, c_cos=np.ascontiguousarray(cosF), c_sin=np.ascontiguousarray(sinS),
                c_iota=iota, c_gmask=gmask)


_B_QPERM = [0, 4, 1, 5, 2, 6, 3, 7]


def _host_weights(w_in, w_br_b):
    qb = w_in[:, :, 2304:2816].reshape(DEPTH, D, 8, 64)[:, :, _B_QPERM, :].reshape(DEPTH, D, 512)
    w_b = np.ascontiguousarray(np.concatenate([qb, w_in[:, :, 2816:3072]], axis=2))
    w_brb = np.ascontiguousarray(w_br_b.reshape(DEPTH, 8, 64, D)[:, _B_QPERM].reshape(DEPTH, 512, D))
    return w_b, w_brb


IN_SPECS = [
    ("xp", [SEQ, D]), ("xs", [128, D]), ("pp", [DEPTH, SEQ, 256]), ("psm", [DEPTH, 128, 256]),
    ("ca1k", [DEPTH, 16, 128, 256]), ("ca1v", [DEPTH, 16, 128, 256]),
    ("ca2k", [DEPTH, 16, 512, 256]), ("ca2v", [DEPTH, 16, 512, 256]),
    ("ca3k", [DEPTH, 16, 2048, 256]), ("ca3v", [DEPTH, 16, 2048, 256]),
    ("cbk", [DEPTH, 16, 128, 128]), ("cbv", [DEPTH, 16, 128, 128]),
    ("sconv", [DEPTH, 32, 256]), ("sdre", [DEPTH, 16, 1024]), ("sdim", [DEPTH, 16, 1024]),
    ("n_mix_pre", [DEPTH, D]), ("n_mix_post", [DEPTH, D]), ("n_ffn_pre", [DEPTH, D]), ("n_ffn_post", [DEPTH, D]),
    ("w_in", [DEPTH, D, 8192]), ("w_b", [DEPTH, D, 768]), ("sinks", [DEPTH, 2, 4]), ("conv_w", [DEPTH, 3, 256]),
    ("lam_re", [DEPTH, 8, 128]), ("lam_im", [DEPTH, 8, 128]), ("log_dt", [DEPTH, 8, 2]),
    ("b_re", [DEPTH, 1024, 16]), ("b_im", [DEPTH, 1024, 16]), ("c_re", [DEPTH, 256, 64]), ("c_im", [DEPTH, 256, 64]),
    ("ssm_d", [DEPTH, 2, 128]), ("w_d_glu", [DEPTH, 256, 512]),
    ("w_br_a", [DEPTH, 256, D]), ("w_br_b", [DEPTH, 512, D]), ("w_br_c", [DEPTH, 256, D]), ("w_br_d", [DEPTH, 256, D]),
    ("w_out", [DEPTH, D, D]), ("w_ffn_gate", [DEPTH, D, DFF]), ("w_ffn_up", [DEPTH, D, DFF]),
    ("w_ffn_down", [DEPTH, DFF, D]), ("w_ple", [DEPTH, 256, D]), ("w_ple_gate", [DEPTH, D, D]),
    ("c_ident", [128, 128]), ("c_masks", [128, NMASK, 128]), ("c_smask", [128, 16]),
    ("c_cos", [128, NT, 64]), ("c_sin", [128, NT, 64]), ("c_iota", [128, 256]), ("c_gmask", [128, 2]),
]
OUT_SPECS = [
    ("yp", [SEQ, D]), ("ys", [128, D]),
    ("a1k_p", [DEPTH, 128, 256]), ("a1k_s", [DEPTH, 16, 128, 256]), ("a1v_p", [DEPTH, 128, 256]), ("a1v_s", [DEPTH, 16, 128, 256]),
    ("a2k_p", [DEPTH, 512, 256]), ("a2k_s", [DEPTH, 16, 512, 256]), ("a2v_p", [DEPTH, 512, 256]), ("a2v_s", [DEPTH, 16, 512, 256]),
    ("a3k_p", [DEPTH, 2048, 256]), ("a3k_s", [DEPTH, 16, 2048, 256]), ("a3v_p", [DEPTH, 2048, 256]), ("a3v_s", [DEPTH, 16, 2048, 256]),
    ("bk_p", [DEPTH, 128, 128]), ("bk_s", [DEPTH, 16, 128, 128]), ("bv_p", [DEPTH, 128, 128]), ("bv_s", [DEPTH, 16, 128, 128]),
    ("conv_p", [DEPTH, 2, 256]), ("conv_s", [DEPTH, 32, 256]),
    ("dre_p", [DEPTH, 1024]), ("dre_s", [DEPTH, 16, 1024]), ("dim_p", [DEPTH, 1024]), ("dim_s", [DEPTH, 16, 1024]),
]


def build_program(depth=DEPTH, stop_after=None, dumps=()):
    nc = bass.Bass("TRN2", target_bir_lowering=False)
    I = {n: nc.dram_tensor(n, s, F32, kind="ExternalInput").ap() for n, s in IN_SPECS}
    O = {n: nc.dram_tensor(n, s, F32, kind="ExternalOutput").ap() for n, s in OUT_SPECS}
    dump_specs = []
    P = Prog(nc)
    if os.environ.get("DBG_LOG"):
        P.log = []
    C = Ctx()
    C.min_rem = (1 << 30, '')
    with ExitStack() as es:
        E = es.enter_context

        uid = [0]

        def sb(name, shape, dt, st=None):
            uid[0] += 1
            t = (st or es).enter_context(nc.sbuf_tensor("%s_%d" % (name, uid[0]), shape, dt))
            rem = nc.sbuf_bytes_remaining
            if rem < C.min_rem[0]:
                C.min_rem = (rem, name)
            if os.environ.get("DBG_ALLOC"):
                nb = int(np.prod(shape[1:])) * (2 if dt == BF16 else 4)
                print("ALLOC %-10s bytes/part=%6d  remaining_after=%d" % (name, nb, rem))
            return t

        C.ps = [E(nc.psum_tensor("ps%d" % i, [128, 512], F32)) for i in range(8)]
        C.X = sb("X", [128, NT, D], F32)
        C.ident_f = sb("ident_f", [128, 128], F32)
        C.ident_b = sb("ident_b", [128, 128], BF16)
        C.ones_b = sb("ones_b", [128, 64], BF16)
        C.iota = sb("iota", [128, 256], F32)
        C.gmask = sb("gmask", [128, 2], F32)
        C.halfpi = sb("halfpi", [128, 1], F32)
        C.gvec = sb("gvec", [128, 1, D], F32)
        C.ss = sb("ss", [128, NT], F32)
        C.rs = sb("rs", [128, NT], F32)
        C.gslot = [0]

        def psb(b):
            return C.ps[b][:].bitcast(BF16)

        def dump(name, ap, shape, reads):
            if name in dumps:
                d = nc.dram_tensor("dbg_" + name, list(shape), ap.dtype, kind="ExternalOutput").ap()
                dump_specs.append(("dbg_" + name, list(shape)))
                P.dma("sp", d, ap, reads=reads)

        barrier = P.barrier

        class WRing:
            def __init__(self, st, nslots, elems, name):
                self.n, self.name, self.cnt = nslots, name, 0
                self.buf = sb("wr_" + name, [128, nslots, elems], BF16, st)

            def next(self):
                s_ = self.cnt % self.n
                self.cnt += 1
                return s_

            def key(self, slot):
                return ("w", self.name, slot)

            def load(self, slot, pieces):
                for (off, kc, ncols, src) in pieces:
                    dst = self.buf[:, slot, off:off + kc * ncols].rearrange("p (k n) -> p k n", k=kc)
                    P.dma("pool", dst, src.rearrange("(k p) n -> p k n", p=128), writes=[self.key(slot)])

            def view(self, slot, off, kc, ncols):
                return self.buf[:, slot, off:off + kc * ncols].rearrange("p (k n) -> p k n", k=kc)

        P.dma("sp", C.ident_f[:], I["c_ident"], writes=["ident_f"])
        P.dma("pool", C.ident_b[:], I["c_ident"], writes=["ident_b"])
        P.dma("sp", C.iota[:], I["c_iota"], writes=["iota"])
        P.dma("sp", C.gmask[:], I["c_gmask"], writes=["gmask"])
        P.op("dve", lambda e: e.memset(C.ones_b[:], 1.0), writes=["ones_b"])
        P.op("dve", lambda e: e.memset(C.halfpi[:], 1.5707963267948966), writes=["halfpi"])
        for j in range(16):
            P.dma("sp", C.X[:, j, :], I["xp"][128 * j:128 * (j + 1), :], writes=[("X", j)])
        P.dma("sp", C.X[:, 16, :], I["xs"], writes=[("X", 16)])

        for L_ in range(depth):
            for g_, w_ in enumerate(A_W):
                for b in range(16):
                    for kv_ in ("k", "v"):
                        P.dma("sp", O["a%d%s_s" % (g_ + 1, kv_)][L_, b, 0:w_ - 8, :], I["ca%d%s" % (g_ + 1, kv_)][L_, b, 8:w_, :])
            for b in range(16):
                for kv_ in ("k", "v"):
                    P.dma("sp", O["b%s_s" % kv_][L_, b, 0:120, :], I["cb%s" % kv_][L_, b, 8:128, :])

        def load_att_consts(st):
            C.masks = sb("masks", [128, NMASK, 128], BF16, st)
            C.smask = sb("smask", [128, 16], BF16, st)
            C.cosF = sb("cosF", [128, NT, 64], F32, st)
            C.sinS = sb("sinS", [128, NT, 64], F32, st)
            P.dma("pool", C.masks[:], I["c_masks"], writes=["masks"])
            P.dma("pool", C.smask[:], I["c_smask"], writes=["smask"])
            P.dma("sp", C.cosF[:], I["c_cos"], writes=["rope"])
            P.dma("sp", C.sinS[:], I["c_sin"], writes=["rope"])

        def norm_to_hT(L, gname, tiles, st, dst, col0):
            gs = 0
            P.dma("sp", C.gvec[:, gs, :], I[gname][L].partition_broadcast(128), writes=[("g", gs)])
            junk = sb("junk_n", [128, D], BF16, st)
            hN = sb("hN", [128, 2, D], BF16, st)
            for j in tiles:
                P.op("act", lambda e, j=j: e.activation(out=junk[:], in_=C.X[:, j, :], func=AF.Square,
                                                         accum_out=C.ss[:, j:j + 1]),
                     reads=[("X", j)], writes=["junk_n", ("ss", j)])
            t0, t1 = tiles[0], tiles[-1] + 1
            P.op("dve", lambda e: e.tensor_scalar(out=C.rs[:, t0:t1], in0=C.ss[:, t0:t1], scalar1=1.0 / D, scalar2=EPS,
                                                  op0=ALU.mult, op1=ALU.add),
                 reads=[("ss", j) for j in tiles], writes=["rs"])
            P.op("act", lambda e: e.activation(out=C.rs[:, t0:t1], in_=C.rs[:, t0:t1], func=AF.Sqrt), reads=["rs"], writes=["rs"])
            P.op("dve", lambda e: e.reciprocal(out=C.rs[:, t0:t1], in_=C.rs[:, t0:t1]), reads=["rs"], writes=["rs"])
            for j in tiles:
                hs = j % 2
                P.op("dve", lambda e, j=j, hs=hs: e.scalar_tensor_tensor(
                    out=hN[:, hs, :], in0=C.X[:, j, :], scalar=C.rs[:, j:j + 1], in1=C.gvec[:, gs, :],
                    op0=ALU.mult, op1=ALU.mult), reads=[("X", j), "rs", ("g", gs)], writes=[("hN", hs)])
                bank = 6 + (j % 2)
                pv = psb(bank)
                for c in range(8):
                    P.op("pe", lambda e, c=c, hs=hs, pv=pv: e.transpose(pv[:, c * 128:(c + 1) * 128], hN[:, hs, c * 128:(c + 1) * 128], C.ident_b[:]),
                         reads=[("hN", hs), "ident_b"], writes=[("ps", bank)])
                lc = j * 128 - col0
                P.op("act", lambda e, lc=lc, pv=pv: e.activation(
                    out=dst[:, :, lc:lc + 128], in_=pv.rearrange("p (c t) -> p c t", c=8), func=AF.Copy),
                    reads=[("ps", bank)], writes=[("hT", j)])

        def rope_ops(ps_ap, nheads, j, out_ap, T1, T2, reads, wkey):
            pin = ps_ap.rearrange("p (h d) -> p h d", h=nheads)
            t1 = T1[:, 0:nheads * 64].rearrange("p (h d) -> p h d", h=nheads)
            t2 = T2[:, 0:nheads * 64].rearrange("p (h d) -> p h d", h=nheads)
            cosb = C.cosF[:, j, :].unsqueeze(1).broadcast_to([128, nheads, 64])
            sin_lo = C.sinS[:, j, 0:32].unsqueeze(1).broadcast_to([128, nheads, 32])
            sin_hi = C.sinS[:, j, 32:64].unsqueeze(1).broadcast_to([128, nheads, 32])
            P.op("dve", lambda e: e.tensor_tensor(out=t1, in0=pin, in1=cosb, op=ALU.mult), reads=reads + ["rope"], writes=["ropeT1"])
            P.op("dve", lambda e: e.tensor_tensor(out=t2[:, :, 0:32], in0=pin[:, :, 32:64], in1=sin_lo, op=ALU.mult),
                 reads=reads + ["rope"], writes=["ropeT2"])
            P.op("dve", lambda e: e.tensor_tensor(out=t2[:, :, 32:64], in0=pin[:, :, 0:32], in1=sin_hi, op=ALU.mult),
                 reads=reads + ["rope"], writes=["ropeT2"])
            P.op("dve", lambda e: e.tensor_tensor(out=out_ap, in0=T1[:, 0:nheads * 64], in1=T2[:, 0:nheads * 64], op=ALU.add),
                 reads=["ropeT1", "ropeT2"], writes=[wkey])

        def mixer_A(L, g, st):
            w, dil = A_W[g], (1, 4, 16)[g]
            ck_in, cv_in = I["ca%dk" % (g + 1)], I["ca%dv" % (g + 1)]
            ok_p, ov_p = O["a%dk_p" % (g + 1)], O["a%dv_p" % (g + 1)]
            ok_s, ov_s = O["a%dk_s" % (g + 1)], O["a%dv_s" % (g + 1)]
            first_out_tile = 16 - w // 128
            nct = (1, 4, 8)[g]
            nsl = min(nct, 4)
            nsub = 2 if g == 2 else 1
            WR = WRing(st, 1, 6144, "A")
            ws = WR.next()
            WR.load(ws, [(0, 8, 256, I["w_in"][L][:, 256 * g:256 * g + 256]),
                         (2048, 8, 256, I["w_in"][L][:, 768 + 256 * g:768 + 256 * g + 256]),
                         (4096, 8, 256, I["w_in"][L][:, 1536 + 256 * g:1536 + 256 * g + 256])])
            Wq, Wk, Wv = WR.view(ws, 0, 8, 256), WR.view(ws, 2048, 8, 256), WR.view(ws, 4096, 8, 256)
            wkey = WR.key(ws)
            KT = sb("KT", [128, 2, NTOK], BF16, st)
            V = sb("Vg", [128, NT, 256], BF16, st)
            qkr = sb("qkr", [128, 2, 512], F32, st)
            V32 = sb("V32", [128, 2, 256], F32, st)
            T1 = sb("ropeT1", [128, 512], F32, st)
            T2 = sb("ropeT2", [128, 512], F32, st)
            qT = sb("qT", [128, 2, 2, 128], BF16, st)
            PT = sb("PT", [128, 3, 512], BF16, st)
            Kc = sb("Kc", [128, 2, nsl, 256], BF16, st)
            Vc = sb("Vc", [128, 2, nsl, 256], BF16, st)
            KcT = sb("KcT", [128, 2, nsl, 2, 128], BF16, st)
            PTs = sb("PTs", [128, 2, 32], BF16, st)
            ptc = [0]
            for j in list(range(min(DBG_A_NT, 16))) + [16]:
                s2 = j % 2
                bqk, bv = (0, 1)[s2], 2
                for kc in range(8):
                    P.op("pe", lambda e, kc=kc, j=j, bqk=bqk: e.matmul(C.ps[bqk][:, 0:256], lhsT=C.hT[:, kc, j * 128:(j + 1) * 128], rhs=Wq[:, kc, :],
                                                                       start=(kc == 0), stop=(kc == 7)),
                         reads=[("hT", j), wkey], writes=[("ps", bqk)])
                for kc in range(8):
                    P.op("pe", lambda e, kc=kc, j=j, bqk=bqk: e.matmul(C.ps[bqk][:, 256:512], lhsT=C.hT[:, kc, j * 128:(j + 1) * 128], rhs=Wk[:, kc, :],
                                                                       start=(kc == 0), stop=(kc == 7), skip_group_check=True),
                         reads=[("hT", j), wkey], writes=[("ps", bqk)])
                for kc in range(8):
                    P.op("pe", lambda e, kc=kc, j=j: e.matmul(C.ps[bv][:, 0:256], lhsT=C.hT[:, kc, j * 128:(j + 1) * 128], rhs=Wv[:, kc, :],
                                                              start=(kc == 0), stop=(kc == 7)),
                         reads=[("hT", j), wkey], writes=[("ps", bv)])
                rope_ops(C.ps[bqk][:, 0:512], 8, j, qkr[:, s2, :], T1, T2, [("ps", bqk)], ("qkr", s2))
                need_out = (j >= first_out_tile)
                P.op("act", lambda e, j=j: e.activation(out=V[:, j, :], in_=C.ps[bv][:, 0:256], func=AF.Copy),
                     reads=[("ps", bv)], writes=[("V", j)])
                if need_out:
                    P.op("act", lambda e, s2=s2: e.activation(out=V32[:, s2, :], in_=C.ps[bv][:, 0:256], func=AF.Copy),
                         reads=[("ps", bv)], writes=[("V32", s2)])
                    if j < 16:
                        r0 = (j - first_out_tile) * 128
                        P.dma("sp", ok_p[L, r0:r0 + 128, :], qkr[:, s2, 256:512], reads=[("qkr", s2)])
                        P.dma("sp", ov_p[L, r0:r0 + 128, :], V32[:, s2, :], reads=[("V32", s2)])
                    else:
                        for b in range(16):
                            P.dma("sp", ok_s[L, b, w - 8:w, :], qkr[8 * b:8 * b + 8, s2, 256:512], reads=[("qkr", s2)])
                            P.dma("sp", ov_s[L, b, w - 8:w, :], V32[8 * b:8 * b + 8, s2, :], reads=[("V32", s2)])
                for c in range(4):
                    P.op("pe", lambda e, c=c, s2=s2: e.transpose(C.ps[3][:, c * 128:(c + 1) * 128], qkr[:, s2, c * 128:(c + 1) * 128], C.ident_f[:]),
                         reads=[("qkr", s2), "ident_f"], writes=[("ps", 3)])
                P.op("act", lambda e, s2=s2: e.activation(out=qT[:, s2], in_=C.ps[3][:, 0:256].rearrange("p (c t) -> p c t", c=2), func=AF.Copy),
                     reads=[("ps", 3)], writes=[("qT", s2)])
                P.op("act", lambda e, j=j: e.activation(out=KT[:, :, j * 128:(j + 1) * 128], in_=C.ps[3][:, 256:512].rearrange("p (c t) -> p c t", c=2), func=AF.Copy),
                     reads=[("ps", 3)], writes=[("KT", j)])
                if j < 16:
                    if g == 0:
                        kts = ([(j - 1, M_P1)] if j >= 1 else []) + [(j, M_C1)]
                    elif g == 1:
                        kts = ([(j - 4, M_P4)] if j >= 4 else []) + [(kt, M_E4) for kt in range(max(0, j - 3), j)] + [(j, M_C4)]
                    else:
                        kts = [(kt, M_E16) for kt in range(0, j)] + [(j, M_C16)]
                else:
                    kts = [(16, (M_BD0, M_BD1, M_BD2)[g])]
                acc = 6 + s2
                ACC = C.ps[acc]
                first = True
                for (kt, mid) in kts:
                    ps_ = ptc[0] % 3
                    ptc[0] += 1
                    for h in range(4):
                        par, c = h % 2, h // 2
                        po = 64 * par
                        P.op("pe", lambda e, par=par, po=po, c=c, kt=kt, s2=s2: e.matmul(
                            C.ps[4 + par][:, c * 128:(c + 1) * 128], lhsT=KT[po:po + 64, c, kt * 128:(kt + 1) * 128],
                            rhs=qT[po:po + 64, s2, c, :], start=True, stop=True, skip_group_check=True),
                            reads=[("KT", kt), ("qT", s2)], writes=[("ps", 4 + par)])
                    for par in range(2):
                        P.op("act", lambda e, par=par, ps_=ps_: e.activation(out=PT[:, ps_, par * 256:par * 256 + 256], in_=C.ps[4 + par][:, 0:256], func=AF.Exp, scale=0.125),
                             reads=[("ps", 4 + par)], writes=[("PT", ps_)])
                    P.op("pool", lambda e, ps_=ps_, mid=mid: e.tensor_tensor(
                        out=PT[:, ps_, :].rearrange("p (h q) -> p h q", h=4), in0=PT[:, ps_, :].rearrange("p (h q) -> p h q", h=4),
                        in1=C.masks[:, mid, :].unsqueeze(1).broadcast_to([128, 4, 128]), op=ALU.mult),
                        reads=[("PT", ps_), "masks"], writes=[("PT", ps_)])
                    for h in range(4):
                        par, c = h % 2, h // 2
                        po = 64 * par
                        P.op("pe", lambda e, h=h, par=par, po=po, c=c, kt=kt, ps_=ps_, st_=(first and h < 2): e.matmul(
                            ACC[po:po + 64, c * 128:(c + 1) * 128], lhsT=V[:, kt, 64 * h:64 * h + 64], rhs=PT[:, ps_, par * 256 + c * 128:par * 256 + c * 128 + 128],
                            start=st_, stop=False, skip_group_check=True),
                            reads=[("V", kt), ("PT", ps_)], writes=[("ps", acc)])
                    for par in range(2):
                        P.op("pe", lambda e, par=par, ps_=ps_: e.matmul(
                            ACC[64 * par:64 * par + 64, 256:512], lhsT=C.ones_b[:, :], rhs=PT[:, ps_, par * 256:par * 256 + 256],
                            start=False, stop=False, skip_group_check=True),
                            reads=[("PT", ps_), "ones_b"], writes=[("ps", acc)])
                    first = False
                if j == 16:
                    step = 0
                    for b in range(DBG_A_NB):
                        for sub in range(nsub):
                            cs = step % 2
                            step += 1
                            if g == 0:
                                ksrc, vsrc = ck_in[L, b].unsqueeze(1), cv_in[L, b].unsqueeze(1)
                                tl = list(range(8))
                            elif g == 1:
                                ksrc = ck_in[L, b].rearrange("(m s) f -> m s f", s=4)
                                vsrc = cv_in[L, b].rearrange("(m s) f -> m s f", s=4)
                                tl = list(range(8))
                            else:
                                ksrc = ck_in[L, b].rearrange("(m s) f -> m s f", s=16)[:, 4 * sub:4 * sub + 4, :]
                                vsrc = cv_in[L, b].rearrange("(m s) f -> m s f", s=16)[:, 4 * sub:4 * sub + 4, :]
                                tl = list(range(4 * sub, 4 * sub + 4))
                            t0, nt_ = tl[0], len(tl)
                            P.dma("pool", Kc[:, cs], ksrc, writes=[("Kc", cs)])
                            P.dma("pool", Vc[:, cs], vsrc, writes=[("Vc", cs)])
                            pv = psb(3)
                            nblk = nsl * 2
                            for ct in range(nsl):
                                for hp in range(2):
                                    bi = ct * 2 + hp
                                    P.op("pe", lambda e, bi=bi, ct=ct, hp=hp, cs=cs, pv=pv: e.transpose(
                                        pv[:, bi * 128:(bi + 1) * 128], Kc[:, cs, ct, hp * 128:(hp + 1) * 128], C.ident_b[:]),
                                        reads=[("Kc", cs), "ident_b"], writes=[("ps", 3)])
                            P.op("act", lambda e, pv=pv, cs=cs, nblk=nblk: e.activation(
                                out=KcT[:, cs].rearrange("p a b m -> p (a b) m"),
                                in_=pv[:, 0:nblk * 128].rearrange("p (a m) -> p a m", a=nblk), func=AF.Copy),
                                reads=[("ps", 3)], writes=[("KcT", cs)])
                            for h in range(4):
                                par, c = h % 2, h // 2
                                po = 64 * par
                                SBk = C.ps[4 + par]
                                if g == 0:
                                    P.op("pe", lambda e, po=po, c=c, cs=cs, b=b, SBk=SBk: e.matmul(
                                        SBk[:, c * 8:c * 8 + 8], lhsT=KcT[po:po + 64, cs, 0, c, :], rhs=qT[po:po + 64, s2, c, 8 * b:8 * b + 8],
                                        start=True, stop=True, skip_group_check=True),
                                        reads=[("KcT", cs), ("qT", s2)], writes=[("ps", 4 + par)])
                                elif g == 1:
                                    for s in range(4):
                                        P.op("pe", lambda e, po=po, c=c, cs=cs, b=b, s=s, SBk=SBk: e.matmul(
                                            SBk[:, c * 8 + s:c * 8 + s + 5:4], lhsT=KcT[po:po + 64, cs, s, c, :],
                                            rhs=qT[po:po + 64, s2, c, 8 * b + s:8 * b + s + 5:4],
                                            start=True, stop=True, skip_group_check=True),
                                            reads=[("KcT", cs), ("qT", s2)], writes=[("ps", 4 + par)])
                                else:
                                    for t in tl:
                                        P.op("pe", lambda e, po=po, c=c, cs=cs, b=b, t=t, t0=t0, SBk=SBk: e.matmul(
                                            SBk[:, c * 8 + t:c * 8 + t + 1], lhsT=KcT[po:po + 64, cs, t - t0, c, :],
                                            rhs=qT[po:po + 64, s2, c, 8 * b + t:8 * b + t + 1],
                                            start=True, stop=True, skip_group_check=True),
                                            reads=[("KcT", cs), ("qT", s2)], writes=[("ps", 4 + par)])
                            PT4 = PTs[:, cs, :].rearrange("p (r c t) -> p r c t", r=2, c=2)
                            for par in range(2):
                                P.op("act", lambda e, par=par, PT4=PT4, t0=t0, nt_=nt_: e.activation(
                                    out=PT4[:, par, :, t0:t0 + nt_], in_=C.ps[4 + par][:, 0:16].rearrange("p (c t) -> p c t", c=2)[:, :, t0:t0 + nt_],
                                    func=AF.Exp, scale=0.125), reads=[("ps", 4 + par)], writes=[("PTs", cs)])
                            PT3 = PTs[:, cs, :].rearrange("p (h t) -> p h t", h=4)
                            if g == 0:
                                P.op("pool", lambda e, PT3=PT3: e.tensor_tensor(
                                    out=PT3, in0=PT3, in1=C.smask[:, 0:8].unsqueeze(1).broadcast_to([128, 4, 8]), op=ALU.mult),
                                    reads=[("PTs", cs), "smask"], writes=[("PTs", cs)])
                            elif g == 1:
                                P.op("pool", lambda e, PT3=PT3: e.tensor_tensor(
                                    out=PT3[:, :, 4:8], in0=PT3[:, :, 4:8],
                                    in1=C.smask[:, 9:10].unsqueeze(1).broadcast_to([128, 4, 4]), op=ALU.mult),
                                    reads=[("PTs", cs), "smask"], writes=[("PTs", cs)])
                            for h in range(4):
                                par, c = h % 2, h // 2
                                po = 64 * par
                                col0 = c * 128 + 8 * b
                                pc0 = par * 16 + c * 8
                                if g == 0:
                                    P.op("pe", lambda e, h=h, po=po, cs=cs, col0=col0, pc0=pc0: e.matmul(
                                        ACC[po:po + 64, col0:col0 + 8], lhsT=Vc[:, cs, 0, 64 * h:64 * h + 64], rhs=PTs[:, cs, pc0:pc0 + 8],
                                        start=False, stop=False, skip_group_check=True),
                                        reads=[("Vc", cs), ("PTs", cs)], writes=[("ps", acc)])
                                elif g == 1:
                                    for s in range(4):
                                        P.op("pe", lambda e, h=h, po=po, cs=cs, col0=col0, pc0=pc0, s=s: e.matmul(
                                            ACC[po:po + 64, col0 + s:col0 + s + 5:4], lhsT=Vc[:, cs, s, 64 * h:64 * h + 64],
                                            rhs=PTs[:, cs, pc0 + s:pc0 + s + 5:4], start=False, stop=False, skip_group_check=True),
                                            reads=[("Vc", cs), ("PTs", cs)], writes=[("ps", acc)])
                                else:
                                    for t in tl:
                                        P.op("pe", lambda e, h=h, po=po, cs=cs, col0=col0, pc0=pc0, t=t, t0=t0: e.matmul(
                                            ACC[po:po + 64, col0 + t:col0 + t + 1], lhsT=Vc[:, cs, t - t0, 64 * h:64 * h + 64],
                                            rhs=PTs[:, cs, pc0 + t:pc0 + t + 1], start=False, stop=False, skip_group_check=True),
                                            reads=[("Vc", cs), ("PTs", cs)], writes=[("ps", acc)])
                            for par in range(2):
                                P.op("pe", lambda e, par=par, cs=cs, b=b, t0=t0, nt_=nt_, PT4=PT4: e.matmul(
                                    ACC[64 * par:64 * par + 64, 256:512].rearrange("p (c q) -> p c q", c=2)[:, :, 8 * b + t0:8 * b + t0 + nt_],
                                    lhsT=C.ones_b[:, :], rhs=PT4[:, par, :, t0:t0 + nt_],
                                    start=False, stop=False, skip_group_check=True),
                                    reads=[("PTs", cs), "ones_b"], writes=[("ps", acc)])
                nv = C.numA[:, :, j * 128:(j + 1) * 128]
                dv = C.denA[:, :, j * 128:(j + 1) * 128]
                pn = ACC[:, 0:256].rearrange("p (c q) -> p c q", c=2)
                pd = ACC[:, 256:512].rearrange("p (c q) -> p c q", c=2)
                if g == 0:
                    P.op("dve", lambda e, nv=nv, pn=pn: e.tensor_copy(out=nv, in_=pn), reads=[("ps", acc)], writes=[("numA", j)])
                    P.op("dve", lambda e, dv=dv, pd=pd: e.tensor_copy(out=dv, in_=pd), reads=[("ps", acc)], writes=[("denA", j)])
                else:
                    P.op("dve", lambda e, nv=nv, pn=pn: e.tensor_tensor(out=nv, in0=pn, in1=nv, op=ALU.add),
                         reads=[("ps", acc), ("numA", j)], writes=[("numA", j)])
                    P.op("dve", lambda e, dv=dv, pd=pd: e.tensor_tensor(out=dv, in0=pd, in1=dv, op=ALU.add),
                         reads=[("ps", acc), ("denA", j)], writes=[("denA", j)])

        def mixer_B(L, st):
            ck_in, cv_in = I["cbk"], I["cbv"]
            ok_p, ov_p, ok_s, ov_s = O["bk_p"], O["bv_p"], O["bk_s"], O["bv_s"]
            WR = WRing(st, 1, 6144, "B")
            ws = WR.next()
            WR.load(ws, [(0, 8, 768, I["w_b"][L])])
            W = WR.view(ws, 0, 8, 768)
            wkey = WR.key(ws)
            KT = sb("KTb", [128, NTOK], BF16, st)
            V = sb("Vb", [128, NT, 128], BF16, st)
            qkr = sb("qkrb", [128, 2, 640], F32, st)
            V32 = sb("V32b", [128, 2, 128], F32, st)
            T1 = sb("ropeT1", [128, 512], F32, st)
            T2 = sb("ropeT2", [128, 512], F32, st)
            qT = sb("qTb", [128, 2, 4, 128], BF16, st)
            PT = sb("PTb", [128, 4, 512], BF16, st)
            Kc = sb("Kcb", [128, 2, 128], BF16, st)
            Vc = sb("Vcb", [128, 2, 128], BF16, st)
            KcT = sb("KcTb", [128, 2, 128], BF16, st)
            PTs = sb("PTsb", [128, 2, 64], BF16, st)
            esink = sb("esink", [128, 4], F32, st)
            dtmp = sb("dtmp", [128, 512], F32, st)
            for kv in range(2):
                P.dma("sp", esink[64 * kv:64 * kv + 64, :], I["sinks"][L, kv].partition_broadcast(64), writes=["esink"])
            P.op("act", lambda e: e.activation(out=esink[:], in_=esink[:], func=AF.Exp), reads=["esink"], writes=["esink"])
            ptc = [0]
            NUM, DEN = C.ps[6], C.ps[7]
            for j in range(DBG_B_NT):
                s2 = j % 2
                bq = (0, 1)[s2]
                for kc in range(8):
                    P.op("pe", lambda e, kc=kc, j=j, bq=bq: e.matmul(C.ps[bq][:, 0:512], lhsT=C.hT[:, kc, j * 128:(j + 1) * 128], rhs=W[:, kc, 0:512],
                                                                     start=(kc == 0), stop=(kc == 7)),
                         reads=[("hT", j), wkey], writes=[("ps", bq)])
                for kc in range(8):
                    P.op("pe", lambda e, kc=kc, j=j: e.matmul(C.ps[2][:, 0:256], lhsT=C.hT[:, kc, j * 128:(j + 1) * 128], rhs=W[:, kc, 512:768],
                                                              start=(kc == 0), stop=(kc == 7)),
                         reads=[("hT", j), wkey], writes=[("ps", 2)])
                rope_ops(C.ps[bq][:, 0:512], 8, j, qkr[:, s2, 0:512], T1, T2, [("ps", bq)], ("qkrb", s2))
                rope_ops(C.ps[2][:, 0:128], 2, j, qkr[:, s2, 512:640], T1, T2, [("ps", 2)], ("qkrb", s2))
                P.op("act", lambda e, j=j: e.activation(out=V[:, j, :], in_=C.ps[2][:, 128:256], func=AF.Copy),
                     reads=[("ps", 2)], writes=[("Vb", j)])
                if j >= 15:
                    P.op("act", lambda e, s2=s2: e.activation(out=V32[:, s2, :], in_=C.ps[2][:, 128:256], func=AF.Copy),
                         reads=[("ps", 2)], writes=[("V32b", s2)])
                    if j == 15:
                        P.dma("sp", ok_p[L], qkr[:, s2, 512:640], reads=[("qkrb", s2)])
                        P.dma("sp", ov_p[L], V32[:, s2, :], reads=[("V32b", s2)])
                    else:
                        for b in range(16):
                            P.dma("sp", ok_s[L, b, 120:128, :], qkr[8 * b:8 * b + 8, s2, 512:640], reads=[("qkrb", s2)])
                            P.dma("sp", ov_s[L, b, 120:128, :], V32[8 * b:8 * b + 8, s2, :], reads=[("V32b", s2)])
                for c in range(4):
                    P.op("pe", lambda e, c=c, s2=s2: e.transpose(C.ps[3][:, c * 128:(c + 1) * 128], qkr[:, s2, c * 128:(c + 1) * 128], C.ident_f[:]),
                         reads=[("qkrb", s2), "ident_f"], writes=[("ps", 3)])
                P.op("act", lambda e, s2=s2: e.activation(out=qT[:, s2], in_=C.ps[3][:, 0:512].rearrange("p (c t) -> p c t", c=4), func=AF.Copy),
                     reads=[("ps", 3)], writes=[("qTb", s2)])
                P.op("pe", lambda e, s2=s2: e.transpose(C.ps[3][:, 0:128], qkr[:, s2, 512:640], C.ident_f[:]),
                     reads=[("qkrb", s2), "ident_f"], writes=[("ps", 3)])
                P.op("act", lambda e, j=j: e.activation(out=KT[:, j * 128:(j + 1) * 128], in_=C.ps[3][:, 0:128], func=AF.Copy),
                     reads=[("ps", 3)], writes=[("KTb", j)])
                if j < 16:
                    kts = ([(j - 1, M_P1)] if j >= 1 else []) + [(j, M_C1)]
                else:
                    kts = [(16, M_BD0)]
                for ki, (kt, mid) in enumerate(kts):
                    for kv in range(2):
                        sbk = 4 + kv
                        ps_ = ptc[0] % 4
                        ptc[0] += 1
                        P.op("pe", lambda e, kv=kv, kt=kt, sbk=sbk, s2=s2: e.matmul(
                            C.ps[sbk][:, :], lhsT=KT[64 * kv:64 * kv + 64, kt * 128:(kt + 1) * 128],
                            rhs=qT[64 * kv:64 * kv + 64, s2, :, :], start=True, stop=True),
                            reads=[("KTb", kt), ("qTb", s2)], writes=[("ps", sbk)])
                        P.op("act", lambda e, sbk=sbk, ps_=ps_: e.activation(out=PT[:, ps_, :], in_=C.ps[sbk][:, :], func=AF.Exp, scale=0.125),
                             reads=[("ps", sbk)], writes=[("PTb", ps_)])
                        P.op("pool", lambda e, ps_=ps_, mid=mid: e.tensor_tensor(
                            out=PT[:, ps_, :].rearrange("p (h q) -> p h q", h=4), in0=PT[:, ps_, :].rearrange("p (h q) -> p h q", h=4),
                            in1=C.masks[:, mid, :].unsqueeze(1).broadcast_to([128, 4, 128]), op=ALU.mult),
                            reads=[("PTb", ps_), "masks"], writes=[("PTb", ps_)])
                        P.op("pe", lambda e, kv=kv, kt=kt, ps_=ps_, ki=ki: e.matmul(
                            NUM[64 * kv:64 * kv + 64, :], lhsT=V[:, kt, 64 * kv:64 * kv + 64], rhs=PT[:, ps_, :],
                            start=(ki == 0), stop=False, skip_group_check=True),
                            reads=[("Vb", kt), ("PTb", ps_)], writes=[("ps", 6)])
                        P.op("pe", lambda e, kv=kv, ps_=ps_, ki=ki: e.matmul(
                            DEN[64 * kv:64 * kv + 64, :], lhsT=C.ones_b[:, :], rhs=PT[:, ps_, :],
                            start=(ki == 0), stop=False, skip_group_check=True),
                            reads=[("PTb", ps_), "ones_b"], writes=[("ps", 7)])
                if j == 16:
                    for b in range(16):
                        cs = b % 2
                        P.dma("pool", Kc[:, cs, :], ck_in[L, b], writes=[("Kcb", cs)])
                        P.dma("pool", Vc[:, cs, :], cv_in[L, b], writes=[("Vcb", cs)])
                        pv = psb(3)
                        P.op("pe", lambda e, cs=cs, pv=pv: e.transpose(pv[:, 0:128], Kc[:, cs, :], C.ident_b[:]),
                             reads=[("Kcb", cs), "ident_b"], writes=[("ps", 3)])
                        P.op("act", lambda e, cs=cs, pv=pv: e.activation(out=KcT[:, cs, :], in_=pv[:, 0:128], func=AF.Copy),
                             reads=[("ps", 3)], writes=[("KcTb", cs)])
                        for kv in range(2):
                            P.op("pe", lambda e, kv=kv, cs=cs, b=b: e.matmul(
                                C.ps[4 + kv][:, 0:32].rearrange("p (c t) -> p c t", c=4), lhsT=KcT[64 * kv:64 * kv + 64, cs, :],
                                rhs=qT[64 * kv:64 * kv + 64, s2, :, 8 * b:8 * b + 8], start=True, stop=True, skip_group_check=True),
                                reads=[("KcTb", cs), ("qTb", s2)], writes=[("ps", 4 + kv)])
                            P.op("act", lambda e, cs=cs, kv=kv: e.activation(out=PTs[:, cs, kv * 32:kv * 32 + 32], in_=C.ps[4 + kv][:, 0:32], func=AF.Exp, scale=0.125),
                                 reads=[("ps", 4 + kv)], writes=[("PTsb", cs)])
                        P.op("pool", lambda e, cs=cs: e.tensor_tensor(
                            out=PTs[:, cs, :].rearrange("p (h t) -> p h t", h=8), in0=PTs[:, cs, :].rearrange("p (h t) -> p h t", h=8),
                            in1=C.smask[:, 0:8].unsqueeze(1).broadcast_to([128, 8, 8]), op=ALU.mult),
                            reads=[("PTsb", cs), "smask"], writes=[("PTsb", cs)])
                        for kv in range(2):
                            P.op("pe", lambda e, kv=kv, cs=cs, b=b: e.matmul(
                                NUM[64 * kv:64 * kv + 64, :].rearrange("p (c q) -> p c q", c=4)[:, :, 8 * b:8 * b + 8],
                                lhsT=Vc[:, cs, 64 * kv:64 * kv + 64], rhs=PTs[:, cs, kv * 32:kv * 32 + 32].rearrange("p (c t) -> p c t", c=4),
                                start=False, stop=False, skip_group_check=True),
                                reads=[("Vcb", cs), ("PTsb", cs)], writes=[("ps", 6)])
                            P.op("pe", lambda e, kv=kv, cs=cs, b=b: e.matmul(
                                DEN[64 * kv:64 * kv + 64, :].rearrange("p (c q) -> p c q", c=4)[:, :, 8 * b:8 * b + 8],
                                lhsT=C.ones_b[:, :], rhs=PTs[:, cs, kv * 32:kv * 32 + 32].rearrange("p (c t) -> p c t", c=4),
                                start=False, stop=False, skip_group_check=True),
                                reads=[("PTsb", cs), "ones_b"], writes=[("ps", 7)])
                P.op("dve", lambda e: e.tensor_tensor(out=dtmp[:].rearrange("p (c q) -> p c q", c=4), in0=DEN[:, :].rearrange("p (c q) -> p c q", c=4),
                                                      in1=esink[:, :].unsqueeze(2).broadcast_to([128, 4, 128]), op=ALU.add),
                     reads=[("ps", 7), "esink"], writes=["dtmp"])
                P.op("dve", lambda e: e.reciprocal(out=dtmp[:], in_=dtmp[:]), reads=["dtmp"], writes=["dtmp"])
                P.op("dve", lambda e, j=j: e.tensor_tensor(out=C.obT[:, :, j * 128:(j + 1) * 128], in0=NUM[:, :].rearrange("p (c q) -> p c q", c=4),
                                                           in1=dtmp[:].rearrange("p (c q) -> p c q", c=4), op=ALU.mult),
                     reads=[("ps", 6), "dtmp"], writes=[("obT", j)])

        TG = [(0, 512), (512, 1024), (1024, 1536), (1536, 2048), (2048, 2176)]

        def tg_tiles(c0, c1):
            return list(range(c0 // 128, c1 // 128))

        def mixer_C(L, st):
            WR = WRing(st, 1, 6144, "C")
            ws = WR.next()
            base = 3072
            WR.load(ws, [(0, 8, 256, I["w_in"][L][:, base:base + 256]),
                         (2048, 8, 256, I["w_in"][L][:, base + 256:base + 512]),
                         (4096, 8, 256, I["w_in"][L][:, base + 512:base + 768])])
            Wh, Wb, Wc = WR.view(ws, 0, 8, 256), WR.view(ws, 2048, 8, 256), WR.view(ws, 4096, 8, 256)
            wkey = WR.key(ws)
            cw = sb("cw", [128, 2, 3], F32, st)
            for k in range(3):
                for ch in range(2):
                    P.dma("sp", cw[:, ch, k:k + 1], I["conv_w"][L, k, 128 * ch:128 * ch + 128].rearrange("(p o) -> p o", o=1), writes=["cw"])
            ZC = sb("ZC", [128, 2, 2 + SEQ], F32, st)
            ZCs = sb("ZCs", [128, 2, 16, 10], F32, st)
            hcS = sb("hcS", [128, 2, 512], F32, st)
            ycv = sb("ycv", [128, 2, 512], F32, st)
            st32 = sb("st32", [32, 256], F32, st)
            cstp = sb("cstp", [2, 256], F32, st)
            csts = sb("csts", [32, 256], F32, st)
            ztmp = sb("ztmp", [128, 32], F32, st)
            P.op("dve", lambda e: e.memset(ZC[:, :, 0:2], 0.0), writes=["ZC0"])
            P.dma("sp", st32[:], I["sconv"][L], writes=["st32"])
            for ch in range(2):
                P.op("pe", lambda e, ch=ch: e.transpose(C.ps[3][:, 0:32], st32[:, ch * 128:(ch + 1) * 128], C.ident_f[0:32, 0:32]),
                     reads=["st32", "ident_f"], writes=[("ps", 3)])
                P.op("dve", lambda e, ch=ch: e.tensor_copy(out=ZCs[:, ch, :, 0:2], in_=C.ps[3][:, 0:32].rearrange("p (b r) -> p b r", b=16)),
                     reads=[("ps", 3)], writes=["ZCs0"])
            it = 0
            for (c0, c1) in TG:
                n = c1 - c0
                tiles = [("hT", j) for j in tg_tiles(c0, c1)]
                for ch in range(2):
                    s2 = it % 2
                    it += 1
                    banks = (0, 1, 2) if s2 == 0 else (4, 5, 6)
                    for bi, Wx in enumerate((Wh, Wb, Wc)):
                        for kc in range(8):
                            P.op("pe", lambda e, kc=kc, Wx=Wx, b_=banks[bi], ch=ch: e.matmul(
                                C.ps[b_][:, 0:n], lhsT=Wx[:, kc, ch * 128:(ch + 1) * 128], rhs=C.hT[:, kc, c0:c1],
                                start=(kc == 0), stop=(kc == 7)), reads=tiles + [wkey], writes=[("ps", banks[bi])])
                    P.op("act", lambda e, s2=s2, b_=banks[0]: e.activation(out=hcS[:, s2, 0:n], in_=C.ps[b_][:, 0:n], func=AF.Copy),
                         reads=[("ps", banks[0])], writes=[("hcS", s2)])
                    if c0 < SEQ:
                        zc = ZC[:, ch, 2 + c0:2 + c1]
                        z1 = ZC[:, ch, 1 + c0:1 + c1]
                        z0 = ZC[:, ch, c0:c1]
                        yv = ycv[:, s2, 0:n]
                        hv = hcS[:, s2, 0:n]
                        pc = C.ps[banks[2]][:, 0:n]
                        pb = C.ps[banks[1]][:, 0:n]
                        ov = C.ocT[:, ch, c0:c1]
                    else:
                        zc = ZCs[:, ch, :, 2:10]
                        z1 = ZCs[:, ch, :, 1:9]
                        z0 = ZCs[:, ch, :, 0:8]
                        yv = ycv[:, s2, 0:128].rearrange("p (b t) -> p b t", b=16)
                        hv = hcS[:, s2, 0:128].rearrange("p (b t) -> p b t", b=16)
                        pc = C.ps[banks[2]][:, 0:128].rearrange("p (b t) -> p b t", b=16)
                        pb = C.ps[banks[1]][:, 0:128].rearrange("p (b t) -> p b t", b=16)
                        ov = C.ocT[:, ch, c0:c1].rearrange("p (b t) -> p b t", b=16)
                    zk = ("ZC", ch)
                    P.op("dve", lambda e, zc=zc, pc=pc, hv=hv: e.tensor_tensor(out=zc, in0=pc, in1=hv, op=ALU.mult),
                         reads=[("ps", banks[2]), ("hcS", s2)], writes=[zk])
                    P.op("dve", lambda e, yv=yv, zc=zc, ch=ch: e.tensor_scalar(out=yv, in0=zc, scalar1=cw[:, ch, 2:3], scalar2=None, op0=ALU.mult),
                         reads=[zk, "cw"], writes=[("ycv", s2)])
                    P.op("dve", lambda e, yv=yv, z1=z1, ch=ch: e.scalar_tensor_tensor(out=yv, in0=z1, scalar=cw[:, ch, 1:2], in1=yv, op0=ALU.mult, op1=ALU.add),
                         reads=[zk, "cw", "ZC0", "ZCs0"], writes=[("ycv", s2)])
                    P.op("dve", lambda e, yv=yv, z0=z0, ch=ch: e.scalar_tensor_tensor(out=yv, in0=z0, scalar=cw[:, ch, 0:1], in1=yv, op0=ALU.mult, op1=ALU.add),
                         reads=[zk, "cw", "ZC0", "ZCs0"], writes=[("ycv", s2)])
                    P.op("dve", lambda e, ov=ov, pb=pb, yv=yv: e.tensor_tensor(out=ov, in0=pb, in1=yv, op=ALU.mult),
                         reads=[("ps", banks[1]), ("ycv", s2)], writes=[("ocT", ch, c0)])
            for ch in range(2):
                P.op("pe", lambda e, ch=ch: e.transpose(C.ps[3][0:2, 0:128], ZC[:, ch, SEQ:SEQ + 2], C.ident_f[:]),
                     reads=[("ZC", ch), "ident_f"], writes=[("ps", 3)])
                P.op("dve", lambda e, ch=ch: e.tensor_copy(out=cstp[:, ch * 128:(ch + 1) * 128], in_=C.ps[3][0:2, 0:128]),
                     reads=[("ps", 3)], writes=["cstp"])
                P.op("dve", lambda e, ch=ch: e.tensor_copy(out=ztmp[:].rearrange("p (b r) -> p b r", b=16), in_=ZCs[:, ch, :, 8:10]),
                     reads=[("ZC", ch)], writes=["ztmp"])
                P.op("pe", lambda e, ch=ch: e.transpose(C.ps[3][0:32, 128:256], ztmp[:], C.ident_f[:]),
                     reads=["ztmp", "ident_f"], writes=[("ps", 3)])
                P.op("dve", lambda e, ch=ch: e.tensor_copy(out=csts[:, ch * 128:(ch + 1) * 128], in_=C.ps[3][0:32, 128:256]),
                     reads=[("ps", 3)], writes=["csts"])
            P.dma("sp", O["conv_p"][L], cstp[:], reads=["cstp"])
            P.dma("sp", O["conv_s"][L], csts[:], reads=["csts"])

        TWO_PI = 6.283185307179586
        MAGIC = 12582912.0

        def range_sincos(src, out_sin, out_cos, tmp, keyp, reads):
            P.op("dve", lambda e: e.tensor_scalar(out=tmp, in0=src, scalar1=1.0 / TWO_PI, scalar2=MAGIC, op0=ALU.mult, op1=ALU.add),
                 reads=reads, writes=[keyp + "t"])
            P.op("dve", lambda e: e.tensor_scalar(out=tmp, in0=tmp, scalar1=-MAGIC, scalar2=None, op0=ALU.add), reads=[keyp + "t"], writes=[keyp + "t"])
            P.op("dve", lambda e: e.scalar_tensor_tensor(out=tmp, in0=tmp, scalar=-TWO_PI, in1=src, op0=ALU.mult, op1=ALU.add),
                 reads=[keyp + "t"] + reads, writes=[keyp + "t"])
            P.op("dve", lambda e: e.tensor_scalar(out=tmp, in0=tmp, scalar1=3.141592, scalar2=-3.141592, op0=ALU.min, op1=ALU.max),
                 reads=[keyp + "t"], writes=[keyp + "t"])
            P.op("act", lambda e: e.activation(out=out_sin, in_=tmp, func=AF.Sin), reads=[keyp + "t"], writes=[keyp + "s"])
            P.op("dve", lambda e: e.scalar_tensor_tensor(out=tmp, in0=tmp, scalar=-1.0, in1=tmp, op0=ALU.mult, op1=ALU.max),
                 reads=[keyp + "t", keyp + "s"], writes=[keyp + "t"])
            P.op("act", lambda e: e.activation(out=out_cos, in_=tmp, func=AF.Sin, scale=-1.0, bias=C.halfpi[:, 0:1]),
                 reads=[keyp + "t", "halfpi"], writes=[keyp + "c"])

        def mixer_D(L, st):
            WR = WRing(st, 1, 3072, "D")
            ws = WR.next()
            WR.load(ws, [(0, 8, 256, I["w_in"][L][:, 3840:4096]), (2048, 2, 512, I["w_d_glu"][L])])
            Wu, Wg = WR.view(ws, 0, 8, 256), WR.view(ws, 2048, 2, 512)
            wkey = WR.key(ws)
            udT = C.odT
            ygT = sb("ygT", [128, 2, NTOK], BF16, st)
            it = 0
            for (c0, c1) in TG:
                n = c1 - c0
                tiles = [("hT", j) for j in tg_tiles(c0, c1)]
                for hf in range(2):
                    bk = it % 2
                    it += 1
                    for kc in range(8):
                        P.op("pe", lambda e, kc=kc, hf=hf, bk=bk: e.matmul(C.ps[bk][:, 0:n], lhsT=Wu[:, kc, hf * 128:(hf + 1) * 128], rhs=C.hT[:, kc, c0:c1],
                                                                          start=(kc == 0), stop=(kc == 7)), reads=tiles + [wkey], writes=[("ps", bk)])
                    P.op("act", lambda e, hf=hf, bk=bk: e.activation(out=udT[:, hf, c0:c1], in_=C.ps[bk][:, 0:n], func=AF.Copy),
                         reads=[("ps", bk)], writes=["udT"])
            ps8 = sb("ps8", [8, 3, 128], F32, st)
            ldt = sb("ldt", [8, 2], F32, st)
            LP = sb("LP", [128, 3, 8], F32, st)
            P.dma("sp", ps8[:, 0, :], I["lam_re"][L], writes=["ps8"])
            P.dma("sp", ps8[:, 1, :], I["lam_im"][L], writes=["ps8"])
            P.dma("sp", ldt[:], I["log_dt"][L], writes=["ldt"])
            P.op("dve", lambda e: e.tensor_copy(out=ps8[:, 2, :].rearrange("p (g n) -> p g n", g=2), in_=ldt[:, :].unsqueeze(2).broadcast_to([8, 2, 64])),
                 reads=["ldt"], writes=["ps8"])
            for k in range(3):
                P.op("pe", lambda e, k=k: e.transpose(C.ps[3][:, 8 * k:8 * k + 8], ps8[:, k, :], C.ident_f[0:8, 0:8]),
                     reads=["ps8", "ident_f"], writes=[("ps", 3)])
            P.op("dve", lambda e: e.tensor_copy(out=LP[:].rearrange("p k a -> p (k a)"), in_=C.ps[3][:, 0:24]), reads=[("ps", 3)], writes=["LP"])
            dtv = sb("dtv", [128, 8], F32, st)
            arv = sb("arv", [128, 8], F32, st)
            thv = sb("thv", [128, 8], F32, st)
            P.op("act", lambda e: e.activation(out=dtv[:], in_=LP[:, 2, :], func=AF.Exp), reads=["LP"], writes=["dtv"])
            P.op("dve", lambda e: e.tensor_tensor(out=arv[:], in0=LP[:, 0, :], in1=dtv[:], op=ALU.mult), reads=["LP", "dtv"], writes=["arv"])
            P.op("dve", lambda e: e.tensor_tensor(out=thv[:], in0=LP[:, 1, :], in1=dtv[:], op=ALU.mult), reads=["LP", "dtv"], writes=["thv"])
            kar = sb("kar", [128, 8, 9], F32, st)
            kth = sb("kth", [128, 8, 9], F32, st)
            ktm = sb("ktm", [128, 8, 9], F32, st)
            ksn = sb("ksn", [128, 8, 9], F32, st)
            kcs = sb("kcs", [128, 8, 9], F32, st)
            APW = sb("APW", [128, 2, 8, 9], F32, st)
            kiota = C.iota[:, 0:9].unsqueeze(1).broadcast_to([128, 8, 9])
            P.op("dve", lambda e: e.tensor_tensor(out=kar[:], in0=arv[:, :].unsqueeze(2).broadcast_to([128, 8, 9]), in1=kiota, op=ALU.mult),
                 reads=["arv", "iota"], writes=["kar"])
            P.op("dve", lambda e: e.tensor_tensor(out=kth[:], in0=thv[:, :].unsqueeze(2).broadcast_to([128, 8, 9]), in1=kiota, op=ALU.mult),
                 reads=["thv", "iota"], writes=["kth"])
            P.op("act", lambda e: e.activation(out=kar[:], in_=kar[:], func=AF.Exp), reads=["kar"], writes=["kar"])
            range_sincos(kth[:], ksn[:], kcs[:], ktm[:], "kk", ["kth"])
            P.op("dve", lambda e: e.tensor_tensor(out=APW[:, 0], in0=kar[:], in1=kcs[:], op=ALU.mult), reads=["kar", "kkc"], writes=["APW"])
            P.op("dve", lambda e: e.tensor_tensor(out=APW[:, 1], in0=kar[:], in1=ksn[:], op=ALU.mult), reads=["kar", "kks"], writes=["APW"])
            dump("ssm_APW", APW[:], [128, 2, 8, 9], ["APW"])
            dump("ssm_udT", udT[:], [128, 2, NTOK], ["udT"])
            na8i = sb("na8i", [128, 8], F32, st)
            phi = sb("phi", [128, 8], F32, st)
            P.op("dve", lambda e: e.tensor_scalar(out=na8i[:], in0=APW[:, 1, :, 8], scalar1=-1.0, scalar2=None, op0=ALU.mult), reads=["APW"], writes=["na8i"])
            P.op("dve", lambda e: e.tensor_scalar(out=phi[:], in0=thv[:], scalar1=8.0, scalar2=None, op0=ALU.mult), reads=["thv"], writes=["phi"])
            cf = sb("cf", [128, 6, 8], F32, st)
            lr, li = LP[:, 0, :], LP[:, 1, :]
            P.op("dve", lambda e: e.tensor_scalar(out=cf[:, 0], in0=APW[:, 0, :, 1], scalar1=-1.0, scalar2=None, op0=ALU.add), reads=["APW"], writes=["cf"])
            P.op("dve", lambda e: e.tensor_copy(out=cf[:, 1], in_=APW[:, 1, :, 1]), reads=["APW"], writes=["cf"])
            P.op("dve", lambda e: e.tensor_tensor(out=cf[:, 2], in0=lr, in1=lr, op=ALU.mult), reads=["LP"], writes=["cf"])
            P.op("dve", lambda e: e.tensor_tensor(out=cf[:, 5], in0=li, in1=li, op=ALU.mult), reads=["LP"], writes=["cf"])
            P.op("dve", lambda e: e.tensor_tensor(out=cf[:, 2], in0=cf[:, 2], in1=cf[:, 5], op=ALU.add), writes=["cf"])
            P.op("dve", lambda e: e.reciprocal(out=cf[:, 2], in_=cf[:, 2]), writes=["cf"])
            P.op("dve", lambda e: e.tensor_tensor(out=cf[:, 3], in0=cf[:, 0], in1=lr, op=ALU.mult), writes=["cf"])
            P.op("dve", lambda e: e.tensor_tensor(out=cf[:, 5], in0=cf[:, 1], in1=li, op=ALU.mult), writes=["cf"])
            P.op("dve", lambda e: e.tensor_tensor(out=cf[:, 3], in0=cf[:, 3], in1=cf[:, 5], op=ALU.add), writes=["cf"])
            P.op("dve", lambda e: e.tensor_tensor(out=cf[:, 3], in0=cf[:, 3], in1=cf[:, 2], op=ALU.mult), writes=["cf"])
            P.op("dve", lambda e: e.tensor_tensor(out=cf[:, 4], in0=cf[:, 1], in1=lr, op=ALU.mult), writes=["cf"])
            P.op("dve", lambda e: e.tensor_tensor(out=cf[:, 5], in0=cf[:, 0], in1=li, op=ALU.mult), writes=["cf"])
            P.op("dve", lambda e: e.tensor_tensor(out=cf[:, 4], in0=cf[:, 4], in1=cf[:, 5], op=ALU.subtract), writes=["cf"])
            P.op("dve", lambda e: e.tensor_tensor(out=cf[:, 4], in0=cf[:, 4], in1=cf[:, 2], op=ALU.mult), writes=["cf"])
            Bm = sb("Bm", [128, 2, 8, 16], F32, st)
            bb = sb("bb", [128, 2, 8, 16], F32, st)
            btmp = sb("btmp", [128, 8, 16], F32, st)
            P.dma("sp", Bm[:, 0], I["b_re"][L].rearrange("(a p) j -> p a j", p=128), writes=["Bm"])
            P.dma("sp", Bm[:, 1], I["b_im"][L].rearrange("(a p) j -> p a j", p=128), writes=["Bm"])
            cre = cf[:, 3].unsqueeze(2).broadcast_to([128, 8, 16])
            cim = cf[:, 4].unsqueeze(2).broadcast_to([128, 8, 16])
            P.op("dve", lambda e: e.tensor_tensor(out=bb[:, 0], in0=Bm[:, 0], in1=cre, op=ALU.mult), reads=["Bm", "cf"], writes=["bb"])
            P.op("dve", lambda e: e.tensor_tensor(out=btmp[:], in0=Bm[:, 1], in1=cim, op=ALU.mult), reads=["Bm", "cf"], writes=["btmp"])
            P.op("dve", lambda e: e.tensor_tensor(out=bb[:, 0], in0=bb[:, 0], in1=btmp[:], op=ALU.subtract), writes=["bb"])
            P.op("dve", lambda e: e.tensor_tensor(out=bb[:, 1], in0=Bm[:, 1], in1=cre, op=ALU.mult), reads=["Bm", "cf"], writes=["bb"])
            P.op("dve", lambda e: e.tensor_tensor(out=btmp[:], in0=Bm[:, 0], in1=cim, op=ALU.mult), writes=["btmp"])
            P.op("dve", lambda e: e.tensor_tensor(out=bb[:, 1], in0=bb[:, 1], in1=btmp[:], op=ALU.add), writes=["bb"])
            dump("ssm_bb", bb[:], [128, 2, 8, 16], ["bb"])
            dump("ssm_cf", cf[:], [128, 6, 8], ["cf", "bb"])
            dump("ssm_Bm", Bm[:], [128, 2, 8, 16], ["Bm", "bb"])
            dump("ssm_LP", LP[:], [128, 3, 8], ["LP", "bb"])
            Cst = sb("Cst", [128, 2, 2, 64], F32, st)
            CT = sb("CT", [128, 2, 8, 16], F32, st)
            P.dma("sp", Cst[:, 0], I["c_re"][L].rearrange("(h p) n -> p h n", p=128), writes=["Cst"])
            P.dma("sp", Cst[:, 1], I["c_im"][L].rearrange("(h p) n -> p h n", p=128), writes=["Cst"])
            for ri in range(2):
                for hf in range(2):
                    P.op("pe", lambda e, ri=ri, hf=hf: e.transpose(C.ps[3][0:64, 256 * ri + 128 * hf:256 * ri + 128 * hf + 128], Cst[:, ri, hf, :], C.ident_f[:]),
                         reads=["Cst", "ident_f"], writes=[("ps", 3)])
            for ri in range(2):
                for hf in range(2):
                    for g2 in range(2):
                        src = C.ps[3][0:64, 256 * ri + 128 * hf:256 * ri + 128 * hf + 128].rearrange("p (a g i) -> p a g i", a=4, g=2)[:, :, g2, :]
                        P.op("dve", lambda e, ri=ri, hf=hf, g2=g2, src=src: e.tensor_copy(out=CT[64 * g2:64 * g2 + 64, ri, 4 * hf:4 * hf + 4, :], in_=src),
                             reads=[("ps", 3)], writes=["CT"])
            dump("ssm_CT", CT[:], [128, 2, 8, 16], ["CT"])
            dcol = sb("dcol", [128, 2], F32, st)
            for hf in range(2):
                P.dma("sp", dcol[:, hf:hf + 1], I["ssm_d"][L, hf].rearrange("(p o) -> p o", o=1), writes=["dcol"])
            x0s = sb("x0s", [16, 1024], F32, st)
            X0 = sb("X0", [128, 2, 8, 16], F32, st)
            for ri in range(2):
                P.dma("sp", x0s[:, :], I["sdre" if ri == 0 else "sdim"][L], writes=["x0s"])
                for pr in range(8):
                    P.op("pe", lambda e, ri=ri, pr=pr: e.transpose(C.ps[2][:, (ri * 8 + pr) * 16:(ri * 8 + pr) * 16 + 16], x0s[:, pr * 128:(pr + 1) * 128], C.ident_f[0:16, 0:16]),
                         reads=["x0s", "ident_f"], writes=[("ps", 2)])
            P.op("dve", lambda e: e.tensor_copy(out=X0[:].rearrange("p r a b -> p (r a b)"), in_=C.ps[2][:, 0:256]), reads=[("ps", 2)], writes=["X0"])
            Mx = sb("Mx", [128, 8, 128], BF16, st)
            CAx = sb("CAx", [128, 4, 9, 2, 64], BF16, st)
            Xprev = sb("Xprev", [128, 2, 4, 272], BF16, st)
            XF = sb("XF", [128, 2, 8], F32, st)
            XFs = sb("XFs", [128, 2, 8, 16], F32, st)
            P.op("pool", lambda e: e.memset(Xprev[:, :, :, 0:1], 0.0), writes=["Xprev0"])
            BA = sb("BA", [128, 2, 8, 16], F32, st)
            BAx = sb("BAx", [128, 2, 8, 2, 64], BF16, st)
            SMt = sb("SMt", [128, 16, 128], BF16, st)
            CA = sb("CA", [128, 2, 9, 16], F32, st)
            tmpa = sb("tmpa", [128, 9, 16], F32, st)
            EC = sb("EC", [128, 256], F32, st)
            ES = sb("ES", [128, 256], F32, st)
            ang = sb("ang", [128, 256], F32, st)
            atm = sb("atm", [128, 256], F32, st)
            SR = sb("SR", [128, 2, 256], F32, st)
            Wt = sb("Wt", [128, 2, 256], F32, st)
            XS = sb("XS", [128, 2, 272], F32, st)
            t256 = sb("t256", [128, 256], F32, st)
            P.op("pool", lambda e: e.memset(BAx[:], 0.0), writes=["BAx0"])
            ity = 0
            for pr in range(8):
                hf, pl = pr // 4, pr % 4
                if pl == 0:
                    ykeys = [("ps", 4), ("ps", 5)]
                    P.op("pool", lambda e: e.memset(Mx[:], 0.0), writes=["Mx", "Mxd"] + [("Mxp", k) for k in range(4)])
                    P.op("pool", lambda e: e.memset(CAx[:], 0.0), writes=["CAx"] + [("CAx", k) for k in range(4)])
                quad, q = pl // 2, pl % 2
                bs = pr % 2
                po = 64 * quad
                apr = APW[:, 0, pr, 0:8].unsqueeze(2).broadcast_to([128, 8, 16])
                api = APW[:, 1, pr, 0:8].unsqueeze(2).broadcast_to([128, 8, 16])
                bbr = bb[:, 0, pr, :].unsqueeze(1).broadcast_to([128, 8, 16])
                bbi = bb[:, 1, pr, :].unsqueeze(1).broadcast_to([128, 8, 16])
                t8 = tmpa[:, 0:8, :]
                P.op("dve", lambda e, apr=apr, bbr=bbr: e.tensor_tensor(out=BA[:, 0], in0=apr, in1=bbr, op=ALU.mult), reads=["APW", "bb"], writes=["BA"])
                P.op("dve", lambda e, api=api, bbi=bbi, t8=t8: e.tensor_tensor(out=t8, in0=api, in1=bbi, op=ALU.mult), reads=["APW", "bb"], writes=["tmpa"])
                P.op("dve", lambda e, t8=t8: e.tensor_tensor(out=BA[:, 0], in0=BA[:, 0], in1=t8, op=ALU.subtract), writes=["BA"])
                P.op("dve", lambda e, apr=apr, bbi=bbi: e.tensor_tensor(out=BA[:, 1], in0=apr, in1=bbi, op=ALU.mult), reads=["APW", "bb"], writes=["BA"])
                P.op("dve", lambda e, api=api, bbr=bbr, t8=t8: e.tensor_tensor(out=t8, in0=api, in1=bbr, op=ALU.mult), writes=["tmpa"])
                P.op("dve", lambda e, t8=t8: e.tensor_tensor(out=BA[:, 1], in0=BA[:, 1], in1=t8, op=ALU.add), writes=["BA"])
                for ri in range(2):
                    for g2 in range(2):
                        P.op("dve", lambda e, ri=ri, g2=g2, bs=bs, q=q: e.tensor_scalar(
                            out=BAx[:, bs, :, ri, 32 * q + 16 * g2:32 * q + 16 * g2 + 16], in0=BA[:, ri], scalar1=C.gmask[:, g2:g2 + 1], scalar2=None, op0=ALU.mult),
                            reads=["BA", "gmask", "BAx0", ("BAx", bs)], writes=[("BAx", bs)])
                for hb in range(2):
                    pv = psb(4 + hb)
                    for k8 in range(8):
                        idx = hb * 8 + k8
                        s_, ri = idx // 2, idx % 2
                        P.op("pe", lambda e, pv=pv, k8=k8, s_=s_, ri=ri, bs=bs, po=po: e.transpose(
                            pv[po:po + 64, k8 * 128:(k8 + 1) * 128], BAx[:, bs, s_, ri, :], C.ident_b[:]),
                            reads=[("BAx", bs), "ident_b"], writes=[("ps", 4 + hb)])
                    P.op("act", lambda e, pv=pv, hb=hb, bs=bs, po=po: e.activation(
                        out=SMt[po:po + 64, hb * 8:hb * 8 + 8, :], in_=pv[po:po + 64, :].rearrange("p (a m) -> p a m", a=8), func=AF.Copy),
                        reads=[("ps", 4 + hb)], writes=["SMt"])
                udv = udT[:, hf, :].rearrange("p (c s) -> p c s", s=8)
                for ri in range(2):
                    for s in range(8):
                        P.op("pe", lambda e, ri=ri, s=s, bs=bs, po=po, udv=udv: e.matmul(
                            C.ps[ri][:, 0:272], lhsT=SMt[po:po + 64, (7 - s) * 2 + ri, :], rhs=udv[po:po + 64, :, s],
                            start=(s == 0), stop=(s == 7)), reads=["SMt", "udT"], writes=[("ps", ri)])
                P.op("dve", lambda e, pr=pr: e.tensor_scalar(out=ang[:], in0=C.iota[:, 0:256], scalar1=phi[:, pr:pr + 1], scalar2=None, op0=ALU.mult),
                     reads=["iota", "phi"], writes=["ang"])
                range_sincos(ang[:], ES[:], EC[:], atm[:], "rot", ["ang"])
                Sre, Sim = C.ps[0][:, 0:256], C.ps[1][:, 0:256]
                P.op("dve", lambda e: e.tensor_tensor(out=SR[:, 0], in0=Sre, in1=EC[:], op=ALU.mult), reads=[("ps", 0), "rotc"], writes=["SR"])
                P.op("dve", lambda e: e.tensor_tensor(out=t256[:], in0=Sim, in1=ES[:], op=ALU.mult), reads=[("ps", 1), "rots"], writes=["t256"])
                P.op("dve", lambda e: e.tensor_tensor(out=SR[:, 0], in0=SR[:, 0], in1=t256[:], op=ALU.add), writes=["SR"])
                P.op("dve", lambda e: e.tensor_tensor(out=SR[:, 1], in0=Sim, in1=EC[:], op=ALU.mult), reads=[("ps", 1), "rotc"], writes=["SR"])
                P.op("dve", lambda e: e.tensor_tensor(out=t256[:], in0=Sre, in1=ES[:], op=ALU.mult), reads=[("ps", 0), "rots"], writes=["t256"])
                P.op("dve", lambda e: e.tensor_tensor(out=SR[:, 1], in0=SR[:, 1], in1=t256[:], op=ALU.subtract), writes=["SR"])
                rho8 = APW[:, 0, pr, 8:9]
                for ri in range(2):
                    P.op("dve", lambda e, ri=ri, pr=pr: e.tensor_tensor_scan(out=Wt[:, ri], data0=kar[:, pr, 8:9].broadcast_to([128, 256]), data1=SR[:, ri],
                                                                           initial=0.0, op0=ALU.mult, op1=ALU.add), reads=["SR", "kar"], writes=["Wt"])
                P.op("dve", lambda e: e.tensor_tensor(out=XS[:, 0, 0:256], in0=Wt[:, 0], in1=EC[:], op=ALU.mult), reads=["Wt", "rotc"], writes=["XS"])
                P.op("dve", lambda e: e.tensor_tensor(out=t256[:], in0=Wt[:, 1], in1=ES[:], op=ALU.mult), reads=["Wt", "rots"], writes=["t256"])
                P.op("dve", lambda e: e.tensor_tensor(out=XS[:, 0, 0:256], in0=XS[:, 0, 0:256], in1=t256[:], op=ALU.subtract), writes=["XS"])
                P.op("dve", lambda e: e.tensor_tensor(out=XS[:, 1, 0:256], in0=Wt[:, 1], in1=EC[:], op=ALU.mult), reads=["Wt", "rotc"], writes=["XS"])
                P.op("dve", lambda e: e.tensor_tensor(out=t256[:], in0=Wt[:, 0], in1=ES[:], op=ALU.mult), writes=["t256"])
                P.op("dve", lambda e: e.tensor_tensor(out=XS[:, 1, 0:256], in0=XS[:, 1, 0:256], in1=t256[:], op=ALU.add), writes=["XS"])
                a8r, a8i = APW[:, 0, pr, 8:9], APW[:, 1, pr, 8:9]
                P.op("dve", lambda e, pr=pr, a8r=a8r: e.tensor_scalar(out=XS[:, 0, 256:272], in0=X0[:, 0, pr, :], scalar1=a8r, scalar2=None, op0=ALU.mult),
                     reads=["X0", "APW"], writes=["XS"])
                P.op("dve", lambda e, pr=pr: e.scalar_tensor_tensor(out=XS[:, 0, 256:272], in0=X0[:, 1, pr, :], scalar=na8i[:, pr:pr + 1], in1=XS[:, 0, 256:272],
                                                                    op0=ALU.mult, op1=ALU.add), reads=["X0", "na8i"], writes=["XS"])
                P.op("dve", lambda e: e.tensor_tensor(out=XS[:, 0, 256:272], in0=C.ps[0][:, 256:272], in1=XS[:, 0, 256:272], op=ALU.add), reads=[("ps", 0)], writes=["XS"])
                P.op("dve", lambda e, pr=pr, a8r=a8r: e.tensor_scalar(out=XS[:, 1, 256:272], in0=X0[:, 1, pr, :], scalar1=a8r, scalar2=None, op0=ALU.mult),
                     reads=["X0", "APW"], writes=["XS"])
                P.op("dve", lambda e, pr=pr, a8i=a8i: e.scalar_tensor_tensor(out=XS[:, 1, 256:272], in0=X0[:, 0, pr, :], scalar=a8i, in1=XS[:, 1, 256:272],
                                                                             op0=ALU.mult, op1=ALU.add), reads=["X0", "APW"], writes=["XS"])
                P.op("dve", lambda e: e.tensor_tensor(out=XS[:, 1, 256:272], in0=C.ps[1][:, 256:272], in1=XS[:, 1, 256:272], op=ALU.add), reads=[("ps", 1)], writes=["XS"])
                for ri in range(2):
                    P.op("act", lambda e, ri=ri, pl=pl: e.activation(out=Xprev[:, ri, pl, 1:256], in_=XS[:, ri, 0:255], func=AF.Copy), reads=["XS"], writes=[("Xprev", pl)])
                    P.op("act", lambda e, ri=ri, pr=pr, pl=pl: e.activation(out=Xprev[:, ri, pl, 256:272], in_=X0[:, ri, pr, :], func=AF.Copy), reads=["X0"], writes=[("Xprev", pl)])
                    P.op("act", lambda e, ri=ri, pr=pr: e.activation(out=XF[:, ri, pr:pr + 1], in_=XS[:, ri, 255:256], func=AF.Copy), reads=["XS"], writes=["XF"])
                    P.op("act", lambda e, ri=ri, pr=pr: e.activation(out=XFs[:, ri, pr, :], in_=XS[:, ri, 256:272], func=AF.Copy), reads=["XS"], writes=["XFs"])
                if pr == 0:
                    dump("ssm_XS0", XS[:], [128, 2, 272], ["XS"])
                    dump("ssm_SR0", SR[:], [128, 2, 256], ["SR"])
                    dump("ssm_EC0", EC[:], [128, 256], ["rotc"])
                    dump("ssm_ES0", ES[:], [128, 256], ["rots"])
                    dump("ssm_SMt0", SMt[:], [128, 16, 128], ["SMt"])
                apr9 = APW[:, 0, pr, :].unsqueeze(2).broadcast_to([128, 9, 16])
                api9 = APW[:, 1, pr, :].unsqueeze(2).broadcast_to([128, 9, 16])
                ctr = CT[:, 0, pr, :].unsqueeze(1).broadcast_to([128, 9, 16])
                cti = CT[:, 1, pr, :].unsqueeze(1).broadcast_to([128, 9, 16])
                P.op("dve", lambda e, apr9=apr9, ctr=ctr: e.tensor_tensor(out=CA[:, 0], in0=apr9, in1=ctr, op=ALU.mult), reads=["APW", "CT"], writes=["CA"])
                P.op("dve", lambda e, api9=api9, cti=cti: e.tensor_tensor(out=tmpa[:], in0=api9, in1=cti, op=ALU.mult), reads=["APW", "CT"], writes=["tmpa"])
                P.op("dve", lambda e: e.tensor_tensor(out=CA[:, 0], in0=CA[:, 0], in1=tmpa[:], op=ALU.subtract), writes=["CA"])
                P.op("dve", lambda e, api9=api9, ctr=ctr: e.tensor_tensor(out=CA[:, 1], in0=api9, in1=ctr, op=ALU.mult), reads=["APW", "CT"], writes=["CA"])
                P.op("dve", lambda e, apr9=apr9, cti=cti: e.tensor_tensor(out=tmpa[:], in0=apr9, in1=cti, op=ALU.mult), writes=["tmpa"])
                P.op("dve", lambda e: e.tensor_tensor(out=CA[:, 1], in0=CA[:, 1], in1=tmpa[:], op=ALU.add), writes=["CA"])
                for ri in range(2):
                    for g2 in range(2):
                        P.op("dve", lambda e, ri=ri, g2=g2, pl=pl, q=q: e.tensor_scalar(
                            out=CAx[:, pl, :, ri, 32 * q + 16 * g2:32 * q + 16 * g2 + 16], in0=CA[:, ri], scalar1=C.gmask[:, g2:g2 + 1],
                            scalar2=(1.0 if ri == 0 else -1.0), op0=ALU.mult, op1=ALU.mult),
                            reads=["CA", "gmask", "CAx"], writes=[("CAx", pl)])
                for ri in range(2):
                    P.op("pe", lambda e, ri=ri, bs=bs, pl=pl, q=q, po=po: e.matmul(
                        C.ps[2][po:po + 64, 0:256].rearrange("p (t c) -> p t c", t=8), lhsT=BAx[:, bs, 0, ri, :],
                        rhs=CAx[:, pl, 0:8, ri, 32 * q:32 * q + 32], start=(ri == 0), stop=(ri == 1)),
                        reads=[("BAx", bs), ("CAx", pl)], writes=[("ps", 2)])
                P.op("act", lambda e, pl=pl, po=po: e.activation(
                    out=Mx[po:po + 64, :, 32 * pl:32 * pl + 32], in_=C.ps[2][po:po + 64, 0:256].rearrange("p (t c) -> p t c", t=8), func=AF.Copy),
                    reads=[("ps", 2), "Mx"], writes=[("Mxp", pl)])
                if pl < 3:
                    continue
                P.op("dve", lambda e, hf=hf: e.scalar_tensor_tensor(out=Mx[:, 0, :], in0=C.ident_f[:], scalar=dcol[:, hf:hf + 1], in1=Mx[:, 0, :],
                                                                    op0=ALU.mult, op1=ALU.add),
                     reads=["ident_f", "dcol"] + [("Mxp", k) for k in range(4)], writes=["Mxd"])
                if hf == 0:
                    dump("ssm_Mx0", Mx[:], [128, 8, 128], ["Mxd"])
                    dump("ssm_Xprev0", Xprev[:], [128, 2, 4, 272], [("Xprev", k) for k in range(4)] + ["Xprev0"])
                udv = udT[:, hf, :].rearrange("p (c s) -> p c s", s=8)
                ygv = ygT[:, hf, :].rearrange("p (c s) -> p c s", s=8)
                for t in range(8):
                    bk = 4 + (ity % 2)
                    ity += 1
                    Y = C.ps[bk]
                    for s in range(t + 1):
                        P.op("pe", lambda e, Y=Y, t=t, s=s, udv=udv: e.matmul(
                            Y[:, 0:272], lhsT=Mx[:, t - s, :], rhs=udv[:, :, s], start=(s == 0), stop=False, skip_group_check=True),
                            reads=["Mxd", "udT"], writes=[("ps", bk)])
                    for pl2 in range(4):
                        po2 = 64 * (pl2 // 2)
                        for ri in range(2):
                            P.op("pe", lambda e, Y=Y, pl2=pl2, po2=po2, ri=ri, t=t: e.matmul(
                                Y[po2:po2 + 64, 0:272], lhsT=CAx[:, pl2, t + 1, ri, :], rhs=Xprev[:, ri, pl2, :], start=False, stop=False, skip_group_check=True),
                                reads=[("CAx", pl2), ("Xprev", pl2), "Xprev0"], writes=[("ps", bk)])
                    P.op("act", lambda e, Y=Y, ygv=ygv, t=t: e.activation(out=ygv[:, :, t], in_=Y[:, 0:272], func=AF.Gelu_apprx_tanh),
                         reads=[("ps", bk)], writes=["ygT"])
            dump("ssm_ygT", ygT[:], [128, 2, NTOK], ["ygT"])
            sg = sb("sg", [128, 2, 512], BF16, st)
            it = 0
            for (c0, c1) in TG:
                n = c1 - c0
                for vc in range(2):
                    s2 = it % 2
                    it += 1
                    bv_, bg_ = (0, 1) if s2 == 0 else (2, 3)
                    for kc in range(2):
                        P.op("pe", lambda e, kc=kc, vc=vc, bv_=bv_: e.matmul(C.ps[bv_][:, 0:n], lhsT=Wg[:, kc, vc * 128:(vc + 1) * 128], rhs=ygT[:, kc, c0:c1],
                                                                           start=(kc == 0), stop=(kc == 1)), reads=["ygT", wkey], writes=[("ps", bv_)])
                    for kc in range(2):
                        P.op("pe", lambda e, kc=kc, vc=vc, bg_=bg_: e.matmul(C.ps[bg_][:, 0:n], lhsT=Wg[:, kc, 256 + vc * 128:256 + (vc + 1) * 128], rhs=ygT[:, kc, c0:c1],
                                                                           start=(kc == 0), stop=(kc == 1)), reads=["ygT", wkey], writes=[("ps", bg_)])
                    P.op("act", lambda e, s2=s2, bg_=bg_: e.activation(out=sg[:, s2, 0:n], in_=C.ps[bg_][:, 0:n], func=AF.Sigmoid),
                         reads=[("ps", bg_)], writes=[("sg", s2)])
                    P.op("dve", lambda e, s2=s2, bv_=bv_, vc=vc: e.tensor_tensor(out=C.odT[:, vc, c0:c1], in0=C.ps[bv_][:, 0:n], in1=sg[:, s2, 0:n], op=ALU.mult),
                         reads=[("ps", bv_), ("sg", s2)], writes=[("odT", vc, c0), "udT"])
            xfo = sb("xfo", [8, 2, 128], F32, st)
            xfs = sb("xfs", [16, 1024], F32, st)
            for ri in range(2):
                P.op("pe", lambda e, ri=ri: e.transpose(C.ps[6][0:8, 128 * ri:128 * ri + 128], XF[:, ri, :], C.ident_f[:]),
                     reads=["XF", "ident_f"], writes=[("ps", 6)])
            P.op("dve", lambda e: e.tensor_copy(out=xfo[:].rearrange("p r n -> p (r n)"), in_=C.ps[6][0:8, 0:256]), reads=[("ps", 6)], writes=["xfo"])
            P.dma("sp", O["dre_p"][L].rearrange("(a n) -> a n", n=128), xfo[:, 0, :], reads=["xfo"])
            P.dma("sp", O["dim_p"][L].rearrange("(a n) -> a n", n=128), xfo[:, 1, :], reads=["xfo"])
            for ri in range(2):
                for hb in range(2):
                    for k4 in range(4):
                        pr = hb * 4 + k4
                        P.op("pe", lambda e, ri=ri, pr=pr, k4=k4: e.transpose(C.ps[7][0:16, 128 * k4:128 * k4 + 128], XFs[:, ri, pr, :], C.ident_f[:]),
                             reads=["XFs", "ident_f"], writes=[("ps", 7)])
                    P.op("dve", lambda e, ri=ri, hb=hb: e.tensor_copy(out=xfs[:, 512 * hb:512 * hb + 512], in_=C.ps[7][0:16, 0:512]),
                         reads=[("ps", 7)], writes=["xfs"])
                P.dma("sp", O["dre_s" if ri == 0 else "dim_s"][L], xfs[:, :], reads=["xfs"])

        HALVES = [[(0, 512), (512, 1024)], [(1024, 1536), (1536, 2048), (2048, 2176)]]

        def finalize_A(st):
            rd = sb("rdA", [128, 2, 128], F32, st)
            nf = sb("nfA", [128, 2, 128], F32, st)
            for j in range(NT):
                nv = C.numA[:, :, j * 128:(j + 1) * 128]
                dv = C.denA[:, :, j * 128:(j + 1) * 128]
                P.op("dve", lambda e, dv=dv: e.tensor_copy(out=rd[:], in_=dv), reads=[("denA", j)], writes=["rdA"])
                P.op("dve", lambda e: e.reciprocal(out=rd[:], in_=rd[:]), reads=["rdA"], writes=["rdA"])
                P.op("dve", lambda e, nv=nv: e.tensor_copy(out=nf[:], in_=nv), reads=[("numA", j)], writes=["nfA"])
                P.op("dve", lambda e: e.tensor_tensor(out=nf[:], in0=nf[:], in1=rd[:], op=ALU.mult), reads=["nfA", "rdA"], writes=["nfA"])
                P.op("dve", lambda e, nv=nv: e.tensor_copy(out=nv, in_=nf[:]), reads=["nfA"], writes=[("numA", j)])

        def merge_half(L, H, st):
            groups = HALVES[H]
            hc0 = groups[0][0]
            ncol = groups[-1][1] - hc0
            mT = sb("mergedT", [128, 8, ncol], BF16, st)
            sgm = sb("sgm", [128, 2, 512], BF16, st)
            tmpm = sb("tmpm", [128, 2, 512], BF16, st)
            WR = WRing(st, 2, 6144, "M")
            srcs = [(C.numA, 2, "numA"), (C.obT, 4, "obT"), (C.ocT, 2, "ocT"), (C.odT, 2, "odT")]
            brw = ["w_br_a", "w_br_b", "w_br_c", "w_br_d"]
            it = 0
            for fcg in range(2):
                for bi_, i in enumerate(DBG_M_BR):
                    oT, kci, _ = srcs[i]
                    ws = WR.next()
                    WR.load(ws, [(0, 8, 512, I["w_in"][L][:, GATE_OFF + 1024 * i + 512 * fcg:GATE_OFF + 1024 * i + 512 * fcg + 512]),
                                 (4096, kci, 512, I[brw[i]][L][:, 512 * fcg:512 * fcg + 512])])
                    Wg_, Wb_ = WR.view(ws, 0, 8, 512), WR.view(ws, 4096, kci, 512)
                    wkey = WR.key(ws)
                    for (c0, c1) in groups:
                        n = c1 - c0
                        tl = tg_tiles(c0, c1)
                        hkeys = [("hT", j) for j in tl]
                        for fc in range(4):
                            s2 = it % 2
                            it += 1
                            bg_, bp_ = (0, 1) if s2 == 0 else (2, 3)
                            for kc in range(8):
                                P.op("pe", lambda e, kc=kc, fc=fc, bg_=bg_, Wg_=Wg_, c0=c0, c1=c1, n=n: e.matmul(
                                    C.ps[bg_][:, 0:n], lhsT=Wg_[:, kc, fc * 128:(fc + 1) * 128], rhs=C.hT[:, kc, c0:c1], start=(kc == 0), stop=(kc == 7)),
                                    reads=hkeys + [wkey], writes=[("ps", bg_)])
                            for kc in range(kci):
                                P.op("pe", lambda e, kc=kc, fc=fc, bp_=bp_, Wb_=Wb_, oT=oT, c0=c0, c1=c1, n=n, kci=kci: e.matmul(
                                    C.ps[bp_][:, 0:n], lhsT=Wb_[:, kc, fc * 128:(fc + 1) * 128], rhs=oT[:, kc, c0:c1], start=(kc == 0), stop=(kc == kci - 1)),
                                    reads=["mix_out", wkey], writes=[("ps", bp_)])
                            P.op("act", lambda e, s2=s2, bg_=bg_, n=n: e.activation(out=sgm[:, s2, 0:n], in_=C.ps[bg_][:, 0:n], func=AF.Sigmoid),
                                 reads=[("ps", bg_)], writes=[("sgm", s2)])
                            mv = mT[:, 4 * fcg + fc, c0 - hc0:c1 - hc0]
                            mk = ("mT", 4 * fcg + fc, c0)
                            if bi_ == 0:
                                P.op("dve", lambda e, mv=mv, bp_=bp_, s2=s2, n=n: e.tensor_tensor(out=mv, in0=C.ps[bp_][:, 0:n], in1=sgm[:, s2, 0:n], op=ALU.mult),
                                     reads=[("ps", bp_), ("sgm", s2)], writes=[mk])
                            else:
                                P.op("dve", lambda e, bp_=bp_, s2=s2, n=n: e.tensor_tensor(out=tmpm[:, s2, 0:n], in0=C.ps[bp_][:, 0:n], in1=sgm[:, s2, 0:n], op=ALU.mult),
                                     reads=[("ps", bp_), ("sgm", s2)], writes=[("tmpm", s2)])
                                P.op("dve", lambda e, mv=mv, s2=s2, n=n: e.tensor_tensor(out=mv, in0=tmpm[:, s2, 0:n], in1=mv, op=ALU.add),
                                     reads=[mk, ("tmpm", s2)], writes=[mk])
            if H == 0:
                dump("mT_%d" % L, mT[:], [128, 8, ncol], [("mT", f, c0) for f in range(8) for (c0, c1) in groups])
            P.dma("sp", C.gvec[:, 0, :], I["n_mix_post"][L].partition_broadcast(128), writes=[("g", 0)])
            wsl = []
            for nh in range(2):
                ws = WR.next()
                WR.load(ws, [(0, 8, 512, I["w_out"][L][:, 512 * nh:512 * nh + 512])])
                wsl.append((WR.view(ws, 0, 8, 512), WR.key(ws)))
            mkeys = [("mT", f, c0) for f in range(8) for (c0, c1) in groups]
            tiles = [j for (c0, c1) in groups for j in tg_tiles(c0, c1)]
            post_norm_residual(tiles, lambda kc, j: mT[:, kc, j * 128 - hc0:j * 128 - hc0 + 128], 8, wsl, mkeys, st)

        def post_norm_residual(tiles, lhs_fn, nk, wsl, in_keys, st):
            ss2 = sb("ss2", [128, 8], F32, st)
            junk = sb("junk_p", [128, 512], BF16, st)
            tmpx = sb("tmpx", [128, 2, 512], F32, st)
            for j in tiles:
                s2 = j % 2
                banks = (0, 1) if s2 == 0 else (2, 3)
                for nh in range(2):
                    Wv, wk = wsl[nh]
                    for kc in range(nk):
                        P.op("pe", lambda e, kc=kc, j=j, b_=banks[nh], Wv=Wv: e.matmul(
                            C.ps[b_][:, :], lhsT=lhs_fn(kc, j), rhs=Wv[:, kc, :], start=(kc == 0), stop=(kc == nk - 1)),
                            reads=in_keys + [wk], writes=[("ps", banks[nh])])
                    P.op("act", lambda e, b_=banks[nh], nh=nh: e.activation(out=junk[:], in_=C.ps[b_][:, :], func=AF.Square, accum_out=ss2[:, nh:nh + 1]),
                         reads=[("ps", banks[nh])], writes=["junk_p", "ss2"])
                P.op("dve", lambda e: e.tensor_scalar(out=ss2[:, 2:4], in0=ss2[:, 0:2], scalar1=ss2[:, 1:2], scalar2=None, op0=ALU.add), reads=["ss2"], writes=["ss2b"])
                P.op("dve", lambda e: e.tensor_scalar(out=ss2[:, 2:4], in0=ss2[:, 2:4], scalar1=1.0 / D, scalar2=EPS, op0=ALU.mult, op1=ALU.add),
                     reads=["ss2b"], writes=["ss2b"])
                P.op("act", lambda e: e.activation(out=ss2[:, 2:4], in_=ss2[:, 2:4], func=AF.Sqrt), reads=["ss2b"], writes=["ss2b"])
                P.op("dve", lambda e: e.reciprocal(out=ss2[:, 4:6], in_=ss2[:, 2:4]), reads=["ss2b"], writes=["ss2c"])
                for nh in range(2):
                    P.op("dve", lambda e, b_=banks[nh], nh=nh: e.scalar_tensor_tensor(
                        out=tmpx[:, nh, :], in0=C.ps[b_][:, :], scalar=ss2[:, 4:5], in1=C.gvec[:, 0, 512 * nh:512 * nh + 512], op0=ALU.mult, op1=ALU.mult),
                        reads=[("ps", banks[nh]), "ss2c", ("g", 0)], writes=[("tmpx", nh)])
                    P.op("pool", lambda e, j=j, nh=nh: e.tensor_tensor(out=C.X[:, j, 512 * nh:512 * nh + 512], in0=C.X[:, j, 512 * nh:512 * nh + 512],
                                                                     in1=tmpx[:, nh, :], op=ALU.add),
                         reads=[("X", j), ("tmpx", nh)], writes=[("X", j)])

        def ffn_half(L, H, st):
            groups = HALVES[H]
            hc0 = groups[0][0]
            ncol = groups[-1][1] - hc0
            tiles = [j for (c0, c1) in groups for j in tg_tiles(c0, c1)]
            hTh = sb("hTh", [128, 8, ncol], BF16, st)
            aT = sb("aT", [128, NFF, ncol], BF16, st)
            sil = sb("sil", [128, 2, 512], BF16, st)
            norm_to_hT(L, "n_ffn_pre", tiles, st, hTh, hc0)
            WR = WRing(st, 2, 6144, "F")
            it = 0
            for ffg in range(NFF // 2):
                ws = WR.next()
                WR.load(ws, [(0, 8, 256, I["w_ffn_gate"][L][:, 256 * ffg:256 * ffg + 256]),
                             (2048, 8, 256, I["w_ffn_up"][L][:, 256 * ffg:256 * ffg + 256])])
                Wg_, Wu_ = WR.view(ws, 0, 8, 256), WR.view(ws, 2048, 8, 256)
                wkey = WR.key(ws)
                for (c0, c1) in groups:
                    n = c1 - c0
                    hkeys = [("hT", j) for j in tg_tiles(c0, c1)]
                    for fc in range(2):
                        s2 = it % 2
                        it += 1
                        bg_, bu_ = (0, 1) if s2 == 0 else (2, 3)
                        for kc in range(8):
                            P.op("pe", lambda e, kc=kc, fc=fc, bg_=bg_, Wg_=Wg_, c0=c0, c1=c1, n=n: e.matmul(
                                C.ps[bg_][:, 0:n], lhsT=Wg_[:, kc, fc * 128:(fc + 1) * 128], rhs=hTh[:, kc, c0 - hc0:c1 - hc0], start=(kc == 0), stop=(kc == 7)),
                                reads=hkeys + [wkey], writes=[("ps", bg_)])
                        for kc in range(8):
                            P.op("pe", lambda e, kc=kc, fc=fc, bu_=bu_, Wu_=Wu_, c0=c0, c1=c1, n=n: e.matmul(
                                C.ps[bu_][:, 0:n], lhsT=Wu_[:, kc, fc * 128:(fc + 1) * 128], rhs=hTh[:, kc, c0 - hc0:c1 - hc0], start=(kc == 0), stop=(kc == 7)),
                                reads=hkeys + [wkey], writes=[("ps", bu_)])
                        P.op("act", lambda e, s2=s2, bg_=bg_, n=n: e.activation(out=sil[:, s2, 0:n], in_=C.ps[bg_][:, 0:n], func=AF.Silu),
                             reads=[("ps", bg_)], writes=[("sil", s2)])
                        P.op("dve", lambda e, s2=s2, bu_=bu_, n=n, ffc=2 * ffg + fc, c0=c0, c1=c1: e.tensor_tensor(
                            out=aT[:, ffc, c0 - hc0:c1 - hc0], in0=C.ps[bu_][:, 0:n], in1=sil[:, s2, 0:n], op=ALU.mult),
                            reads=[("ps", bu_), ("sil", s2)], writes=[("aT", 2 * ffg + fc, c0)])
            akeys = {c0: [("aT", f, c0) for f in range(NFF)] for (c0, c1) in groups}
            it = 0
            for fcg in range(4):
                ws = WR.next()
                WR.load(ws, [(0, NFF, 256, I["w_ffn_down"][L][:, 256 * fcg:256 * fcg + 256])])
                Wd_ = WR.view(ws, 0, NFF, 256)
                wkey = WR.key(ws)
                for (c0, c1) in groups:
                    n = c1 - c0
                    for fc in range(2):
                        bk = 4 + (it % 2)
                        it += 1
                        for ffc in range(NFF):
                            P.op("pe", lambda e, ffc=ffc, fc=fc, bk=bk, Wd_=Wd_, c0=c0, c1=c1, n=n: e.matmul(
                                C.ps[bk][:, 0:n], lhsT=Wd_[:, ffc, fc * 128:(fc + 1) * 128], rhs=aT[:, ffc, c0 - hc0:c1 - hc0],
                                start=(ffc == 0), stop=(ffc == NFF - 1)), reads=akeys[c0] + [wkey], writes=[("ps", bk)])
                        P.op("act", lambda e, bk=bk, n=n, f=2 * fcg + fc, c0=c0, c1=c1: e.activation(
                            out=hTh[:, f, c0 - hc0:c1 - hc0], in_=C.ps[bk][:, 0:n], func=AF.Copy),
                            reads=[("ps", bk)] + [("hT", j) for j in tg_tiles(c0, c1)], writes=[("fT", 2 * fcg + fc, c0)])
            P.dma("sp", C.gvec[:, 0, :], I["n_ffn_post"][L].partition_broadcast(128), writes=[("g", 0)])
            ss2 = sb("ss2f", [128, 8], F32, st)
            P.op("dve", lambda e: e.memset(ss2[:], 1.0), writes=["ss2f"])
            junk = sb("junk_f", [128, D], BF16, st)
            tmpx = sb("tmpxf", [128, D], F32, st)
            for j in tiles:
                bank = 6 + (j % 2)
                pv = psb(bank)
                c0g = [c0 for (c0, c1) in groups if c0 <= j * 128 < c1][0]
                lc = j * 128 - hc0
                for f in range(8):
                    P.op("pe", lambda e, f=f, lc=lc, pv=pv: e.transpose(pv[:, f * 128:(f + 1) * 128], hTh[:, f, lc:lc + 128], C.ident_b[:]),
                         reads=[("fT", f, c0g), "ident_b"], writes=[("ps", bank)])
                P.op("act", lambda e, pv=pv: e.activation(out=junk[:], in_=pv, func=AF.Square, accum_out=ss2[:, 0:1]),
                     reads=[("ps", bank)], writes=["junk_f", "ss2f"])
                P.op("dve", lambda e: e.tensor_scalar(out=ss2[:, 2:4], in0=ss2[:, 0:2], scalar1=1.0 / D, scalar2=EPS, op0=ALU.mult, op1=ALU.add),
                     reads=["ss2f"], writes=["ss2f2"])
                P.op("act", lambda e: e.activation(out=ss2[:, 2:4], in_=ss2[:, 2:4], func=AF.Sqrt), reads=["ss2f2"], writes=["ss2f2"])
                P.op("dve", lambda e: e.reciprocal(out=ss2[:, 4:6], in_=ss2[:, 2:4]), reads=["ss2f2"], writes=["ss2g"])
                P.op("act", lambda e, pv=pv: e.activation(out=tmpx[:], in_=pv, func=AF.Copy), reads=[("ps", bank)], writes=["tmpxf"])
                P.op("dve", lambda e: e.scalar_tensor_tensor(out=tmpx[:], in0=tmpx[:], scalar=ss2[:, 4:5], in1=C.gvec[:, 0, :], op0=ALU.mult, op1=ALU.mult),
                     reads=["tmpxf", "ss2g", ("g", 0)], writes=["tmpxf"])
                P.op("pool", lambda e, j=j: e.tensor_tensor(out=C.X[:, j, :], in0=C.X[:, j, :], in1=tmpx[:], op=ALU.add),
                     reads=[("X", j), "tmpxf"], writes=[("X", j)])

        def ple_half(L, H, st, last):
            groups = HALVES[H]
            hc0 = groups[0][0]
            ncol = groups[-1][1] - hc0
            tiles = [j for (c0, c1) in groups for j in tg_tiles(c0, c1)]
            WR = WRing(st, 2, 6144, "P")
            wsl = []
            for nh in range(2):
                ws = WR.next()
                pieces = [(0, 8, 512, I["w_ple_gate"][L][:, 512 * nh:512 * nh + 512])]
                if nh == 0:
                    pieces.append((4096, 2, 1024, I["w_ple"][L]))
                WR.load(ws, pieces)
                wsl.append((WR.view(ws, 0, 8, 512), WR.key(ws)))
            Wp = WR.view(0, 4096, 2, 1024)
            wpk = WR.key(0)
            xb = sb("xb", [128, 2, D], BF16, st)
            xT = sb("xT", [128, 2, 8, 128], BF16, st)
            pt = sb("pt", [128, 2, 256], BF16, st)
            pT = sb("pT", [128, 2, 2, 128], BF16, st)
            sgp = sb("sgp", [128, 2, 512], F32, st)
            tmpp = sb("tmpp", [128, 2, 512], F32, st)
            for j in tiles:
                s2 = j % 2
                P.op("act", lambda e, j=j, s2=s2: e.activation(out=xb[:, s2, :], in_=C.X[:, j, :], func=AF.Copy), reads=[("X", j)], writes=[("xb", s2)])
                pv = psb(6 + s2)
                for c in range(8):
                    P.op("pe", lambda e, c=c, s2=s2, pv=pv: e.transpose(pv[:, c * 128:(c + 1) * 128], xb[:, s2, c * 128:(c + 1) * 128], C.ident_b[:]),
                         reads=[("xb", s2), "ident_b"], writes=[("ps", 6 + s2)])
                P.op("act", lambda e, s2=s2, pv=pv: e.activation(out=xT[:, s2], in_=pv.rearrange("p (c t) -> p c t", c=8), func=AF.Copy),
                     reads=[("ps", 6 + s2)], writes=[("xT", s2)])
                psrc = I["pp"][L, 128 * j:128 * (j + 1), :] if j < 16 else I["psm"][L]
                P.dma("pool", pt[:, s2, :], psrc, writes=[("pt", s2)])
                pv2 = psb(4 + s2)
                for c in range(2):
                    P.op("pe", lambda e, c=c, s2=s2, pv2=pv2: e.transpose(pv2[:, c * 128:(c + 1) * 128], pt[:, s2, c * 128:(c + 1) * 128], C.ident_b[:]),
                         reads=[("pt", s2), "ident_b"], writes=[("ps", 4 + s2)])
                P.op("act", lambda e, s2=s2, pv2=pv2: e.activation(out=pT[:, s2], in_=pv2[:, 0:256].rearrange("p (c t) -> p c t", c=2), func=AF.Copy),
                     reads=[("ps", 4 + s2)], writes=[("pT", s2)])
                for nh in range(2):
                    Wv, wk = wsl[nh]
                    bg_, bp_ = (0, 1) if nh == 0 else (2, 3)
                    for kc in range(8):
                        P.op("pe", lambda e, kc=kc, s2=s2, bg_=bg_, Wv=Wv: e.matmul(C.ps[bg_][:, :], lhsT=xT[:, s2, kc, :], rhs=Wv[:, kc, :], start=(kc == 0), stop=(kc == 7)),
                             reads=[("xT", s2), wk], writes=[("ps", bg_)])
                    for kc in range(2):
                        P.op("pe", lambda e, kc=kc, s2=s2, bp_=bp_, nh=nh: e.matmul(C.ps[bp_][:, :], lhsT=pT[:, s2, kc, :], rhs=Wp[:, kc, 512 * nh:512 * nh + 512],
                                                                                start=(kc == 0), stop=(kc == 1)),
                             reads=[("pT", s2), wpk], writes=[("ps", bp_)])
                    P.op("act", lambda e, nh=nh, bg_=bg_: e.activation(out=sgp[:, nh, :], in_=C.ps[bg_][:, :], func=AF.Sigmoid), reads=[("ps", bg_)], writes=[("sgp", nh)])
                    P.op("dve", lambda e, nh=nh, bp_=bp_: e.tensor_tensor(out=tmpp[:, nh, :], in0=C.ps[bp_][:, :], in1=sgp[:, nh, :], op=ALU.mult),
                         reads=[("ps", bp_), ("sgp", nh)], writes=[("tmpp", nh)])
                    P.op("pool", lambda e, j=j, nh=nh: e.tensor_tensor(out=C.X[:, j, 512 * nh:512 * nh + 512], in0=C.X[:, j, 512 * nh:512 * nh + 512],
                                                                     in1=tmpp[:, nh, :], op=ALU.add),
                         reads=[("X", j), ("tmpp", nh)], writes=[("X", j)])
                if last:
                    if j < 16:
                        P.dma("sp", O["yp"][128 * j:128 * (j + 1), :], C.X[:, j, :], reads=[("X", j)])
                    else:
                        P.dma("sp", O["ys"], C.X[:, j, :], reads=[("X", j)])

        stages = []
        for L in range(depth):
            with ExitStack() as lay:
                C.hT = sb("hT", [128, 8, NTOK], BF16, lay)
                C.odT = sb("odT", [128, 2, NTOK], BF16, lay)
                with ExitStack() as st:
                    norm_to_hT(L, "n_mix_pre", list(range(NT)), st, C.hT, 0)
                barrier()
                dump("hT_%d" % L, C.hT[:], [128, 8, NTOK], [("hT", j) for j in range(NT)])
                if stop_after == ("norm", L):
                    break
                with ExitStack() as st:
                    mixer_D(L, st)
                barrier()
                dump("odT_%d" % L, C.odT[:], [128, 2, NTOK], [("odT", v, c0) for v in range(2) for (c0, c1) in TG])
                if stop_after == ("D", L):
                    break
                C.numA = sb("numA", [128, 2, NTOK], BF16, lay)
                with ExitStack() as att:
                    load_att_consts(att)
                    with ExitStack() as st:
                        C.denA = sb("denA", [128, 2, NTOK], BF16, st)
                        for g in range(3):
                            if g not in DBG_A_GROUPS:
                                continue
                            with ExitStack() as st2:
                                mixer_A(L, g, st2)
                            barrier()
                        dump("numA_%d" % L, C.numA[:], [128, 2, NTOK], [("numA", j) for j in range(NT)])
                        dump("denA_%d" % L, C.denA[:], [128, 2, NTOK], [("denA", j) for j in range(NT)])
                        finalize_A(st)
                    barrier()
                dump("oaT_%d" % L, C.numA[:], [128, 2, NTOK], [("numA", j) for j in range(NT)])
                if stop_after == ("A", L):
                    break
                C.obT = sb("obT", [128, 4, NTOK], BF16, lay)
                with ExitStack() as att:
                    load_att_consts(att)
                    with ExitStack() as st:
                        mixer_B(L, st)
                    barrier()
                dump("obT_%d" % L, C.obT[:], [128, 4, NTOK], [("obT", j) for j in range(NT)])
                dump("oaTb_%d" % L, C.numA[:], [128, 2, NTOK], [("numA", j) for j in range(NT)])
                if stop_after == ("B", L):
                    break
                C.ocT = sb("ocT", [128, 2, NTOK], BF16, lay)
                with ExitStack() as st:
                    mixer_C(L, st)
                barrier()
                dump("ocT_%d" % L, C.ocT[:], [128, 2, NTOK], [("ocT", ch, c0) for ch in range(2) for (c0, c1) in TG])
                dump("oaTc_%d" % L, C.numA[:], [128, 2, NTOK], [("numA", j) for j in range(NT)])
                if stop_after == ("C", L):
                    break
                P.op("dve", lambda e: e.engine_nop(), reads=[("numA", j) for j in range(NT)] + [("obT", j) for j in range(NT)]
                     + [("ocT", ch, c0) for ch in range(2) for (c0, c1) in TG] + [("odT", v, c0) for v in range(2) for (c0, c1) in TG], writes=["mix_out"])
                for H in range(2):
                    with ExitStack() as st:
                        merge_half(L, H, st)
                    barrier()
                dump("x1_%d" % L, C.X[:], [128, NT, D], [("X", j) for j in range(NT)])
            barrier()
            if stop_after == ("merge", L):
                break
            for H in range(2):
                with ExitStack() as st:
                    ffn_half(L, H, st)
                barrier()
            dump("x2_%d" % L, C.X[:], [128, NT, D], [("X", j) for j in range(NT)])
            if stop_after == ("ffn", L):
                break
            for H in range(2):
                with ExitStack() as st:
                    ple_half(L, H, st, last=(L == depth - 1))
                barrier()
        P.emit(es)
    P.min_rem = C.min_rem
    return nc, dump_specs, P


_CACHE = {}


def _get_program(depth=DEPTH, stop_after=None, dumps=()):
    key = (depth, stop_after, tuple(dumps))
    if key not in _CACHE:
        _CACHE[key] = build_program(depth, stop_after, dumps)
    return _CACHE[key]


def _core_inputs(c, a, consts, w_b, w_brb):
    f = np.ascontiguousarray
    sl = slice(16 * c, 16 * c + 16)
    m = dict(
        xp=f(a["x_prompt"][c]), xs=f(a["x_sample"][sl].reshape(128, D)),
        pp=f(a["p_prompt"][:, c]), psm=f(a["p_sample"][:, sl].reshape(DEPTH, 128, 256)),
        ca1k=f(a["cache_a1_k"][:, sl].reshape(DEPTH, 16, 128, 256)), ca1v=f(a["cache_a1_v"][:, sl].reshape(DEPTH, 16, 128, 256)),
        ca2k=f(a["cache_a2_k"][:, sl].reshape(DEPTH, 16, 512, 256)), ca2v=f(a["cache_a2_v"][:, sl].reshape(DEPTH, 16, 512, 256)),
        ca3k=f(a["cache_a3_k"][:, sl].reshape(DEPTH, 16, 2048, 256)), ca3v=f(a["cache_a3_v"][:, sl].reshape(DEPTH, 16, 2048, 256)),
        cbk=f(a["cache_b_k"][:, sl].reshape(DEPTH, 16, 128, 128)), cbv=f(a["cache_b_v"][:, sl].reshape(DEPTH, 16, 128, 128)),
        sconv=f(a["state_c_conv"][:, sl].reshape(DEPTH, 32, 256)),
        sdre=f(a["state_d_re"][:, sl].reshape(DEPTH, 16, 1024)), sdim=f(a["state_d_im"][:, sl].reshape(DEPTH, 16, 1024)),
        n_mix_pre=a["norm_mix_pre"], n_mix_post=a["norm_mix_post"], n_ffn_pre=a["norm_ffn_pre"], n_ffn_post=a["norm_ffn_post"],
        w_in=a["w_in"], w_b=w_b, sinks=a["attn_sinks"], conv_w=a["conv_c_w"],
        lam_re=a["ssm_lam_re"].reshape(DEPTH, 8, 128), lam_im=a["ssm_lam_im"].reshape(DEPTH, 8, 128), log_dt=a["ssm_log_dt"].reshape(DEPTH, 8, 2),
        b_re=a["ssm_b_re"].reshape(DEPTH, 1024, 16), b_im=a["ssm_b_im"].reshape(DEPTH, 1024, 16),
        c_re=a["ssm_c_re"].reshape(DEPTH, 256, 64), c_im=a["ssm_c_im"].reshape(DEPTH, 256, 64),
        ssm_d=a["ssm_d"].reshape(DEPTH, 2, 128), w_d_glu=a["w_d_glu"],
        w_br_a=a["w_br_a"], w_br_b=w_brb, w_br_c=a["w_br_c"], w_br_d=a["w_br_d"],
        w_out=a["w_out"], w_ffn_gate=a["w_ffn_gate"], w_ffn_up=a["w_ffn_up"], w_ffn_down=a["w_ffn_down"],
        w_ple=a["w_ple"], w_ple_gate=a["w_ple_gate"],
    )
    m.update(consts)
    return {k: np.ascontiguousarray(v, dtype=np.float32) for k, v in m.items()}


def _run(inputs, depth=DEPTH, stop_after=None, dumps=(), cores=NCORES):
    a = {k: np.asarray(v) for k, v in inputs.items()}
    nc, dump_specs, _ = _get_program(depth, stop_after, dumps)
    consts = _host_consts()
    w_b, w_brb = _host_weights(a["w_in"], a["w_br_b"])
    in_maps = [_core_inputs(c, a, consts, w_b, w_brb) for c in range(cores)]
    res = run_bass_kernel_spmd(nc, in_maps, core_ids=list(range(cores)))
    return res.results


def kernel(**inputs):
    r = _run(inputs)
    n = NCORES

    def stack_p(name, shape_tail):
        return np.stack([r[c][name] for c in range(n)], axis=1).reshape((DEPTH, n) + shape_tail)

    def cat_s(name, shape_tail):
        return np.concatenate([r[c][name] for c in range(n)], axis=1).reshape((DEPTH, 16 * n) + shape_tail)

    yp = np.stack([r[c]["yp"] for c in range(n)], axis=0)
    ys = np.concatenate([r[c]["ys"] for c in range(n)], axis=0).reshape(16 * n, 8, D)
    outs = [yp, ys]
    for g, w in ((1, 128), (2, 512), (3, 2048)):
        for kv in ("k", "v"):
            outs.append(stack_p("a%d%s_p" % (g, kv), (w, 4, 64)))
            outs.append(cat_s("a%d%s_s" % (g, kv), (w, 4, 64)))
    for kv in ("k", "v"):
        outs.append(stack_p("b%s_p" % kv, (128, 2, 64)))
        outs.append(cat_s("b%s_s" % kv, (128, 2, 64)))
    outs.append(stack_p("conv_p", (2, 256)))
    outs.append(cat_s("conv_s", (2, 256)))
    for nm in ("dre", "dim"):
        outs.append(stack_p(nm + "_p", (16, 64)))
        outs.append(cat_s(nm + "_s", (16, 64)))
    return tuple(np.ascontiguousarray(o, dtype=np.float32) for o in outs)
```

```python
import types
import numpy as np
from contextlib import ExitStack
import concourse.bass as bass
import concourse.mybir as mybir
from concourse.bass_utils import run_bass_kernel_spmd

F32 = mybir.dt.float32
BF16 = mybir.dt.bfloat16
ALU = mybir.AluOpType
AF = mybir.ActivationFunctionType
AX = mybir.AxisListType

NCORES = 8
D = 1024
SEQ = 2048
NT = 17
NTOK = NT * 128
DEPTH = 2
DFF = 2816
NFF = DFF // 128
EPS = 1e-6
PAST = 16384
A_W = (128, 512, 2048)
GATE_OFF = 4096

import os
DBG_A_GROUPS = [int(x) for x in os.environ.get("DBG_A_GROUPS", "0,1,2").split(",")]
DBG_A_NB = int(os.environ.get("DBG_A_NB", "16"))
DBG_A_NT = int(os.environ.get("DBG_A_NT", str(NT)))
DBG_B_NT = int(os.environ.get("DBG_B_NT", str(NT)))
DBG_M_BR = [int(x) for x in os.environ.get("DBG_M_BR", "0,1,2,3").split(",")]

M_C1, M_P1, M_C4, M_E4, M_P4, M_C16, M_E16, M_BD0, M_BD1, M_BD2 = range(10)
NMASK = 10


class _Rec:
    def __init__(self):
        self.call = None

    def __getattr__(self, name):
        def f(*a, **k):
            self.call = (name, a, k)
            return self
        return f

    @staticmethod
    def _rnd(n):
        return 32 if n <= 32 else (64 if n <= 64 else 128)

    def pe_class(self):
        name, a, k = self.call
        out = a[0] if a else k["out"]
        if name == "matmul":
            lhsT = k["lhsT"] if "lhsT" in k else a[1]
        elif name == "transpose":
            lhsT = a[1] if len(a) > 1 else k["in_"]
        else:
            raise ValueError(name)
        return (self._rnd(lhsT.partition_size()), lhsT.base_partition(), self._rnd(lhsT.free_size()), out.base_partition())


class Prog:
    ENG = ("pe", "act", "dve", "pool", "sp")

    def __init__(self, nc):
        self.nc = nc
        self.ops = []
        self.by_eng = {e: [] for e in self.ENG}
        self.writers = {}
        self.readers = {}
        self.self_sync = {"pool", "dve", "act"}
        self.serial = {"pool", "dve", "act"}
        self.eng_w = {e: {} for e in self.ENG}
        self.eng_r = {e: {} for e in self.ENG}
        self.pending_bar = {}
        self.dma_since_bar = []
        self.pe_last = {}
        self.log = None
        self.marks = []

    def _add(self, eng, fn, reads, writes, is_dma):
        oid = len(self.ops)
        cls = None
        if not is_dma:
            r = _Rec()
            fn(r)
            name_, a_, k_ = r.call
            if eng == "pe":
                cls = r.pe_class()

            def fn(e, name_=name_, a_=a_, k_=k_):
                return getattr(e, name_)(*a_, **k_)
            auto_out, auto_in = [], []
            if eng in self.self_sync:
                def _isap(x):
                    return hasattr(x, "tensor") and hasattr(x, "ap")
                items = [("arg%d" % i, v) for i, v in enumerate(a_)] + list(k_.items())
                has_out_kw = "out" in k_
                for nm, v in items:
                    if not _isap(v):
                        continue
                    is_out = (nm in ("out", "accum_out")) or (nm == "arg0" and not has_out_kw)
                    (auto_out if is_out else auto_in).append(v.name)
        sub = ("dma", oid) if is_dma else ((eng, cls) if eng == "pe" else (eng,))
        deps = set()
        for r in reads:
            deps.update(self.writers.get(r, {}).values())
        for w_ in writes:
            deps.update(self.writers.get(w_, {}).values())
            deps.update(self.readers.get(w_, ()))
        if not is_dma and eng in self.self_sync:
            ew, er = self.eng_w[eng], self.eng_r[eng]
            for t in auto_in:
                if t in ew:
                    deps.add(ew[t])
            for t in auto_out:
                if t in ew:
                    deps.add(ew[t])
                deps.update(er.get(t, ()))
            for t in auto_in:
                er.setdefault(t, []).append(oid)
            for t in auto_out:
                ew[t] = oid
                er[t] = []
        pb = self.pending_bar.pop(eng, None)
        if pb:
            deps |= pb
        deps.discard(oid)
        op = dict(id=oid, eng=eng, fn=fn, deps=deps, dma=is_dma, need_inc=False, cls=cls)
        self.ops.append(op)
        self.by_eng[eng].append(op)
        if is_dma:
            self.dma_since_bar.append(oid)
        if cls is not None:
            self.pe_last[cls] = oid
        for r in reads:
            self.readers.setdefault(r, []).append(oid)
        for w_ in writes:
            if self.readers.get(w_):
                self.writers[w_] = {sub: oid}
                self.readers[w_] = []
            else:
                self.writers.setdefault(w_, {})[sub] = oid
                self.readers.setdefault(w_, [])
        return op

    @property
    def last_w(self):
        return self.writers

    @staticmethod
    def _snap(fn):
        if getattr(fn, "__closure__", None) is None:
            return fn
        cells = []
        for c in fn.__closure__:
            try:
                cells.append(types.CellType(c.cell_contents))
            except ValueError:
                cells.append(c)
        return types.FunctionType(fn.__code__, fn.__globals__, fn.__name__, fn.__defaults__, tuple(cells))

    def op(self, eng, fn, reads=(), writes=()):
        return self._add(eng, fn, tuple(reads), tuple(writes), False)

    def mark(self, name):
        self.marks.append((name, {e: len(v) for e, v in self.by_eng.items()}))

    def barrier(self):
        d = set(self.dma_since_bar)
        for e in self.ENG:
            if e == "pe":
                d.update(self.pe_last.values())
                continue
            for op in reversed(self.by_eng[e]):
                if not op["dma"]:
                    d.add(op["id"])
                    break
        for e in self.ENG:
            self.pending_bar[e] = set(d) | self.pending_bar.get(e, set())
        self.dma_since_bar = []

    def dma(self, queue, out, in_, reads=(), writes=(), **kw):
        def fn(e, out=out, in_=in_, kw=kw):
            return e.dma_start(out=out, in_=in_, **kw)
        return self._add(queue, fn, tuple(reads), tuple(writes), True)

    def emit(self, es):
        nc = self.nc
        ops = self.ops
        for e in self.serial:
            prev = None
            for op in self.by_eng[e]:
                if prev is not None and not op["dma"]:
                    op["deps"].add(prev)
                if not op["dma"]:
                    prev = op["id"]
        for op in ops:
            kept = set()
            for d in op["deps"]:
                p = ops[d]
                same = (p["eng"] == op["eng"]) and (not p["dma"]) and (not op["dma"]) and (p["eng"] not in self.self_sync)
                if not same:
                    kept.add(d)
                    p["need_inc"] = True
            op["kept"] = kept
        NDS = 40
        esem = {e: es.enter_context(nc.semaphore("sem_" + e)) for e in self.ENG if e != "pe"}
        classes = sorted({op["cls"] for op in ops if op["cls"] is not None})
        pesem = {c: es.enter_context(nc.semaphore("sem_pe%d" % i)) for i, c in enumerate(classes)}
        self.pe_classes = classes
        dsem = [es.enter_context(nc.semaphore("dsem%d" % i)) for i in range(NDS)]
        dcount = [0] * NDS
        cnt = {}
        nd = 0
        for op in ops:
            if op["dma"]:
                s = nd % NDS
                nd += 1
                op["prev_ev"] = (dsem[s], dcount[s]) if dcount[s] > 0 else None
                dcount[s] += 16
                op["ev"] = (dsem[s], dcount[s])
            elif op["need_inc"]:
                sem = pesem[op["cls"]] if op["eng"] == "pe" else esem[op["eng"]]
                cnt[id(sem)] = cnt.get(id(sem), 0) + 1
                op["ev"] = (sem, cnt[id(sem)])
            else:
                op["ev"] = None
        final_waits = [(dsem[s], dcount[s]) for s in range(NDS) if dcount[s] > 0]
        block = es.enter_context(nc.Block())
        by_eng = self.by_eng
        self.n_instr = {e: len(by_eng[e]) for e in self.ENG}

        def run(e, eng):
            seen = {}

            def wait(ev):
                sem, val = ev
                k = id(sem)
                if seen.get(k, 0) >= val:
                    return
                eng.wait_ge(sem, val)
                seen[k] = val
            for op in by_eng[e]:
                wl = []
                for d in sorted(op["kept"]):
                    wait(ops[d]["ev"])
                    wl.append((ops[d]["ev"][0].name, ops[d]["ev"][1]))
                if op["dma"] and op["prev_ev"] is not None:
                    wait(op["prev_ev"])
                ins = op["fn"](eng)
                if op["ev"] is not None:
                    sem, val = op["ev"]
                    ins.then_inc(sem, 16 if op["dma"] else 1)
                if self.log is not None:
                    self.log.append((op["id"], e, wl, (op["ev"][0].name, op["ev"][1]) if op["ev"] else None, str(ins)[:150]))
            if e == "sp":
                for ev in final_waits:
                    wait(ev)

        @block.tensor
        def _(eng):
            run("pe", eng)

        @block.scalar
        def _(eng):
            run("act", eng)

        @block.vector
        def _(eng):
            run("dve", eng)

        @block.gpsimd
        def _(eng):
            run("pool", eng)

        @block.sync
        def _(eng):
            run("sp", eng)


class Ctx:
    pass


def _host_consts():
    kk = np.arange(128)[:, None]
    qq = np.arange(128)[None, :]
    m = np.zeros((128, NMASK, 128), np.float32)
    m[:, M_C1] = kk <= qq
    m[:, M_P1] = kk >= qq
    m[:, M_C4] = (kk <= qq) & ((qq - kk) % 4 == 0)
    m[:, M_E4] = ((qq - kk) % 4 == 0)
    m[:, M_P4] = (kk >= qq) & ((qq - kk) % 4 == 0)
    m[:, M_C16] = (kk <= qq) & ((qq - kk) % 16 == 0)
    m[:, M_E16] = ((qq - kk) % 16 == 0)
    bk, tk = kk // 8, kk % 8
    bq, tq = qq // 8, qq % 8
    m[:, M_BD0] = (bk == bq) & (tk <= tq)
    m[:, M_BD1] = (bk == bq) & (tk <= tq) & ((tq - tk) % 4 == 0)
    m[:, M_BD2] = (bk == bq) & (tk == tq)
    sm = np.zeros((128, 16), np.float32)
    r = np.arange(128)[:, None]
    sm[:, 0:8] = r >= np.arange(8)[None, :]
    sm[:, 8] = 1.0
    sm[:, 9] = (np.arange(128) >= 1)
    half = 32
    inv = (np.float32(10000.0) ** (-np.arange(half, dtype=np.float32) / np.float32(half))).astype(np.float32)
    pos = np.zeros((128, NT), np.float32)
    for j in range(16):
        pos[:, j] = 128 * j + np.arange(128)
    pos[:, 16] = PAST + (np.arange(128) % 8)
    ang = (pos[:, :, None] * inv[None, None, :]).astype(np.float32)
    cos = np.cos(ang).astype(np.float32)
    sin = np.sin(ang).astype(np.float32)
    cosF = np.concatenate([cos, cos], axis=-1)
    sinS = np.concatenate([-sin, sin], axis=-1)
    iota = np.tile(np.arange(256, dtype=np.float32)[None, :], (128, 1))
    gmask = np.zeros((128, 2), np.float32)
    gmask[0:64, 0] = 1.0
    gmask[64:128, 1] = 1.0
    return dict(c_ident=np.eye(128, dtype=np.float32), c_masks=np.ascontiguousarray(m),
                c_smask=sm, c_cos=np.ascontiguousarray(cosF), c_sin=np.ascontiguousarray(sinS),
                c_iota=iota, c_gmask=gmask)


_B_QPERM = [0, 4, 1, 5, 2, 6, 3, 7]


def _host_weights(w_in, w_br_b):
    qb = w_in[:, :, 2304:2816].reshape(DEPTH, D, 8, 64)[:, :, _B_QPERM, :].reshape(DEPTH, D, 512)
    w_b = np.ascontiguousarray(np.concatenate([qb, w_in[:, :, 2816:3072]], axis=2))
    w_brb = np.ascontiguousarray(w_br_b.reshape(DEPTH, 8, 64, D)[:, _B_QPERM].reshape(DEPTH, 512, D))
    return w_b, w_brb


IN_SPECS = [
    ("xp", [SEQ, D]), ("xs", [128, D]), ("pp", [DEPTH, SEQ, 256]), ("psm", [DEPTH, 128, 256]),
    ("ca1k", [DEPTH, 16, 128, 256]), ("ca1v", [DEPTH, 16, 128, 256]),
    ("ca2k", [DEPTH, 16, 512, 256]), ("ca2v", [DEPTH, 16, 512, 256]),
    ("ca3k", [DEPTH, 16, 2048, 256]), ("ca3v", [DEPTH, 16, 2048, 256]),
    ("cbk", [DEPTH, 16, 128, 128]), ("cbv", [DEPTH, 16, 128, 128]),
    ("sconv", [DEPTH, 32, 256]), ("sdre", [DEPTH, 16, 1024]), ("sdim", [DEPTH, 16, 1024]),
    ("n_mix_pre", [DEPTH, D]), ("n_mix_post", [DEPTH, D]), ("n_ffn_pre", [DEPTH, D]), ("n_ffn_post", [DEPTH, D]),
    ("w_in", [DEPTH, D, 8192]), ("w_b", [DEPTH, D, 768]), ("sinks", [DEPTH, 2, 4]), ("conv_w", [DEPTH, 3, 256]),
    ("lam_re", [DEPTH, 8, 128]), ("lam_im", [DEPTH, 8, 128]), ("log_dt", [DEPTH, 8, 2]),
    ("b_re", [DEPTH, 1024, 16]), ("b_im", [DEPTH, 1024, 16]), ("c_re", [DEPTH, 256, 64]), ("c_im", [DEPTH, 256, 64]),
    ("ssm_d", [DEPTH, 2, 128]), ("w_d_glu", [DEPTH, 256, 512]),
    ("w_br_a", [DEPTH, 256, D]), ("w_br_b", [DEPTH, 512, D]), ("w_br_c", [DEPTH, 256, D]), ("w_br_d", [DEPTH, 256, D]),
    ("w_out", [DEPTH, D, D]), ("w_ffn_gate", [DEPTH, D, DFF]), ("w_ffn_up", [DEPTH, D, DFF]),
    ("w_ffn_down", [DEPTH, DFF, D]), ("w_ple", [DEPTH, 256, D]), ("w_ple_gate", [DEPTH, D, D]),
    ("c_ident", [128, 128]), ("c_masks", [128, NMASK, 128]), ("c_smask", [128, 16]),
    ("c_cos", [128, NT, 64]), ("c_sin", [128, NT, 64]), ("c_iota", [128, 256]), ("c_gmask", [128, 2]),
]
OUT_SPECS = [
    ("yp", [SEQ, D]), ("ys", [128, D]),
    ("a1k_p", [DEPTH, 128, 256]), ("a1k_s", [DEPTH, 16, 128, 256]), ("a1v_p", [DEPTH, 128, 256]), ("a1v_s", [DEPTH, 16, 128, 256]),
    ("a2k_p", [DEPTH, 512, 256]), ("a2k_s", [DEPTH, 16, 512, 256]), ("a2v_p", [DEPTH, 512, 256]), ("a2v_s", [DEPTH, 16, 512, 256]),
    ("a3k_p", [DEPTH, 2048, 256]), ("a3k_s", [DEPTH, 16, 2048, 256]), ("a3v_p", [DEPTH, 2048, 256]), ("a3v_s", [DEPTH, 16, 2048, 256]),
    ("bk_p", [DEPTH, 128, 128]), ("bk_s", [DEPTH, 16, 128, 128]), ("bv_p", [DEPTH, 128, 128]), ("bv_s", [DEPTH, 16, 128, 128]),
    ("conv_p", [DEPTH, 2, 256]), ("conv_s", [DEPTH, 32, 256]),
    ("dre_p", [DEPTH, 1024]), ("dre_s", [DEPTH, 16, 1024]), ("dim_p", [DEPTH, 1024]), ("dim_s", [DEPTH, 16, 1024]),
]


def build_program(depth=DEPTH, stop_after=None, dumps=()):
    nc = bass.Bass("TRN2", target_bir_lowering=False)
    I = {n: nc.dram_tensor(n, s, F32, kind="ExternalInput").ap() for n, s in IN_SPECS}
    O = {n: nc.dram_tensor(n, s, F32, kind="ExternalOutput").ap() for n, s in OUT_SPECS}
    dump_specs = []
    P = Prog(nc)
    if os.environ.get("DBG_LOG"):
        P.log = []
    C = Ctx()
    C.min_rem = (1 << 30, '')
    with ExitStack() as es:
        E = es.enter_context

        uid = [0]

        def sb(name, shape, dt, st=None):
            uid[0] += 1
            t = (st or es).enter_context(nc.sbuf_tensor("%s_%d" % (name, uid[0]), shape, dt))
            rem = nc.sbuf_bytes_remaining
            if rem < C.min_rem[0]:
                C.min_rem = (rem, name)
            if os.environ.get("DBG_ALLOC"):
                nb = int(np.prod(shape[1:])) * (2 if dt == BF16 else 4)
                print("ALLOC %-10s bytes/part=%6d  remaining_after=%d" % (name, nb, rem))
            return t

        C.ps = [E(nc.psum_tensor("ps%d" % i, [128, 512], F32)) for i in range(8)]
        C.X = sb("X", [128, NT, D], F32)
        C.ident_f = sb("ident_f", [128, 128], F32)
        C.ident_b = sb("ident_b", [128, 128], BF16)
        C.ones_b = sb("ones_b", [128, 64], BF16)
        C.iota = sb("iota", [128, 256], F32)
        C.gmask = sb("gmask", [128, 2], F32)
        C.halfpi = sb("halfpi", [128, 1], F32)
        C.gvec = sb("gvec", [128, 1, D], F32)
        C.ss = sb("ss", [128, NT], F32)
        C.rs = sb("rs", [128, NT], F32)
        C.gslot = [0]

        def psb(b):
            return C.ps[b][:].bitcast(BF16)

        def dump(name, ap, shape, reads):
            if name in dumps:
                d = nc.dram_tensor("dbg_" + name, list(shape), ap.dtype, kind="ExternalOutput").ap()
                dump_specs.append(("dbg_" + name, list(shape)))
                P.dma("sp", d, ap, reads=reads)

        barrier = P.barrier

        class WRing:
            def __init__(self, st, nslots, elems, name):
                self.n, self.name, self.cnt = nslots, name, 0
                self.buf = sb("wr_" + name, [128, nslots, elems], BF16, st)

            def next(self):
                s_ = self.cnt % self.n
                self.cnt += 1
                return s_

            def key(self, slot):
                return ("w", self.name, slot)

            def load(self, slot, pieces):
                for (off, kc, ncols, src) in pieces:
                    dst = self.buf[:, slot, off:off + kc * ncols].rearrange("p (k n) -> p k n", k=kc)
                    P.dma("pool", dst, src.rearrange("(k p) n -> p k n", p=128), writes=[self.key(slot)])

            def view(self, slot, off, kc, ncols):
                return self.buf[:, slot, off:off + kc * ncols].rearrange("p (k n) -> p k n", k=kc)

        P.dma("sp", C.ident_f[:], I["c_ident"], writes=["ident_f"])
        P.dma("pool", C.ident_b[:], I["c_ident"], writes=["ident_b"])
        P.dma("sp", C.iota[:], I["c_iota"], writes=["iota"])
        P.dma("sp", C.gmask[:], I["c_gmask"], writes=["gmask"])
        P.op("dve", lambda e: e.memset(C.ones_b[:], 1.0), writes=["ones_b"])
        P.op("dve", lambda e: e.memset(C.halfpi[:], 1.5707963267948966), writes=["halfpi"])
        for j in range(16):
            P.dma("sp", C.X[:, j, :], I["xp"][128 * j:128 * (j + 1), :], writes=[("X", j)])
        P.dma("sp", C.X[:, 16, :], I["xs"], writes=[("X", 16)])

        for L_ in range(depth):
            for g_, w_ in enumerate(A_W):
                for b in range(16):
                    for kv_ in ("k", "v"):
                        P.dma("sp", O["a%d%s_s" % (g_ + 1, kv_)][L_, b, 0:w_ - 8, :], I["ca%d%s" % (g_ + 1, kv_)][L_, b, 8:w_, :])
            for b in range(16):
                for kv_ in ("k", "v"):
                    P.dma("sp", O["b%s_s" % kv_][L_, b, 0:120, :], I["cb%s" % kv_][L_, b, 8:128, :])

        def load_att_consts(st):
            C.masks = sb("masks", [128, NMASK, 128], BF16, st)
            C.smask = sb("smask", [128, 16], BF16, st)
            C.cosF = sb("cosF", [128, NT, 64], F32, st)
            C.sinS = sb("sinS", [128, NT, 64], F32, st)
            P.dma("pool", C.masks[:], I["c_masks"], writes=["masks"])
            P.dma("pool", C.smask[:], I["c_smask"], writes=["smask"])
            P.dma("sp", C.cosF[:], I["c_cos"], writes=["rope"])
            P.dma("sp", C.sinS[:], I["c_sin"], writes=["rope"])

        def norm_to_hT(L, gname, tiles, st, dst, col0):
            gs = 0
            P.dma("sp", C.gvec[:, gs, :], I[gname][L].partition_broadcast(128), writes=[("g", gs)])
            junk = sb("junk_n", [128, D], BF16, st)
            hN = sb("hN", [128, 2, D], BF16, st)
            for j in tiles:
                P.op("act", lambda e, j=j: e.activation(out=junk[:], in_=C.X[:, j, :], func=AF.Square,
                                                         accum_out=C.ss[:, j:j + 1]),
                     reads=[("X", j)], writes=["junk_n", ("ss", j)])
            t0, t1 = tiles[0], tiles[-1] + 1
            P.op("dve", lambda e: e.tensor_scalar(out=C.rs[:, t0:t1], in0=C.ss[:, t0:t1], scalar1=1.0 / D, scalar2=EPS,
                                                  op0=ALU.mult, op1=ALU.add),
                 reads=[("ss", j) for j in tiles], writes=["rs"])
            P.op("act", lambda e: e.activation(out=C.rs[:, t0:t1], in_=C.rs[:, t0:t1], func=AF.Sqrt), reads=["rs"], writes=["rs"])
            P.op("dve", lambda e: e.reciprocal(out=C.rs[:, t0:t1], in_=C.rs[:, t0:t1]), reads=["rs"], writes=["rs"])
            for j in tiles:
                hs = j % 2
                P.op("dve", lambda e, j=j, hs=hs: e.scalar_tensor_tensor(
                    out=hN[:, hs, :], in0=C.X[:, j, :], scalar=C.rs[:, j:j + 1], in1=C.gvec[:, gs, :],
                    op0=ALU.mult, op1=ALU.mult), reads=[("X", j), "rs", ("g", gs)], writes=[("hN", hs)])
                bank = 6 + (j % 2)
                pv = psb(bank)
                for c in range(8):
                    P.op("pe", lambda e, c=c, hs=hs, pv=pv: e.transpose(pv[:, c * 128:(c + 1) * 128], hN[:, hs, c * 128:(c + 1) * 128], C.ident_b[:]),
                         reads=[("hN", hs), "ident_b"], writes=[("ps", bank)])
                lc = j * 128 - col0
                P.op("act", lambda e, lc=lc, pv=pv: e.activation(
                    out=dst[:, :, lc:lc + 128], in_=pv.rearrange("p (c t) -> p c t", c=8), func=AF.Copy),
                    reads=[("ps", bank)], writes=[("hT", j)])

        def rope_ops(ps_ap, nheads, j, out_ap, T1, T2, reads, wkey):
            pin = ps_ap.rearrange("p (h d) -> p h d", h=nheads)
            t1 = T1[:, 0:nheads * 64].rearrange("p (h d) -> p h d", h=nheads)
            t2 = T2[:, 0:nheads * 64].rearrange("p (h d) -> p h d", h=nheads)
            cosb = C.cosF[:, j, :].unsqueeze(1).broadcast_to([128, nheads, 64])
            sin_lo = C.sinS[:, j, 0:32].unsqueeze(1).broadcast_to([128, nheads, 32])
            sin_hi = C.sinS[:, j, 32:64].unsqueeze(1).broadcast_to([128, nheads, 32])
            P.op("dve", lambda e: e.tensor_tensor(out=t1, in0=pin, in1=cosb, op=ALU.mult), reads=reads + ["rope"], writes=["ropeT1"])
            P.op("dve", lambda e: e.tensor_tensor(out=t2[:, :, 0:32], in0=pin[:, :, 32:64], in1=sin_lo, op=ALU.mult),
                 reads=reads + ["rope"], writes=["ropeT2"])
            P.op("dve", lambda e: e.tensor_tensor(out=t2[:, :, 32:64], in0=pin[:, :, 0:32], in1=sin_hi, op=ALU.mult),
                 reads=reads + ["rope"], writes=["ropeT2"])
            P.op("dve", lambda e: e.tensor_tensor(out=out_ap, in0=T1[:, 0:nheads * 64], in1=T2[:, 0:nheads * 64], op=ALU.add),
                 reads=["ropeT1", "ropeT2"], writes=[wkey])

        def mixer_A(L, g, st):
            w, dil = A_W[g], (1, 4, 16)[g]
            ck_in, cv_in = I["ca%dk" % (g + 1)], I["ca%dv" % (g + 1)]
            ok_p, ov_p = O["a%dk_p" % (g + 1)], O["a%dv_p" % (g + 1)]
            ok_s, ov_s = O["a%dk_s" % (g + 1)], O["a%dv_s" % (g + 1)]
            first_out_tile = 16 - w // 128
            nct = (1, 4, 8)[g]
            nsl = min(nct, 4)
            nsub = 2 if g == 2 else 1
            WR = WRing(st, 1, 6144, "A")
            ws = WR.next()
            WR.load(ws, [(0, 8, 256, I["w_in"][L][:, 256 * g:256 * g + 256]),
                         (2048, 8, 256, I["w_in"][L][:, 768 + 256 * g:768 + 256 * g + 256]),
                         (4096, 8, 256, I["w_in"][L][:, 1536 + 256 * g:1536 + 256 * g + 256])])
            Wq, Wk, Wv = WR.view(ws, 0, 8, 256), WR.view(ws, 2048, 8, 256), WR.view(ws, 4096, 8, 256)
            wkey = WR.key(ws)
            KT = sb("KT", [128, 2, NTOK], BF16, st)
            V = sb("Vg", [128, NT, 256], BF16, st)
            qkr = sb("qkr", [128, 2, 512], F32, st)
            V32 = sb("V32", [128, 2, 256], F32, st)
            T1 = sb("ropeT1", [128, 512], F32, st)
            T2 = sb("ropeT2", [128, 512], F32, st)
            qT = sb("qT", [128, 2, 2, 128], BF16, st)
            PT = sb("PT", [128, 3, 512], BF16, st)
            Kc = sb("Kc", [128, 2, nsl, 256], BF16, st)
            Vc = sb("Vc", [128, 2, nsl, 256], BF16, st)
            KcT = sb("KcT", [128, 2, nsl, 2, 128], BF16, st)
            PTs = sb("PTs", [128, 2, 32], BF16, st)
            ptc = [0]
            for j in list(range(min(DBG_A_NT, 16))) + [16]:
                s2 = j % 2
                bqk, bv = 0, 2
                for kc in range(8):
                    P.op("pe", lambda e, kc=kc, j=j, bqk=bqk: e.matmul(C.ps[bqk][:, 0:256], lhsT=C.hT[:, kc, j * 128:(j + 1) * 128], rhs=Wq[:, kc, :],
                                                                       start=(kc == 0), stop=(kc == 7)),
                         reads=[("hT", j), wkey], writes=[("ps", bqk)])
                for kc in range(8):
                    P.op("pe", lambda e, kc=kc, j=j, bqk=bqk: e.matmul(C.ps[bqk][:, 256:512], lhsT=C.hT[:, kc, j * 128:(j + 1) * 128], rhs=Wk[:, kc, :],
                                                                       start=(kc == 0), stop=(kc == 7), skip_group_check=True),
                         reads=[("hT", j), wkey], writes=[("ps", bqk)])
                for kc in range(8):
                    P.op("pe", lambda e, kc=kc, j=j: e.matmul(C.ps[bv][:, 0:256], lhsT=C.hT[:, kc, j * 128:(j + 1) * 128], rhs=Wv[:, kc, :],
                                                              start=(kc == 0), stop=(kc == 7)),
                         reads=[("hT", j), wkey], writes=[("ps", bv)])
                rope_ops(C.ps[bqk][:, 0:512], 8, j, qkr[:, s2, :], T1, T2, [("ps", bqk)], ("qkr", s2))
                need_out = (j >= first_out_tile)
                P.op("act", lambda e, j=j: e.activation(out=V[:, j, :], in_=C.ps[bv][:, 0:256], func=AF.Copy),
                     reads=[("ps", bv)], writes=[("V", j)])
                if need_out:
                    P.op("act", lambda e, s2=s2: e.activation(out=V32[:, s2, :], in_=C.ps[bv][:, 0:256], func=AF.Copy),
                         reads=[("ps", bv)], writes=[("V32", s2)])
                    if j < 16:
                        r0 = (j - first_out_tile) * 128
                        P.dma("sp", ok_p[L, r0:r0 + 128, :], qkr[:, s2, 256:512], reads=[("qkr", s2)])
                        P.dma("sp", ov_p[L, r0:r0 + 128, :], V32[:, s2, :], reads=[("V32", s2)])
                    else:
                        for b in range(16):
                            P.dma("sp", ok_s[L, b, w - 8:w, :], qkr[8 * b:8 * b + 8, s2, 256:512], reads=[("qkr", s2)])
                            P.dma("sp", ov_s[L, b, w - 8:w, :], V32[8 * b:8 * b + 8, s2, :], reads=[("V32", s2)])
                for c in range(4):
                    P.op("pe", lambda e, c=c, s2=s2: e.transpose(C.ps[3][:, c * 128:(c + 1) * 128], qkr[:, s2, c * 128:(c + 1) * 128], C.ident_f[:]),
                         reads=[("qkr", s2), "ident_f"], writes=[("ps", 3)])
                P.op("act", lambda e, s2=s2: e.activation(out=qT[:, s2], in_=C.ps[3][:, 0:256].rearrange("p (c t) -> p c t", c=2), func=AF.Copy),
                     reads=[("ps", 3)], writes=[("qT", s2)])
                P.op("act", lambda e, j=j: e.activation(out=KT[:, :, j * 128:(j + 1) * 128], in_=C.ps[3][:, 256:512].rearrange("p (c t) -> p c t", c=2), func=AF.Copy),
                     reads=[("ps", 3)], writes=[("KT", j)])
                if j < 16:
                    if g == 0:
                        kts = ([(j - 1, M_P1)] if j >= 1 else []) + [(j, M_C1)]
                    elif g == 1:
                        kts = ([(j - 4, M_P4)] if j >= 4 else []) + [(kt, M_E4) for kt in range(max(0, j - 3), j)] + [(j, M_C4)]
                    else:
                        kts = [(kt, M_E16) for kt in range(0, j)] + [(j, M_C16)]
                else:
                    kts = [(16, (M_BD0, M_BD1, M_BD2)[g])]
                acc = 6
                ACC = C.ps[acc]
                SPAIR = ((4, 5), (1, 7))

                def emit_S(idx, kt, mid, ps_):
                    pair = SPAIR[idx % 2]
                    for h in range(4):
                        par, c = h % 2, h // 2
                        po = 64 * par
                        P.op("pe", lambda e: e.matmul(
                            C.ps[pair[par]][:, c * 128:(c + 1) * 128], lhsT=KT[po:po + 64, c, kt * 128:(kt + 1) * 128],
                            rhs=qT[po:po + 64, s2, c, :], start=True, stop=True, skip_group_check=True),
                            reads=[("KT", kt), ("qT", s2)], writes=[("ps", pair[par])])
                    for par in range(2):
                        P.op("act", lambda e: e.activation(out=PT[:, ps_, par * 256:par * 256 + 256], in_=C.ps[pair[par]][:, 0:256], func=AF.Exp, scale=0.125),
                             reads=[("ps", pair[par])], writes=[("PT", ps_)])
                    P.op("pool", lambda e: e.tensor_tensor(
                        out=PT[:, ps_, :].rearrange("p (h q) -> p h q", h=4), in0=PT[:, ps_, :].rearrange("p (h q) -> p h q", h=4),
                        in1=C.masks[:, mid, :].unsqueeze(1).broadcast_to([128, 4, 128]), op=ALU.mult),
                        reads=[("PT", ps_), "masks"], writes=[("PT", ps_)])

                def emit_PV(idx, kt, ps_):
                    for h in range(4):
                        par, c = h % 2, h // 2
                        po = 64 * par
                        P.op("pe", lambda e: e.matmul(
                            ACC[po:po + 64, c * 128:(c + 1) * 128], lhsT=V[:, kt, 64 * h:64 * h + 64], rhs=PT[:, ps_, par * 256 + c * 128:par * 256 + c * 128 + 128],
                            start=(idx == 0 and h < 2), stop=False, skip_group_check=True),
                            reads=[("V", kt), ("PT", ps_)], writes=[("ps", acc)])
                    for par in range(2):
                        P.op("pe", lambda e: e.matmul(
                            ACC[64 * par:64 * par + 64, 256:512], lhsT=C.ones_b[:, :], rhs=PT[:, ps_, par * 256:par * 256 + 256],
                            start=False, stop=False, skip_group_check=True),
                            reads=[("PT", ps_), "ones_b"], writes=[("ps", acc)])

                slots = []
                for idx, (kt, mid) in enumerate(kts):
                    ps_ = ptc[0] % 3
                    ptc[0] += 1
                    slots.append(ps_)
                    emit_S(idx, kt, mid, ps_)
                    if idx >= 1:
                        emit_PV(idx - 1, kts[idx - 1][0], slots[idx - 1])
                emit_PV(len(kts) - 1, kts[-1][0], slots[-1])
                if j == 16:
                    step = 0
                    for b in range(DBG_A_NB):
                        for sub in range(nsub):
                            cs = step % 2
                            step += 1
                            if g == 0:
                                ksrc, vsrc = ck_in[L, b].unsqueeze(1), cv_in[L, b].unsqueeze(1)
                                tl = list(range(8))
                            elif g == 1:
                                ksrc = ck_in[L, b].rearrange("(m s) f -> m s f", s=4)
                                vsrc = cv_in[L, b].rearrange("(m s) f -> m s f", s=4)
                                tl = list(range(8))
                            else:
                                ksrc = ck_in[L, b].rearrange("(m s) f -> m s f", s=16)[:, 4 * sub:4 * sub + 4, :]
                                vsrc = cv_in[L, b].rearrange("(m s) f -> m s f", s=16)[:, 4 * sub:4 * sub + 4, :]
                                tl = list(range(4 * sub, 4 * sub + 4))
                            t0, nt_ = tl[0], len(tl)
                            P.dma("pool", Kc[:, cs], ksrc, writes=[("Kc", cs)])
                            P.dma("pool", Vc[:, cs], vsrc, writes=[("Vc", cs)])
                            pv = psb(3)
                            nblk = nsl * 2
                            for ct in range(nsl):
                                for hp in range(2):
                                    bi = ct * 2 + hp
                                    P.op("pe", lambda e, bi=bi, ct=ct, hp=hp, cs=cs, pv=pv: e.transpose(
                                        pv[:, bi * 128:(bi + 1) * 128], Kc[:, cs, ct, hp * 128:(hp + 1) * 128], C.ident_b[:]),
                                        reads=[("Kc", cs), "ident_b"], writes=[("ps", 3)])
                            P.op("act", lambda e, pv=pv, cs=cs, nblk=nblk: e.activation(
                                out=KcT[:, cs].rearrange("p a b m -> p (a b) m"),
                                in_=pv[:, 0:nblk * 128].rearrange("p (a m) -> p a m", a=nblk), func=AF.Copy),
                                reads=[("ps", 3)], writes=[("KcT", cs)])
                            for h in range(4):
                                par, c = h % 2, h // 2
                                po = 64 * par
                                SBk = C.ps[4 + par]
                                if g == 0:
                                    P.op("pe", lambda e, po=po, c=c, cs=cs, b=b, SBk=SBk: e.matmul(
                                        SBk[:, c * 8:c * 8 + 8], lhsT=KcT[po:po + 64, cs, 0, c, :], rhs=qT[po:po + 64, s2, c, 8 * b:8 * b + 8],
                                        start=True, stop=True, skip_group_check=True),
                                        reads=[("KcT", cs), ("qT", s2)], writes=[("ps", 4 + par)])
                                elif g == 1:
                                    for s in range(4):
                                        P.op("pe", lambda e, po=po, c=c, cs=cs, b=b, s=s, SBk=SBk: e.matmul(
                                            SBk[:, c * 8 + s:c * 8 + s + 5:4], lhsT=KcT[po:po + 64, cs, s, c, :],
                                            rhs=qT[po:po + 64, s2, c, 8 * b + s:8 * b + s + 5:4],
                                            start=True, stop=True, skip_group_check=True),
                                            reads=[("KcT", cs), ("qT", s2)], writes=[("ps", 4 + par)])
                                else:
                                    for t in tl:
                                        P.op("pe", lambda e, po=po, c=c, cs=cs, b=b, t=t, t0=t0, SBk=SBk: e.matmul(
                                            SBk[:, c * 8 + t:c * 8 + t + 1], lhsT=KcT[po:po + 64, cs, t - t0, c, :],
                                            rhs=qT[po:po + 64, s2, c, 8 * b + t:8 * b + t + 1],
                                            start=True, stop=True, skip_group_check=True),
                                            reads=[("KcT", cs), ("qT", s2)], writes=[("ps", 4 + par)])
                            PT4 = PTs[:, cs, :].rearrange("p (r c t) -> p r c t", r=2, c=2)
                            for par in range(2):
                                P.op("act", lambda e, par=par, PT4=PT4, t0=t0, nt_=nt_: e.activation(
                                    out=PT4[:, par, :, t0:t0 + nt_], in_=C.ps[4 + par][:, 0:16].rearrange("p (c t) -> p c t", c=2)[:, :, t0:t0 + nt_],
                                    func=AF.Exp, scale=0.125), reads=[("ps", 4 + par)], writes=[("PTs", cs)])
                            PT3 = PTs[:, cs, :].rearrange("p (h t) -> p h t", h=4)
                            if g == 0:
                                P.op("pool", lambda e, PT3=PT3: e.tensor_tensor(
                                    out=PT3, in0=PT3, in1=C.smask[:, 0:8].unsqueeze(1).broadcast_to([128, 4, 8]), op=ALU.mult),
                                    reads=[("PTs", cs), "smask"], writes=[("PTs", cs)])
                            elif g == 1:
                                P.op("pool", lambda e, PT3=PT3: e.tensor_tensor(
                                    out=PT3[:, :, 4:8], in0=PT3[:, :, 4:8],
                                    in1=C.smask[:, 9:10].unsqueeze(1).broadcast_to([128, 4, 4]), op=ALU.mult),
                                    reads=[("PTs", cs), "smask"], writes=[("PTs", cs)])
                            for h in range(4):
                                par, c = h % 2, h // 2
                                po = 64 * par
                                col0 = c * 128 + 8 * b
                                pc0 = par * 16 + c * 8
                                if g == 0:
                                    P.op("pe", lambda e, h=h, po=po, cs=cs, col0=col0, pc0=pc0: e.matmul(
                                        ACC[po:po + 64, col0:col0 + 8], lhsT=Vc[:, cs, 0, 64 * h:64 * h + 64], rhs=PTs[:, cs, pc0:pc0 + 8],
                                        start=False, stop=False, skip_group_check=True),
                                        reads=[("Vc", cs), ("PTs", cs)], writes=[("ps", acc)])
                                elif g == 1:
                                    for s in range(4):
                                        P.op("pe", lambda e, h=h, po=po, cs=cs, col0=col0, pc0=pc0, s=s: e.matmul(
                                            ACC[po:po + 64, col0 + s:col0 + s + 5:4], lhsT=Vc[:, cs, s, 64 * h:64 * h + 64],
                                            rhs=PTs[:, cs, pc0 + s:pc0 + s + 5:4], start=False, stop=False, skip_group_check=True),
                                            reads=[("Vc", cs), ("PTs", cs)], writes=[("ps", acc)])
                                else:
                                    for t in tl:
                                        P.op("pe", lambda e, h=h, po=po, cs=cs, col0=col0, pc0=pc0, t=t, t0=t0: e.matmul(
                                            ACC[po:po + 64, col0 + t:col0 + t + 1], lhsT=Vc[:, cs, t - t0, 64 * h:64 * h + 64],
                                            rhs=PTs[:, cs, pc0 + t:pc0 + t + 1], start=False, stop=False, skip_group_check=True),
                                            reads=[("Vc", cs), ("PTs", cs)], writes=[("ps", acc)])
                            for par in range(2):
                                P.op("pe", lambda e, par=par, cs=cs, b=b, t0=t0, nt_=nt_, PT4=PT4: e.matmul(
                                    ACC[64 * par:64 * par + 64, 256:512].rearrange("p (c q) -> p c q", c=2)[:, :, 8 * b + t0:8 * b + t0 + nt_],
                                    lhsT=C.ones_b[:, :], rhs=PT4[:, par, :, t0:t0 + nt_],
                                    start=False, stop=False, skip_group_check=True),
                                    reads=[("PTs", cs), "ones_b"], writes=[("ps", acc)])
                nv = C.numA[:, :, j * 128:(j + 1) * 128]
                dv = C.denA[:, :, j * 128:(j + 1) * 128]
                pn = ACC[:, 0:256].rearrange("p (c q) -> p c q", c=2)
                pd = ACC[:, 256:512].rearrange("p (c q) -> p c q", c=2)
                if g == 0:
                    P.op("dve", lambda e, nv=nv, pn=pn: e.tensor_copy(out=nv, in_=pn), reads=[("ps", acc)], writes=[("numA", j)])
                    P.op("dve", lambda e, dv=dv, pd=pd: e.tensor_copy(out=dv, in_=pd), reads=[("ps", acc)], writes=[("denA", j)])
                else:
                    P.op("dve", lambda e, nv=nv, pn=pn: e.tensor_tensor(out=nv, in0=pn, in1=nv, op=ALU.add),
                         reads=[("ps", acc), ("numA", j)], writes=[("numA", j)])
                    P.op("dve", lambda e, dv=dv, pd=pd: e.tensor_tensor(out=dv, in0=pd, in1=dv, op=ALU.add),
                         reads=[("ps", acc), ("denA", j)], writes=[("denA", j)])

        def mixer_B(L, st):
            ck_in, cv_in = I["cbk"], I["cbv"]
            ok_p, ov_p, ok_s, ov_s = O["bk_p"], O["bv_p"], O["bk_s"], O["bv_s"]
            WR = WRing(st, 1, 6144, "B")
            ws = WR.next()
            WR.load(ws, [(0, 8, 768, I["w_b"][L])])
            W = WR.view(ws, 0, 8, 768)
            wkey = WR.key(ws)
            KT = sb("KTb", [128, NTOK], BF16, st)
            V = sb("Vb", [128, NT, 128], BF16, st)
            qkr = sb("qkrb", [128, 2, 640], F32, st)
            V32 = sb("V32b", [128, 2, 128], F32, st)
            T1 = sb("ropeT1", [128, 512], F32, st)
            T2 = sb("ropeT2", [128, 512], F32, st)
            qT = sb("qTb", [128, 2, 4, 128], BF16, st)
            PT = sb("PTb", [128, 4, 512], BF16, st)
            Kc = sb("Kcb", [128, 2, 128], BF16, st)
            Vc = sb("Vcb", [128, 2, 128], BF16, st)
            KcT = sb("KcTb", [128, 2, 128], BF16, st)
            PTs = sb("PTsb", [128, 2, 64], BF16, st)
            esink = sb("esink", [128, 4], F32, st)
            dtmp = sb("dtmp", [128, 512], F32, st)
            for kv in range(2):
                P.dma("sp", esink[64 * kv:64 * kv + 64, :], I["sinks"][L, kv].partition_broadcast(64), writes=["esink"])
            P.op("act", lambda e: e.activation(out=esink[:], in_=esink[:], func=AF.Exp), reads=["esink"], writes=["esink"])
            ptc = [0]
            NUM, DEN = C.ps[6], C.ps[7]
            for j in range(DBG_B_NT):
                s2 = j % 2
                bq = (0, 1)[s2]
                for kc in range(8):
                    P.op("pe", lambda e, kc=kc, j=j, bq=bq: e.matmul(C.ps[bq][:, 0:512], lhsT=C.hT[:, kc, j * 128:(j + 1) * 128], rhs=W[:, kc, 0:512],
                                                                     start=(kc == 0), stop=(kc == 7)),
                         reads=[("hT", j), wkey], writes=[("ps", bq)])
                for kc in range(8):
                    P.op("pe", lambda e, kc=kc, j=j: e.matmul(C.ps[2][:, 0:256], lhsT=C.hT[:, kc, j * 128:(j + 1) * 128], rhs=W[:, kc, 512:768],
                                                              start=(kc == 0), stop=(kc == 7)),
                         reads=[("hT", j), wkey], writes=[("ps", 2)])
                rope_ops(C.ps[bq][:, 0:512], 8, j, qkr[:, s2, 0:512], T1, T2, [("ps", bq)], ("qkrb", s2))
                rope_ops(C.ps[2][:, 0:128], 2, j, qkr[:, s2, 512:640], T1, T2, [("ps", 2)], ("qkrb", s2))
                P.op("act", lambda e, j=j: e.activation(out=V[:, j, :], in_=C.ps[2][:, 128:256], func=AF.Copy),
                     reads=[("ps", 2)], writes=[("Vb", j)])
                if j >= 15:
                    P.op("act", lambda e, s2=s2: e.activation(out=V32[:, s2, :], in_=C.ps[2][:, 128:256], func=AF.Copy),
                         reads=[("ps", 2)], writes=[("V32b", s2)])
                    if j == 15:
                        P.dma("sp", ok_p[L], qkr[:, s2, 512:640], reads=[("qkrb", s2)])
                        P.dma("sp", ov_p[L], V32[:, s2, :], reads=[("V32b", s2)])
                    else:
                        for b in range(16):
                            P.dma("sp", ok_s[L, b, 120:128, :], qkr[8 * b:8 * b + 8, s2, 512:640], reads=[("qkrb", s2)])
                            P.dma("sp", ov_s[L, b, 120:128, :], V32[8 * b:8 * b + 8, s2, :], reads=[("V32b", s2)])
                for c in range(4):
                    P.op("pe", lambda e, c=c, s2=s2: e.transpose(C.ps[3][:, c * 128:(c + 1) * 128], qkr[:, s2, c * 128:(c + 1) * 128], C.ident_f[:]),
                         reads=[("qkrb", s2), "ident_f"], writes=[("ps", 3)])
                P.op("act", lambda e, s2=s2: e.activation(out=qT[:, s2], in_=C.ps[3][:, 0:512].rearrange("p (c t) -> p c t", c=4), func=AF.Copy),
                     reads=[("ps", 3)], writes=[("qTb", s2)])
                P.op("pe", lambda e, s2=s2: e.transpose(C.ps[3][:, 0:128], qkr[:, s2, 512:640], C.ident_f[:]),
                     reads=[("qkrb", s2), "ident_f"], writes=[("ps", 3)])
                P.op("act", lambda e, j=j: e.activation(out=KT[:, j * 128:(j + 1) * 128], in_=C.ps[3][:, 0:128], func=AF.Copy),
                     reads=[("ps", 3)], writes=[("KTb", j)])
                if j < 16:
                    kts = ([(j - 1, M_P1)] if j >= 1 else []) + [(j, M_C1)]
                else:
                    kts = [(16, M_BD0)]
                for ki, (kt, mid) in enumerate(kts):
                    for kv in range(2):
                        sbk = 4 + kv
                        ps_ = ptc[0] % 4
                        ptc[0] += 1
                        P.op("pe", lambda e, kv=kv, kt=kt, sbk=sbk, s2=s2: e.matmul(
                            C.ps[sbk][:, :], lhsT=KT[64 * kv:64 * kv + 64, kt * 128:(kt + 1) * 128],
                            rhs=qT[64 * kv:64 * kv + 64, s2, :, :], start=True, stop=True),
                            reads=[("KTb", kt), ("qTb", s2)], writes=[("ps", sbk)])
                        P.op("act", lambda e, sbk=sbk, ps_=ps_: e.activation(out=PT[:, ps_, :], in_=C.ps[sbk][:, :], func=AF.Exp, scale=0.125),
                             reads=[("ps", sbk)], writes=[("PTb", ps_)])
                        P.op("pool", lambda e, ps_=ps_, mid=mid: e.tensor_tensor(
                            out=PT[:, ps_, :].rearrange("p (h q) -> p h q", h=4), in0=PT[:, ps_, :].rearrange("p (h q) -> p h q", h=4),
                            in1=C.masks[:, mid, :].unsqueeze(1).broadcast_to([128, 4, 128]), op=ALU.mult),
                            reads=[("PTb", ps_), "masks"], writes=[("PTb", ps_)])
                        P.op("pe", lambda e, kv=kv, kt=kt, ps_=ps_, ki=ki: e.matmul(
                            NUM[64 * kv:64 * kv + 64, :], lhsT=V[:, kt, 64 * kv:64 * kv + 64], rhs=PT[:, ps_, :],
                            start=(ki == 0), stop=False, skip_group_check=True),
                            reads=[("Vb", kt), ("PTb", ps_)], writes=[("ps", 6)])
                        P.op("pe", lambda e, kv=kv, ps_=ps_, ki=ki: e.matmul(
                            DEN[64 * kv:64 * kv + 64, :], lhsT=C.ones_b[:, :], rhs=PT[:, ps_, :],
                            start=(ki == 0), stop=False, skip_group_check=True),
                            reads=[("PTb", ps_), "ones_b"], writes=[("ps", 7)])
                if j == 16:
                    for b in range(16):
                        cs = b % 2
                        P.dma("pool", Kc[:, cs, :], ck_in[L, b], writes=[("Kcb", cs)])
                        P.dma("pool", Vc[:, cs, :], cv_in[L, b], writes=[("Vcb", cs)])
                        pv = psb(3)
                        P.op("pe", lambda e, cs=cs, pv=pv: e.transpose(pv[:, 0:128], Kc[:, cs, :], C.ident_b[:]),
                             reads=[("Kcb", cs), "ident_b"], writes=[("ps", 3)])
                        P.op("act", lambda e, cs=cs, pv=pv: e.activation(out=KcT[:, cs, :], in_=pv[:, 0:128], func=AF.Copy),
                             reads=[("ps", 3)], writes=[("KcTb", cs)])
                        for kv in range(2):
                            P.op("pe", lambda e, kv=kv, cs=cs, b=b: e.matmul(
                                C.ps[4 + kv][:, 0:32].rearrange("p (c t) -> p c t", c=4), lhsT=KcT[64 * kv:64 * kv + 64, cs, :],
                                rhs=qT[64 * kv:64 * kv + 64, s2, :, 8 * b:8 * b + 8], start=True, stop=True, skip_group_check=True),
                                reads=[("KcTb", cs), ("qTb", s2)], writes=[("ps", 4 + kv)])
                            P.op("act", lambda e, cs=cs, kv=kv: e.activation(out=PTs[:, cs, kv * 32:kv * 32 + 32], in_=C.ps[4 + kv][:, 0:32], func=AF.Exp, scale=0.125),
                                 reads=[("ps", 4 + kv)], writes=[("PTsb", cs)])
                        P.op("pool", lambda e, cs=cs: e.tensor_tensor(
                            out=PTs[:, cs, :].rearrange("p (h t) -> p h t", h=8), in0=PTs[:, cs, :].rearrange("p (h t) -> p h t", h=8),
                            in1=C.smask[:, 0:8].unsqueeze(1).broadcast_to([128, 8, 8]), op=ALU.mult),
                            reads=[("PTsb", cs), "smask"], writes=[("PTsb", cs)])
                        for kv in range(2):
                            P.op("pe", lambda e, kv=kv, cs=cs, b=b: e.matmul(
                                NUM[64 * kv:64 * kv + 64, :].rearrange("p (c q) -> p c q", c=4)[:, :, 8 * b:8 * b + 8],
                                lhsT=Vc[:, cs, 64 * kv:64 * kv + 64], rhs=PTs[:, cs, kv * 32:kv * 32 + 32].rearrange("p (c t) -> p c t", c=4),
                                start=False, stop=False, skip_group_check=True),
                                reads=[("Vcb", cs), ("PTsb", cs)], writes=[("ps", 6)])
                            P.op("pe", lambda e, kv=kv, cs=cs, b=b: e.matmul(
                                DEN[64 * kv:64 * kv + 64, :].rearrange("p (c q) -> p c q", c=4)[:, :, 8 * b:8 * b + 8],
                                lhsT=C.ones_b[:, :], rhs=PTs[:, cs, kv * 32:kv * 32 + 32].rearrange("p (c t) -> p c t", c=4),
                                start=False, stop=False, skip_group_check=True),
                                reads=[("PTsb", cs), "ones_b"], writes=[("ps", 7)])
                P.op("dve", lambda e: e.tensor_tensor(out=dtmp[:].rearrange("p (c q) -> p c q", c=4), in0=DEN[:, :].rearrange("p (c q) -> p c q", c=4),
                                                      in1=esink[:, :].unsqueeze(2).broadcast_to([128, 4, 128]), op=ALU.add),
                     reads=[("ps", 7), "esink"], writes=["dtmp"])
                P.op("dve", lambda e: e.reciprocal(out=dtmp[:], in_=dtmp[:]), reads=["dtmp"], writes=["dtmp"])
                P.op("dve", lambda e, j=j: e.tensor_tensor(out=C.obT[:, :, j * 128:(j + 1) * 128], in0=NUM[:, :].rearrange("p (c q) -> p c q", c=4),
                                                           in1=dtmp[:].rearrange("p (c q) -> p c q", c=4), op=ALU.mult),
                     reads=[("ps", 6), "dtmp"], writes=[("obT", j)])

        TG = [(0, 512), (512, 1024), (1024, 1536), (1536, 2048), (2048, 2176)]

        def tg_tiles(c0, c1):
            return list(range(c0 // 128, c1 // 128))

        def mixer_C(L, st):
            WR = WRing(st, 1, 6144, "C")
            ws = WR.next()
            base = 3072
            WR.load(ws, [(0, 8, 256, I["w_in"][L][:, base:base + 256]),
                         (2048, 8, 256, I["w_in"][L][:, base + 256:base + 512]),
                         (4096, 8, 256, I["w_in"][L][:, base + 512:base + 768])])
            Wh, Wb, Wc = WR.view(ws, 0, 8, 256), WR.view(ws, 2048, 8, 256), WR.view(ws, 4096, 8, 256)
            wkey = WR.key(ws)
            cw = sb("cw", [128, 2, 3], F32, st)
            for k in range(3):
                for ch in range(2):
                    P.dma("sp", cw[:, ch, k:k + 1], I["conv_w"][L, k, 128 * ch:128 * ch + 128].rearrange("(p o) -> p o", o=1), writes=["cw"])
            ZC = sb("ZC", [128, 2, 2 + SEQ], F32, st)
            ZCs = sb("ZCs", [128, 2, 16, 10], F32, st)
            hcS = sb("hcS", [128, 2, 512], F32, st)
            ycv = sb("ycv", [128, 2, 512], F32, st)
            st32 = sb("st32", [32, 256], F32, st)
            cstp = sb("cstp", [2, 256], F32, st)
            csts = sb("csts", [32, 256], F32, st)
            ztmp = sb("ztmp", [128, 32], F32, st)
            P.op("dve", lambda e: e.memset(ZC[:, :, 0:2], 0.0), writes=["ZC0"])
            P.dma("sp", st32[:], I["sconv"][L], writes=["st32"])
            for ch in range(2):
                P.op("pe", lambda e, ch=ch: e.transpose(C.ps[3][:, 0:32], st32[:, ch * 128:(ch + 1) * 128], C.ident_f[0:32, 0:32]),
                     reads=["st32", "ident_f"], writes=[("ps", 3)])
                P.op("dve", lambda e, ch=ch: e.tensor_copy(out=ZCs[:, ch, :, 0:2], in_=C.ps[3][:, 0:32].rearrange("p (b r) -> p b r", b=16)),
                     reads=[("ps", 3)], writes=["ZCs0"])
            it = 0
            for (c0, c1) in TG:
                n = c1 - c0
                tiles = [("hT", j) for j in tg_tiles(c0, c1)]
                for ch in range(2):
                    s2 = it % 2
                    it += 1
                    banks = (0, 1, 2) if s2 == 0 else (4, 5, 6)
                    for bi, Wx in enumerate((Wh, Wb, Wc)):
                        for kc in range(8):
                            P.op("pe", lambda e, kc=kc, Wx=Wx, b_=banks[bi], ch=ch: e.matmul(
                                C.ps[b_][:, 0:n], lhsT=Wx[:, kc, ch * 128:(ch + 1) * 128], rhs=C.hT[:, kc, c0:c1],
                                start=(kc == 0), stop=(kc == 7)), reads=tiles + [wkey], writes=[("ps", banks[bi])])
                    P.op("act", lambda e, s2=s2, b_=banks[0]: e.activation(out=hcS[:, s2, 0:n], in_=C.ps[b_][:, 0:n], func=AF.Copy),
                         reads=[("ps", banks[0])], writes=[("hcS", s2)])
                    if c0 < SEQ:
                        zc = ZC[:, ch, 2 + c0:2 + c1]
                        z1 = ZC[:, ch, 1 + c0:1 + c1]
                        z0 = ZC[:, ch, c0:c1]
                        yv = ycv[:, s2, 0:n]
                        hv = hcS[:, s2, 0:n]
                        pc = C.ps[banks[2]][:, 0:n]
                        pb = C.ps[banks[1]][:, 0:n]
                        ov = C.ocT[:, ch, c0:c1]
                    else:
                        zc = ZCs[:, ch, :, 2:10]
                        z1 = ZCs[:, ch, :, 1:9]
                        z0 = ZCs[:, ch, :, 0:8]
                        yv = ycv[:, s2, 0:128].rearrange("p (b t) -> p b t", b=16)
                        hv = hcS[:, s2, 0:128].rearrange("p (b t) -> p b t", b=16)
                        pc = C.ps[banks[2]][:, 0:128].rearrange("p (b t) -> p b t", b=16)
                        pb = C.ps[banks[1]][:, 0:128].rearrange("p (b t) -> p b t", b=16)
                        ov = C.ocT[:, ch, c0:c1].rearrange("p (b t) -> p b t", b=16)
                    zk = ("ZC", ch)
                    P.op("dve", lambda e, zc=zc, pc=pc, hv=hv: e.tensor_tensor(out=zc, in0=pc, in1=hv, op=ALU.mult),
                         reads=[("ps", banks[2]), ("hcS", s2)], writes=[zk])
                    P.op("dve", lambda e, yv=yv, zc=zc, ch=ch: e.tensor_scalar(out=yv, in0=zc, scalar1=cw[:, ch, 2:3], scalar2=None, op0=ALU.mult),
                         reads=[zk, "cw"], writes=[("ycv", s2)])
                    P.op("dve", lambda e, yv=yv, z1=z1, ch=ch: e.scalar_tensor_tensor(out=yv, in0=z1, scalar=cw[:, ch, 1:2], in1=yv, op0=ALU.mult, op1=ALU.add),
                         reads=[zk, "cw", "ZC0", "ZCs0"], writes=[("ycv", s2)])
                    P.op("dve", lambda e, yv=yv, z0=z0, ch=ch: e.scalar_tensor_tensor(out=yv, in0=z0, scalar=cw[:, ch, 0:1], in1=yv, op0=ALU.mult, op1=ALU.add),
                         reads=[zk, "cw", "ZC0", "ZCs0"], writes=[("ycv", s2)])
                    P.op("dve", lambda e, ov=ov, pb=pb, yv=yv: e.tensor_tensor(out=ov, in0=pb, in1=yv, op=ALU.mult),
                         reads=[("ps", banks[1]), ("ycv", s2)], writes=[("ocT", ch, c0)])
            for ch in range(2):
                P.op("pe", lambda e, ch=ch: e.transpose(C.ps[3][0:2, 0:128], ZC[:, ch, SEQ:SEQ + 2], C.ident_f[:]),
                     reads=[("ZC", ch), "ident_f"], writes=[("ps", 3)])
                P.op("dve", lambda e, ch=ch: e.tensor_copy(out=cstp[:, ch * 128:(ch + 1) * 128], in_=C.ps[3][0:2, 0:128]),
                     reads=[("ps", 3)], writes=["cstp"])
                P.op("dve", lambda e, ch=ch: e.tensor_copy(out=ztmp[:].rearrange("p (b r) -> p b r", b=16), in_=ZCs[:, ch, :, 8:10]),
                     reads=[("ZC", ch)], writes=["ztmp"])
                P.op("pe", lambda e, ch=ch: e.transpose(C.ps[3][0:32, 128:256], ztmp[:], C.ident_f[:]),
                     reads=["ztmp", "ident_f"], writes=[("ps", 3)])
                P.op("dve", lambda e, ch=ch: e.tensor_copy(out=csts[:, ch * 128:(ch + 1) * 128], in_=C.ps[3][0:32, 128:256]),
                     reads=[("ps", 3)], writes=["csts"])
            P.dma("sp", O["conv_p"][L], cstp[:], reads=["cstp"])
            P.dma("sp", O["conv_s"][L], csts[:], reads=["csts"])

        TWO_PI = 6.283185307179586
        MAGIC = 12582912.0

        def range_sincos(src, out_sin, out_cos, tmp, keyp, reads):
            P.op("dve", lambda e: e.tensor_scalar(out=tmp, in0=src, scalar1=1.0 / TWO_PI, scalar2=MAGIC, op0=ALU.mult, op1=ALU.add),
                 reads=reads, writes=[keyp + "t"])
            P.op("dve", lambda e: e.tensor_scalar(out=tmp, in0=tmp, scalar1=-MAGIC, scalar2=None, op0=ALU.add), reads=[keyp + "t"], writes=[keyp + "t"])
            P.op("dve", lambda e: e.scalar_tensor_tensor(out=tmp, in0=tmp, scalar=-TWO_PI, in1=src, op0=ALU.mult, op1=ALU.add),
                 reads=[keyp + "t"] + reads, writes=[keyp + "t"])
            P.op("dve", lambda e: e.tensor_scalar(out=tmp, in0=tmp, scalar1=3.141592, scalar2=-3.141592, op0=ALU.min, op1=ALU.max),
                 reads=[keyp + "t"], writes=[keyp + "t"])
            P.op("act", lambda e: e.activation(out=out_sin, in_=tmp, func=AF.Sin), reads=[keyp + "t"], writes=[keyp + "s"])
            P.op("dve", lambda e: e.scalar_tensor_tensor(out=tmp, in0=tmp, scalar=-1.0, in1=tmp, op0=ALU.mult, op1=ALU.max),
                 reads=[keyp + "t", keyp + "s"], writes=[keyp + "t"])
            P.op("act", lambda e: e.activation(out=out_cos, in_=tmp, func=AF.Sin, scale=-1.0, bias=C.halfpi[:, 0:1]),
                 reads=[keyp + "t", "halfpi"], writes=[keyp + "c"])

        def mixer_D(L, st):
            WR = WRing(st, 1, 3072, "D")
            ws = WR.next()
            WR.load(ws, [(0, 8, 256, I["w_in"][L][:, 3840:4096]), (2048, 2, 512, I["w_d_glu"][L])])
            Wu, Wg = WR.view(ws, 0, 8, 256), WR.view(ws, 2048, 2, 512)
            wkey = WR.key(ws)
            udT = C.odT
            ygT = sb("ygT", [128, 2, NTOK], BF16, st)
            it = 0
            for (c0, c1) in TG:
                n = c1 - c0
                tiles = [("hT", j) for j in tg_tiles(c0, c1)]
                for hf in range(2):
                    bk = it % 2
                    it += 1
                    for kc in range(8):
                        P.op("pe", lambda e, kc=kc, hf=hf, bk=bk: e.matmul(C.ps[bk][:, 0:n], lhsT=Wu[:, kc, hf * 128:(hf + 1) * 128], rhs=C.hT[:, kc, c0:c1],
                                                                          start=(kc == 0), stop=(kc == 7)), reads=tiles + [wkey], writes=[("ps", bk)])
                    P.op("act", lambda e, hf=hf, bk=bk: e.activation(out=udT[:, hf, c0:c1], in_=C.ps[bk][:, 0:n], func=AF.Copy),
                         reads=[("ps", bk)], writes=["udT"])
            ps8 = sb("ps8", [8, 3, 128], F32, st)
            ldt = sb("ldt", [8, 2], F32, st)
            LP = sb("LP", [128, 3, 8], F32, st)
            P.dma("sp", ps8[:, 0, :], I["lam_re"][L], writes=["ps8"])
            P.dma("sp", ps8[:, 1, :], I["lam_im"][L], writes=["ps8"])
            P.dma("sp", ldt[:], I["log_dt"][L], writes=["ldt"])
            P.op("dve", lambda e: e.tensor_copy(out=ps8[:, 2, :].rearrange("p (g n) -> p g n", g=2), in_=ldt[:, :].unsqueeze(2).broadcast_to([8, 2, 64])),
                 reads=["ldt"], writes=["ps8"])
            for k in range(3):
                P.op("pe", lambda e, k=k: e.transpose(C.ps[3][:, 8 * k:8 * k + 8], ps8[:, k, :], C.ident_f[0:8, 0:8]),
                     reads=["ps8", "ident_f"], writes=[("ps", 3)])
            P.op("dve", lambda e: e.tensor_copy(out=LP[:].rearrange("p k a -> p (k a)"), in_=C.ps[3][:, 0:24]), reads=[("ps", 3)], writes=["LP"])
            dtv = sb("dtv", [128, 8], F32, st)
            arv = sb("arv", [128, 8], F32, st)
            thv = sb("thv", [128, 8], F32, st)
            P.op("act", lambda e: e.activation(out=dtv[:], in_=LP[:, 2, :], func=AF.Exp), reads=["LP"], writes=["dtv"])
            P.op("dve", lambda e: e.tensor_tensor(out=arv[:], in0=LP[:, 0, :], in1=dtv[:], op=ALU.mult), reads=["LP", "dtv"], writes=["arv"])
            P.op("dve", lambda e: e.tensor_tensor(out=thv[:], in0=LP[:, 1, :], in1=dtv[:], op=ALU.mult), reads=["LP", "dtv"], writes=["thv"])
            kar = sb("kar", [128, 8, 9], F32, st)
            kth = sb("kth", [128, 8, 9], F32, st)
            ktm = sb("ktm", [128, 8, 9], F32, st)
            ksn = sb("ksn", [128, 8, 9], F32, st)
            kcs = sb("kcs", [128, 8, 9], F32, st)
            APW = sb("APW", [128, 2, 8, 9], F32, st)
            kiota = C.iota[:, 0:9].unsqueeze(1).broadcast_to([128, 8, 9])
            P.op("dve", lambda e: e.tensor_tensor(out=kar[:], in0=arv[:, :].unsqueeze(2).broadcast_to([128, 8, 9]), in1=kiota, op=ALU.mult),
                 reads=["arv", "iota"], writes=["kar"])
            P.op("dve", lambda e: e.tensor_tensor(out=kth[:], in0=thv[:, :].unsqueeze(2).broadcast_to([128, 8, 9]), in1=kiota, op=ALU.mult),
                 reads=["thv", "iota"], writes=["kth"])
            P.op("act", lambda e: e.activation(out=kar[:], in_=kar[:], func=AF.Exp), reads=["kar"], writes=["kar"])
            range_sincos(kth[:], ksn[:], kcs[:], ktm[:], "kk", ["kth"])
            P.op("dve", lambda e: e.tensor_tensor(out=APW[:, 0], in0=kar[:], in1=kcs[:], op=ALU.mult), reads=["kar", "kkc"], writes=["APW"])
            P.op("dve", lambda e: e.tensor_tensor(out=APW[:, 1], in0=kar[:], in1=ksn[:], op=ALU.mult), reads=["kar", "kks"], writes=["APW"])
            dump("ssm_APW", APW[:], [128, 2, 8, 9], ["APW"])
            dump("ssm_udT", udT[:], [128, 2, NTOK], ["udT"])
            na8i = sb("na8i", [128, 8], F32, st)
            phi = sb("phi", [128, 8], F32, st)
            P.op("dve", lambda e: e.tensor_scalar(out=na8i[:], in0=APW[:, 1, :, 8], scalar1=-1.0, scalar2=None, op0=ALU.mult), reads=["APW"], writes=["na8i"])
            P.op("dve", lambda e: e.tensor_scalar(out=phi[:], in0=thv[:], scalar1=8.0, scalar2=None, op0=ALU.mult), reads=["thv"], writes=["phi"])
            cf = sb("cf", [128, 6, 8], F32, st)
            lr, li = LP[:, 0, :], LP[:, 1, :]
            P.op("dve", lambda e: e.tensor_scalar(out=cf[:, 0], in0=APW[:, 0, :, 1], scalar1=-1.0, scalar2=None, op0=ALU.add), reads=["APW"], writes=["cf"])
            P.op("dve", lambda e: e.tensor_copy(out=cf[:, 1], in_=APW[:, 1, :, 1]), reads=["APW"], writes=["cf"])
            P.op("dve", lambda e: e.tensor_tensor(out=cf[:, 2], in0=lr, in1=lr, op=ALU.mult), reads=["LP"], writes=["cf"])
            P.op("dve", lambda e: e.tensor_tensor(out=cf[:, 5], in0=li, in1=li, op=ALU.mult), reads=["LP"], writes=["cf"])
            P.op("dve", lambda e: e.tensor_tensor(out=cf[:, 2], in0=cf[:, 2], in1=cf[:, 5], op=ALU.add), writes=["cf"])
            P.op("dve", lambda e: e.reciprocal(out=cf[:, 2], in_=cf[:, 2]), writes=["cf"])
            P.op("dve", lambda e: e.tensor_tensor(out=cf[:, 3], in0=cf[:, 0], in1=lr, op=ALU.mult), writes=["cf"])
            P.op("dve", lambda e: e.tensor_tensor(out=cf[:, 5], in0=cf[:, 1], in1=li, op=ALU.mult), writes=["cf"])
            P.op("dve", lambda e: e.tensor_tensor(out=cf[:, 3], in0=cf[:, 3], in1=cf[:, 5], op=ALU.add), writes=["cf"])
            P.op("dve", lambda e: e.tensor_tensor(out=cf[:, 3], in0=cf[:, 3], in1=cf[:, 2], op=ALU.mult), writes=["cf"])
            P.op("dve", lambda e: e.tensor_tensor(out=cf[:, 4], in0=cf[:, 1], in1=lr, op=ALU.mult), writes=["cf"])
            P.op("dve", lambda e: e.tensor_tensor(out=cf[:, 5], in0=cf[:, 0], in1=li, op=ALU.mult), writes=["cf"])
            P.op("dve", lambda e: e.tensor_tensor(out=cf[:, 4], in0=cf[:, 4], in1=cf[:, 5], op=ALU.subtract), writes=["cf"])
            P.op("dve", lambda e: e.tensor_tensor(out=cf[:, 4], in0=cf[:, 4], in1=cf[:, 2], op=ALU.mult), writes=["cf"])
            Bm = sb("Bm", [128, 2, 8, 16], F32, st)
            bb = sb("bb", [128, 2, 8, 16], F32, st)
            btmp = sb("btmp", [128, 8, 16], F32, st)
            P.dma("sp", Bm[:, 0], I["b_re"][L].rearrange("(a p) j -> p a j", p=128), writes=["Bm"])
            P.dma("sp", Bm[:, 1], I["b_im"][L].rearrange("(a p) j -> p a j", p=128), writes=["Bm"])
            cre = cf[:, 3].unsqueeze(2).broadcast_to([128, 8, 16])
            cim = cf[:, 4].unsqueeze(2).broadcast_to([128, 8, 16])
            P.op("dve", lambda e: e.tensor_tensor(out=bb[:, 0], in0=Bm[:, 0], in1=cre, op=ALU.mult), reads=["Bm", "cf"], writes=["bb"])
            P.op("dve", lambda e: e.tensor_tensor(out=btmp[:], in0=Bm[:, 1], in1=cim, op=ALU.mult), reads=["Bm", "cf"], writes=["btmp"])
            P.op("dve", lambda e: e.tensor_tensor(out=bb[:, 0], in0=bb[:, 0], in1=btmp[:], op=ALU.subtract), writes=["bb"])
            P.op("dve", lambda e: e.tensor_tensor(out=bb[:, 1], in0=Bm[:, 1], in1=cre, op=ALU.mult), reads=["Bm", "cf"], writes=["bb"])
            P.op("dve", lambda e: e.tensor_tensor(out=btmp[:], in0=Bm[:, 0], in1=cim, op=ALU.mult), writes=["btmp"])
            P.op("dve", lambda e: e.tensor_tensor(out=bb[:, 1], in0=bb[:, 1], in1=btmp[:], op=ALU.add), writes=["bb"])
            dump("ssm_bb", bb[:], [128, 2, 8, 16], ["bb"])
            dump("ssm_cf", cf[:], [128, 6, 8], ["cf", "bb"])
            dump("ssm_Bm", Bm[:], [128, 2, 8, 16], ["Bm", "bb"])
            dump("ssm_LP", LP[:], [128, 3, 8], ["LP", "bb"])
            Cst = sb("Cst", [128, 2, 2, 64], F32, st)
            CT = sb("CT", [128, 2, 8, 16], F32, st)
            P.dma("sp", Cst[:, 0], I["c_re"][L].rearrange("(h p) n -> p h n", p=128), writes=["Cst"])
            P.dma("sp", Cst[:, 1], I["c_im"][L].rearrange("(h p) n -> p h n", p=128), writes=["Cst"])
            for ri in range(2):
                for hf in range(2):
                    P.op("pe", lambda e, ri=ri, hf=hf: e.transpose(C.ps[3][0:64, 256 * ri + 128 * hf:256 * ri + 128 * hf + 128], Cst[:, ri, hf, :], C.ident_f[:]),
                         reads=["Cst", "ident_f"], writes=[("ps", 3)])
            for ri in range(2):
                for hf in range(2):
                    for g2 in range(2):
                        src = C.ps[3][0:64, 256 * ri + 128 * hf:256 * ri + 128 * hf + 128].rearrange("p (a g i) -> p a g i", a=4, g=2)[:, :, g2, :]
                        P.op("dve", lambda e, ri=ri, hf=hf, g2=g2, src=src: e.tensor_copy(out=CT[64 * g2:64 * g2 + 64, ri, 4 * hf:4 * hf + 4, :], in_=src),
                             reads=[("ps", 3)], writes=["CT"])
            dump("ssm_CT", CT[:], [128, 2, 8, 16], ["CT"])
            dcol = sb("dcol", [128, 2], F32, st)
            for hf in range(2):
                P.dma("sp", dcol[:, hf:hf + 1], I["ssm_d"][L, hf].rearrange("(p o) -> p o", o=1), writes=["dcol"])
            x0s = sb("x0s", [16, 1024], F32, st)
            X0 = sb("X0", [128, 2, 8, 16], F32, st)
            for ri in range(2):
                P.dma("sp", x0s[:, :], I["sdre" if ri == 0 else "sdim"][L], writes=["x0s"])
                for pr in range(8):
                    P.op("pe", lambda e, ri=ri, pr=pr: e.transpose(C.ps[2][:, (ri * 8 + pr) * 16:(ri * 8 + pr) * 16 + 16], x0s[:, pr * 128:(pr + 1) * 128], C.ident_f[0:16, 0:16]),
                         reads=["x0s", "ident_f"], writes=[("ps", 2)])
            P.op("dve", lambda e: e.tensor_copy(out=X0[:].rearrange("p r a b -> p (r a b)"), in_=C.ps[2][:, 0:256]), reads=[("ps", 2)], writes=["X0"])
            Mx = sb("Mx", [128, 8, 128], BF16, st)
            CAx = sb("CAx", [128, 4, 9, 2, 64], BF16, st)
            Xprev = sb("Xprev", [128, 2, 4, 272], BF16, st)
            XF = sb("XF", [128, 2, 8], F32, st)
            XFs = sb("XFs", [128, 2, 8, 16], F32, st)
            P.op("pool", lambda e: e.memset(Xprev[:, :, :, 0:1], 0.0), writes=["Xprev0"])
            BA = sb("BA", [128, 2, 8, 16], F32, st)
            BAx = sb("BAx", [128, 2, 8, 2, 64], BF16, st)
            SMt = sb("SMt", [128, 16, 128], BF16, st)
            CA = sb("CA", [128, 2, 9, 16], F32, st)
            tmpa = sb("tmpa", [128, 9, 16], F32, st)
            EC = sb("EC", [128, 256], F32, st)
            ES = sb("ES", [128, 256], F32, st)
            ang = sb("ang", [128, 256], F32, st)
            atm = sb("atm", [128, 256], F32, st)
            SR = sb("SR", [128, 2, 256], F32, st)
            Wt = sb("Wt", [128, 2, 256], F32, st)
            XS = sb("XS", [128, 2, 272], F32, st)
            t256 = sb("t256", [128, 256], F32, st)
            P.op("pool", lambda e: e.memset(BAx[:], 0.0), writes=["BAx0"])
            ity = 0
            for pr in range(8):
                hf, pl = pr // 4, pr % 4
                if pl == 0:
                    ykeys = [("ps", 4), ("ps", 5)]
                    P.op("pool", lambda e: e.memset(Mx[:], 0.0), writes=["Mx", "Mxd"] + [("Mxp", k) for k in range(4)])
                    P.op("pool", lambda e: e.memset(CAx[:], 0.0), writes=["CAx"] + [("CAx", k) for k in range(4)])
                quad, q = pl // 2, pl % 2
                bs = pr % 2
                po = 64 * quad
                apr = APW[:, 0, pr, 0:8].unsqueeze(2).broadcast_to([128, 8, 16])
                api = APW[:, 1, pr, 0:8].unsqueeze(2).broadcast_to([128, 8, 16])
                bbr = bb[:, 0, pr, :].unsqueeze(1).broadcast_to([128, 8, 16])
                bbi = bb[:, 1, pr, :].unsqueeze(1).broadcast_to([128, 8, 16])
                t8 = tmpa[:, 0:8, :]
                P.op("dve", lambda e, apr=apr, bbr=bbr: e.tensor_tensor(out=BA[:, 0], in0=apr, in1=bbr, op=ALU.mult), reads=["APW", "bb"], writes=["BA"])
                P.op("dve", lambda e, api=api, bbi=bbi, t8=t8: e.tensor_tensor(out=t8, in0=api, in1=bbi, op=ALU.mult), reads=["APW", "bb"], writes=["tmpa"])
                P.op("dve", lambda e, t8=t8: e.tensor_tensor(out=BA[:, 0], in0=BA[:, 0], in1=t8, op=ALU.subtract), writes=["BA"])
                P.op("dve", lambda e, apr=apr, bbi=bbi: e.tensor_tensor(out=BA[:, 1], in0=apr, in1=bbi, op=ALU.mult), reads=["APW", "bb"], writes=["BA"])
                P.op("dve", lambda e, api=api, bbr=bbr, t8=t8: e.tensor_tensor(out=t8, in0=api, in1=bbr, op=ALU.mult), writes=["tmpa"])
                P.op("dve", lambda e, t8=t8: e.tensor_tensor(out=BA[:, 1], in0=BA[:, 1], in1=t8, op=ALU.add), writes=["BA"])
                for ri in range(2):
                    for g2 in range(2):
                        P.op("dve", lambda e, ri=ri, g2=g2, bs=bs, q=q: e.tensor_scalar(
                            out=BAx[:, bs, :, ri, 32 * q + 16 * g2:32 * q + 16 * g2 + 16], in0=BA[:, ri], scalar1=C.gmask[:, g2:g2 + 1], scalar2=None, op0=ALU.mult),
                            reads=["BA", "gmask", "BAx0", ("BAx", bs)], writes=[("BAx", bs)])
                for hb in range(2):
                    pv = psb(4 + hb)
                    for k8 in range(8):
                        idx = hb * 8 + k8
                        s_, ri = idx // 2, idx % 2
                        P.op("pe", lambda e, pv=pv, k8=k8, s_=s_, ri=ri, bs=bs, po=po: e.transpose(
                            pv[po:po + 64, k8 * 128:(k8 + 1) * 128], BAx[:, bs, s_, ri, :], C.ident_b[:]),
                            reads=[("BAx", bs), "ident_b"], writes=[("ps", 4 + hb)])
                    P.op("act", lambda e, pv=pv, hb=hb, bs=bs, po=po: e.activation(
                        out=SMt[po:po + 64, hb * 8:hb * 8 + 8, :], in_=pv[po:po + 64, :].rearrange("p (a m) -> p a m", a=8), func=AF.Copy),
                        reads=[("ps", 4 + hb)], writes=["SMt"])
                udv = udT[:, hf, :].rearrange("p (c s) -> p c s", s=8)
                for ri in range(2):
                    for s in range(8):
                        P.op("pe", lambda e, ri=ri, s=s, bs=bs, po=po, udv=udv: e.matmul(
                            C.ps[ri][:, 0:272], lhsT=SMt[po:po + 64, (7 - s) * 2 + ri, :], rhs=udv[po:po + 64, :, s],
                            start=(s == 0), stop=(s == 7)), reads=["SMt", "udT"], writes=[("ps", ri)])
                P.op("dve", lambda e, pr=pr: e.tensor_scalar(out=ang[:], in0=C.iota[:, 0:256], scalar1=phi[:, pr:pr + 1], scalar2=None, op0=ALU.mult),
                     reads=["iota", "phi"], writes=["ang"])
                range_sincos(ang[:], ES[:], EC[:], atm[:], "rot", ["ang"])
                Sre, Sim = C.ps[0][:, 0:256], C.ps[1][:, 0:256]
                P.op("dve", lambda e: e.tensor_tensor(out=SR[:, 0], in0=Sre, in1=EC[:], op=ALU.mult), reads=[("ps", 0), "rotc"], writes=["SR"])
                P.op("dve", lambda e: e.tensor_tensor(out=t256[:], in0=Sim, in1=ES[:], op=ALU.mult), reads=[("ps", 1), "rots"], writes=["t256"])
                P.op("dve", lambda e: e.tensor_tensor(out=SR[:, 0], in0=SR[:, 0], in1=t256[:], op=ALU.add), writes=["SR"])
                P.op("dve", lambda e: e.tensor_tensor(out=SR[:, 1], in0=Sim, in1=EC[:], op=ALU.mult), reads=[("ps", 1), "rotc"], writes=["SR"])
                P.op("dve", lambda e: e.tensor_tensor(out=t256[:], in0=Sre, in1=ES[:], op=ALU.mult), reads=[("ps", 0), "rots"], writes=["t256"])
                P.op("dve", lambda e: e.tensor_tensor(out=SR[:, 1], in0=SR[:, 1], in1=t256[:], op=ALU.subtract), writes=["SR"])
                rho8 = APW[:, 0, pr, 8:9]
                for ri in range(2):
                    P.op("dve", lambda e, ri=ri, pr=pr: e.tensor_tensor_scan(out=Wt[:, ri], data0=kar[:, pr, 8:9].broadcast_to([128, 256]), data1=SR[:, ri],
                                                                           initial=0.0, op0=ALU.mult, op1=ALU.add), reads=["SR", "kar"], writes=["Wt"])
                P.op("dve", lambda e: e.tensor_tensor(out=XS[:, 0, 0:256], in0=Wt[:, 0], in1=EC[:], op=ALU.mult), reads=["Wt", "rotc"], writes=["XS"])
                P.op("dve", lambda e: e.tensor_tensor(out=t256[:], in0=Wt[:, 1], in1=ES[:], op=ALU.mult), reads=["Wt", "rots"], writes=["t256"])
                P.op("dve", lambda e: e.tensor_tensor(out=XS[:, 0, 0:256], in0=XS[:, 0, 0:256], in1=t256[:], op=ALU.subtract), writes=["XS"])
                P.op("dve", lambda e: e.tensor_tensor(out=XS[:, 1, 0:256], in0=Wt[:, 1], in1=EC[:], op=ALU.mult), reads=["Wt", "rotc"], writes=["XS"])
                P.op("dve", lambda e: e.tensor_tensor(out=t256[:], in0=Wt[:, 0], in1=ES[:], op=ALU.mult), writes=["t256"])
                P.op("dve", lambda e: e.tensor_tensor(out=XS[:, 1, 0:256], in0=XS[:, 1, 0:256], in1=t256[:], op=ALU.add), writes=["XS"])
                a8r, a8i = APW[:, 0, pr, 8:9], APW[:, 1, pr, 8:9]
                P.op("dve", lambda e, pr=pr, a8r=a8r: e.tensor_scalar(out=XS[:, 0, 256:272], in0=X0[:, 0, pr, :], scalar1=a8r, scalar2=None, op0=ALU.mult),
                     reads=["X0", "APW"], writes=["XS"])
                P.op("dve", lambda e, pr=pr: e.scalar_tensor_tensor(out=XS[:, 0, 256:272], in0=X0[:, 1, pr, :], scalar=na8i[:, pr:pr + 1], in1=XS[:, 0, 256:272],
                                                                    op0=ALU.mult, op1=ALU.add), reads=["X0", "na8i"], writes=["XS"])
                P.op("dve", lambda e: e.tensor_tensor(out=XS[:, 0, 256:272], in0=C.ps[0][:, 256:272], in1=XS[:, 0, 256:272], op=ALU.add), reads=[("ps", 0)], writes=["XS"])
                P.op("dve", lambda e, pr=pr, a8r=a8r: e.tensor_scalar(out=XS[:, 1, 256:272], in0=X0[:, 1, pr, :], scalar1=a8r, scalar2=None, op0=ALU.mult),
                     reads=["X0", "APW"], writes=["XS"])
                P.op("dve", lambda e, pr=pr, a8i=a8i: e.scalar_tensor_tensor(out=XS[:, 1, 256:272], in0=X0[:, 0, pr, :], scalar=a8i, in1=XS[:, 1, 256:272],
                                                                             op0=ALU.mult, op1=ALU.add), reads=["X0", "APW"], writes=["XS"])
                P.op("dve", lambda e: e.tensor_tensor(out=XS[:, 1, 256:272], in0=C.ps[1][:, 256:272], in1=XS[:, 1, 256:272], op=ALU.add), reads=[("ps", 1)], writes=["XS"])
                for ri in range(2):
                    P.op("act", lambda e, ri=ri, pl=pl: e.activation(out=Xprev[:, ri, pl, 1:256], in_=XS[:, ri, 0:255], func=AF.Copy), reads=["XS"], writes=[("Xprev", pl)])
                    P.op("act", lambda e, ri=ri, pr=pr, pl=pl: e.activation(out=Xprev[:, ri, pl, 256:272], in_=X0[:, ri, pr, :], func=AF.Copy), reads=["X0"], writes=[("Xprev", pl)])
                    P.op("act", lambda e, ri=ri, pr=pr: e.activation(out=XF[:, ri, pr:pr + 1], in_=XS[:, ri, 255:256], func=AF.Copy), reads=["XS"], writes=["XF"])
                    P.op("act", lambda e, ri=ri, pr=pr: e.activation(out=XFs[:, ri, pr, :], in_=XS[:, ri, 256:272], func=AF.Copy), reads=["XS"], writes=["XFs"])
                if pr == 0:
                    dump("ssm_XS0", XS[:], [128, 2, 272], ["XS"])
                    dump("ssm_SR0", SR[:], [128, 2, 256], ["SR"])
                    dump("ssm_EC0", EC[:], [128, 256], ["rotc"])
                    dump("ssm_ES0", ES[:], [128, 256], ["rots"])
                    dump("ssm_SMt0", SMt[:], [128, 16, 128], ["SMt"])
                apr9 = APW[:, 0, pr, :].unsqueeze(2).broadcast_to([128, 9, 16])
                api9 = APW[:, 1, pr, :].unsqueeze(2).broadcast_to([128, 9, 16])
                ctr = CT[:, 0, pr, :].unsqueeze(1).broadcast_to([128, 9, 16])
                cti = CT[:, 1, pr, :].unsqueeze(1).broadcast_to([128, 9, 16])
                P.op("dve", lambda e, apr9=apr9, ctr=ctr: e.tensor_tensor(out=CA[:, 0], in0=apr9, in1=ctr, op=ALU.mult), reads=["APW", "CT"], writes=["CA"])
                P.op("dve", lambda e, api9=api9, cti=cti: e.tensor_tensor(out=tmpa[:], in0=api9, in1=cti, op=ALU.mult), reads=["APW", "CT"], writes=["tmpa"])
                P.op("dve", lambda e: e.tensor_tensor(out=CA[:, 0], in0=CA[:, 0], in1=tmpa[:], op=ALU.subtract), writes=["CA"])
                P.op("dve", lambda e, api9=api9, ctr=ctr: e.tensor_tensor(out=CA[:, 1], in0=api9, in1=ctr, op=ALU.mult), reads=["APW", "CT"], writes=["CA"])
                P.op("dve", lambda e, apr9=apr9, cti=cti: e.tensor_tensor(out=tmpa[:], in0=apr9, in1=cti, op=ALU.mult), writes=["tmpa"])
                P.op("dve", lambda e: e.tensor_tensor(out=CA[:, 1], in0=CA[:, 1], in1=tmpa[:], op=ALU.add), writes=["CA"])
                for ri in range(2):
                    for g2 in range(2):
                        P.op("dve", lambda e, ri=ri, g2=g2, pl=pl, q=q: e.tensor_scalar(
                            out=CAx[:, pl, :, ri, 32 * q + 16 * g2:32 * q + 16 * g2 + 16], in0=CA[:, ri], scalar1=C.gmask[:, g2:g2 + 1],
                            scalar2=(1.0 if ri == 0 else -1.0), op0=ALU.mult, op1=ALU.mult),
                            reads=["CA", "gmask", "CAx"], writes=[("CAx", pl)])
                for ri in range(2):
                    P.op("pe", lambda e, ri=ri, bs=bs, pl=pl, q=q, po=po: e.matmul(
                        C.ps[2][po:po + 64, 0:256].rearrange("p (t c) -> p t c", t=8), lhsT=BAx[:, bs, 0, ri, :],
                        rhs=CAx[:, pl, 0:8, ri, 32 * q:32 * q + 32], start=(ri == 0), stop=(ri == 1)),
                        reads=[("BAx", bs), ("CAx", pl)], writes=[("ps", 2)])
                P.op("act", lambda e, pl=pl, po=po: e.activation(
                    out=Mx[po:po + 64, :, 32 * pl:32 * pl + 32], in_=C.ps[2][po:po + 64, 0:256].rearrange("p (t c) -> p t c", t=8), func=AF.Copy),
                    reads=[("ps", 2), "Mx"], writes=[("Mxp", pl)])
                if pl < 3:
                    continue
                P.op("dve", lambda e, hf=hf: e.scalar_tensor_tensor(out=Mx[:, 0, :], in0=C.ident_f[:], scalar=dcol[:, hf:hf + 1], in1=Mx[:, 0, :],
                                                                    op0=ALU.mult, op1=ALU.add),
                     reads=["ident_f", "dcol"] + [("Mxp", k) for k in range(4)], writes=["Mxd"])
                if hf == 0:
                    dump("ssm_Mx0", Mx[:], [128, 8, 128], ["Mxd"])
                    dump("ssm_Xprev0", Xprev[:], [128, 2, 4, 272], [("Xprev", k) for k in range(4)] + ["Xprev0"])
                udv = udT[:, hf, :].rearrange("p (c s) -> p c s", s=8)
                ygv = ygT[:, hf, :].rearrange("p (c s) -> p c s", s=8)
                for t in range(8):
                    bk = 4 + (ity % 2)
                    ity += 1
                    Y = C.ps[bk]
                    for s in range(t + 1):
                        P.op("pe", lambda e, Y=Y, t=t, s=s, udv=udv: e.matmul(
                            Y[:, 0:272], lhsT=Mx[:, t - s, :], rhs=udv[:, :, s], start=(s == 0), stop=False, skip_group_check=True),
                            reads=["Mxd", "udT"], writes=[("ps", bk)])
                    for pl2 in range(4):
                        po2 = 64 * (pl2 // 2)
                        for ri in range(2):
                            P.op("pe", lambda e, Y=Y, pl2=pl2, po2=po2, ri=ri, t=t: e.matmul(
                                Y[po2:po2 + 64, 0:272], lhsT=CAx[:, pl2, t + 1, ri, :], rhs=Xprev[:, ri, pl2, :], start=False, stop=False, skip_group_check=True),
                                reads=[("CAx", pl2), ("Xprev", pl2), "Xprev0"], writes=[("ps", bk)])
                    P.op("act", lambda e, Y=Y, ygv=ygv, t=t: e.activation(out=ygv[:, :, t], in_=Y[:, 0:272], func=AF.Gelu_apprx_tanh),
                         reads=[("ps", bk)], writes=["ygT"])
            dump("ssm_ygT", ygT[:], [128, 2, NTOK], ["ygT"])
            sg = sb("sg", [128, 2, 512], BF16, st)
            it = 0
            for (c0, c1) in TG:
                n = c1 - c0
                for vc in range(2):
                    s2 = it % 2
                    it += 1
                    bv_, bg_ = (0, 1) if s2 == 0 else (2, 3)
                    for kc in range(2):
                        P.op("pe", lambda e, kc=kc, vc=vc, bv_=bv_: e.matmul(C.ps[bv_][:, 0:n], lhsT=Wg[:, kc, vc * 128:(vc + 1) * 128], rhs=ygT[:, kc, c0:c1],
                                                                           start=(kc == 0), stop=(kc == 1)), reads=["ygT", wkey], writes=[("ps", bv_)])
                    for kc in range(2):
                        P.op("pe", lambda e, kc=kc, vc=vc, bg_=bg_: e.matmul(C.ps[bg_][:, 0:n], lhsT=Wg[:, kc, 256 + vc * 128:256 + (vc + 1) * 128], rhs=ygT[:, kc, c0:c1],
                                                                           start=(kc == 0), stop=(kc == 1)), reads=["ygT", wkey], writes=[("ps", bg_)])
                    P.op("act", lambda e, s2=s2, bg_=bg_: e.activation(out=sg[:, s2, 0:n], in_=C.ps[bg_][:, 0:n], func=AF.Sigmoid),
                         reads=[("ps", bg_)], writes=[("sg", s2)])
                    P.op("dve", lambda e, s2=s2, bv_=bv_, vc=vc: e.tensor_tensor(out=C.odT[:, vc, c0:c1], in0=C.ps[bv_][:, 0:n], in1=sg[:, s2, 0:n], op=ALU.mult),
                         reads=[("ps", bv_), ("sg", s2)], writes=[("odT", vc, c0), "udT"])
            xfo = sb("xfo", [8, 2, 128], F32, st)
            xfs = sb("xfs", [16, 1024], F32, st)
            for ri in range(2):
                P.op("pe", lambda e, ri=ri: e.transpose(C.ps[6][0:8, 128 * ri:128 * ri + 128], XF[:, ri, :], C.ident_f[:]),
                     reads=["XF", "ident_f"], writes=[("ps", 6)])
            P.op("dve", lambda e: e.tensor_copy(out=xfo[:].rearrange("p r n -> p (r n)"), in_=C.ps[6][0:8, 0:256]), reads=[("ps", 6)], writes=["xfo"])
            P.dma("sp", O["dre_p"][L].rearrange("(a n) -> a n", n=128), xfo[:, 0, :], reads=["xfo"])
            P.dma("sp", O["dim_p"][L].rearrange("(a n) -> a n", n=128), xfo[:, 1, :], reads=["xfo"])
            for ri in range(2):
                for hb in range(2):
                    for k4 in range(4):
                        pr = hb * 4 + k4
                        P.op("pe", lambda e, ri=ri, pr=pr, k4=k4: e.transpose(C.ps[7][0:16, 128 * k4:128 * k4 + 128], XFs[:, ri, pr, :], C.ident_f[:]),
                             reads=["XFs", "ident_f"], writes=[("ps", 7)])
                    P.op("dve", lambda e, ri=ri, hb=hb: e.tensor_copy(out=xfs[:, 512 * hb:512 * hb + 512], in_=C.ps[7][0:16, 0:512]),
                         reads=[("ps", 7)], writes=["xfs"])
                P.dma("sp", O["dre_s" if ri == 0 else "dim_s"][L], xfs[:, :], reads=["xfs"])

        HALVES = [[(0, 512), (512, 1024)], [(1024, 1536), (1536, 2048), (2048, 2176)]]

        def finalize_A(st):
            rd = sb("rdA", [128, 2, 128], F32, st)
            nf = sb("nfA", [128, 2, 128], F32, st)
            for j in range(NT):
                nv = C.numA[:, :, j * 128:(j + 1) * 128]
                dv = C.denA[:, :, j * 128:(j + 1) * 128]
                P.op("dve", lambda e, dv=dv: e.tensor_copy(out=rd[:], in_=dv), reads=[("denA", j)], writes=["rdA"])
                P.op("dve", lambda e: e.reciprocal(out=rd[:], in_=rd[:]), reads=["rdA"], writes=["rdA"])
                P.op("dve", lambda e, nv=nv: e.tensor_copy(out=nf[:], in_=nv), reads=[("numA", j)], writes=["nfA"])
                P.op("dve", lambda e: e.tensor_tensor(out=nf[:], in0=nf[:], in1=rd[:], op=ALU.mult), reads=["nfA", "rdA"], writes=["nfA"])
                P.op("dve", lambda e, nv=nv: e.tensor_copy(out=nv, in_=nf[:]), reads=["nfA"], writes=[("numA", j)])

        def merge_half(L, H, st):
            groups = HALVES[H]
            hc0 = groups[0][0]
            ncol = groups[-1][1] - hc0
            mT = sb("mergedT", [128, 8, ncol], BF16, st)
            sgm = sb("sgm", [128, 2, 512], BF16, st)
            tmpm = sb("tmpm", [128, 2, 512], BF16, st)
            WR = WRing(st, 2, 6144, "M")
            srcs = [(C.numA, 2, "numA"), (C.obT, 4, "obT"), (C.ocT, 2, "ocT"), (C.odT, 2, "odT")]
            brw = ["w_br_a", "w_br_b", "w_br_c", "w_br_d"]
            it = 0
            for fcg in range(2):
                for bi_, i in enumerate(DBG_M_BR):
                    oT, kci, _ = srcs[i]
                    ws = WR.next()
                    WR.load(ws, [(0, 8, 512, I["w_in"][L][:, GATE_OFF + 1024 * i + 512 * fcg:GATE_OFF + 1024 * i + 512 * fcg + 512]),
                                 (4096, kci, 512, I[brw[i]][L][:, 512 * fcg:512 * fcg + 512])])
                    Wg_, Wb_ = WR.view(ws, 0, 8, 512), WR.view(ws, 4096, kci, 512)
                    wkey = WR.key(ws)
                    for (c0, c1) in groups:
                        n = c1 - c0
                        tl = tg_tiles(c0, c1)
                        hkeys = [("hT", j) for j in tl]
                        for fc in range(4):
                            s2 = it % 2
                            it += 1
                            bg_, bp_ = (0, 1) if s2 == 0 else (2, 3)
                            for kc in range(8):
                                P.op("pe", lambda e, kc=kc, fc=fc, bg_=bg_, Wg_=Wg_, c0=c0, c1=c1, n=n: e.matmul(
                                    C.ps[bg_][:, 0:n], lhsT=Wg_[:, kc, fc * 128:(fc + 1) * 128], rhs=C.hT[:, kc, c0:c1], start=(kc == 0), stop=(kc == 7)),
                                    reads=hkeys + [wkey], writes=[("ps", bg_)])
                            for kc in range(kci):
                                P.op("pe", lambda e, kc=kc, fc=fc, bp_=bp_, Wb_=Wb_, oT=oT, c0=c0, c1=c1, n=n, kci=kci: e.matmul(
                                    C.ps[bp_][:, 0:n], lhsT=Wb_[:, kc, fc * 128:(fc + 1) * 128], rhs=oT[:, kc, c0:c1], start=(kc == 0), stop=(kc == kci - 1)),
                                    reads=["mix_out", wkey], writes=[("ps", bp_)])
                            P.op("act", lambda e, s2=s2, bg_=bg_, n=n: e.activation(out=sgm[:, s2, 0:n], in_=C.ps[bg_][:, 0:n], func=AF.Sigmoid),
                                 reads=[("ps", bg_)], writes=[("sgm", s2)])
                            mv = mT[:, 4 * fcg + fc, c0 - hc0:c1 - hc0]
                            mk = ("mT", 4 * fcg + fc, c0)
                            if bi_ == 0:
                                P.op("dve", lambda e, mv=mv, bp_=bp_, s2=s2, n=n: e.tensor_tensor(out=mv, in0=C.ps[bp_][:, 0:n], in1=sgm[:, s2, 0:n], op=ALU.mult),
                                     reads=[("ps", bp_), ("sgm", s2)], writes=[mk])
                            else:
                                P.op("dve", lambda e, bp_=bp_, s2=s2, n=n: e.tensor_tensor(out=tmpm[:, s2, 0:n], in0=C.ps[bp_][:, 0:n], in1=sgm[:, s2, 0:n], op=ALU.mult),
                                     reads=[("ps", bp_), ("sgm", s2)], writes=[("tmpm", s2)])
                                P.op("dve", lambda e, mv=mv, s2=s2, n=n: e.tensor_tensor(out=mv, in0=tmpm[:, s2, 0:n], in1=mv, op=ALU.add),
                                     reads=[mk, ("tmpm", s2)], writes=[mk])
            if H == 0:
                dump("mT_%d" % L, mT[:], [128, 8, ncol], [("mT", f, c0) for f in range(8) for (c0, c1) in groups])
            P.dma("sp", C.gvec[:, 0, :], I["n_mix_post"][L].partition_broadcast(128), writes=[("g", 0)])
            wsl = []
            for nh in range(2):
                ws = WR.next()
                WR.load(ws, [(0, 8, 512, I["w_out"][L][:, 512 * nh:512 * nh + 512])])
                wsl.append((WR.view(ws, 0, 8, 512), WR.key(ws)))
            mkeys = [("mT", f, c0) for f in range(8) for (c0, c1) in groups]
            tiles = [j for (c0, c1) in groups for j in tg_tiles(c0, c1)]
            post_norm_residual(tiles, lambda kc, j: mT[:, kc, j * 128 - hc0:j * 128 - hc0 + 128], 8, wsl, mkeys, st)

        def post_norm_residual(tiles, lhs_fn, nk, wsl, in_keys, st):
            ss2 = sb("ss2", [128, 8], F32, st)
            junk = sb("junk_p", [128, 512], BF16, st)
            tmpx = sb("tmpx", [128, 2, 512], F32, st)
            for j in tiles:
                s2 = j % 2
                banks = (0, 1) if s2 == 0 else (2, 3)
                for nh in range(2):
                    Wv, wk = wsl[nh]
                    for kc in range(nk):
                        P.op("pe", lambda e, kc=kc, j=j, b_=banks[nh], Wv=Wv: e.matmul(
                            C.ps[b_][:, :], lhsT=lhs_fn(kc, j), rhs=Wv[:, kc, :], start=(kc == 0), stop=(kc == nk - 1)),
                            reads=in_keys + [wk], writes=[("ps", banks[nh])])
                    P.op("act", lambda e, b_=banks[nh], nh=nh: e.activation(out=junk[:], in_=C.ps[b_][:, :], func=AF.Square, accum_out=ss2[:, nh:nh + 1]),
                         reads=[("ps", banks[nh])], writes=["junk_p", "ss2"])
                P.op("dve", lambda e: e.tensor_scalar(out=ss2[:, 2:4], in0=ss2[:, 0:2], scalar1=ss2[:, 1:2], scalar2=None, op0=ALU.add), reads=["ss2"], writes=["ss2b"])
                P.op("dve", lambda e: e.tensor_scalar(out=ss2[:, 2:4], in0=ss2[:, 2:4], scalar1=1.0 / D, scalar2=EPS, op0=ALU.mult, op1=ALU.add),
                     reads=["ss2b"], writes=["ss2b"])
                P.op("act", lambda e: e.activation(out=ss2[:, 2:4], in_=ss2[:, 2:4], func=AF.Sqrt), reads=["ss2b"], writes=["ss2b"])
                P.op("dve", lambda e: e.reciprocal(out=ss2[:, 4:6], in_=ss2[:, 2:4]), reads=["ss2b"], writes=["ss2c"])
                for nh in range(2):
                    P.op("dve", lambda e, b_=banks[nh], nh=nh: e.scalar_tensor_tensor(
                        out=tmpx[:, nh, :], in0=C.ps[b_][:, :], scalar=ss2[:, 4:5], in1=C.gvec[:, 0, 512 * nh:512 * nh + 512], op0=ALU.mult, op1=ALU.mult),
                        reads=[("ps", banks[nh]), "ss2c", ("g", 0)], writes=[("tmpx", nh)])
                    P.op("pool", lambda e, j=j, nh=nh: e.tensor_tensor(out=C.X[:, j, 512 * nh:512 * nh + 512], in0=C.X[:, j, 512 * nh:512 * nh + 512],
                                                                     in1=tmpx[:, nh, :], op=ALU.add),
                         reads=[("X", j), ("tmpx", nh)], writes=[("X", j)])

        def ffn_half(L, H, st):
            groups = HALVES[H]
            hc0 = groups[0][0]
            ncol = groups[-1][1] - hc0
            tiles = [j for (c0, c1) in groups for j in tg_tiles(c0, c1)]
            hTh = sb("hTh", [128, 8, ncol], BF16, st)
            aT = sb("aT", [128, NFF, ncol], BF16, st)
            sil = sb("sil", [128, 2, 512], BF16, st)
            norm_to_hT(L, "n_ffn_pre", tiles, st, hTh, hc0)
            WR = WRing(st, 2, 6144, "F")
            it = 0
            for ffg in range(NFF // 2):
                ws = WR.next()
                WR.load(ws, [(0, 8, 256, I["w_ffn_gate"][L][:, 256 * ffg:256 * ffg + 256]),
                             (2048, 8, 256, I["w_ffn_up"][L][:, 256 * ffg:256 * ffg + 256])])
                Wg_, Wu_ = WR.view(ws, 0, 8, 256), WR.view(ws, 2048, 8, 256)
                wkey = WR.key(ws)
                for (c0, c1) in groups:
                    n = c1 - c0
                    hkeys = [("hT", j) for j in tg_tiles(c0, c1)]
                    for fc in range(2):
                        s2 = it % 2
                        it += 1
                        bg_, bu_ = (0, 1) if s2 == 0 else (2, 3)
                        for kc in range(8):
                            P.op("pe", lambda e, kc=kc, fc=fc, bg_=bg_, Wg_=Wg_, c0=c0, c1=c1, n=n: e.matmul(
                                C.ps[bg_][:, 0:n], lhsT=Wg_[:, kc, fc * 128:(fc + 1) * 128], rhs=hTh[:, kc, c0 - hc0:c1 - hc0], start=(kc == 0), stop=(kc == 7)),
                                reads=hkeys + [wkey], writes=[("ps", bg_)])
                        for kc in range(8):
                            P.op("pe", lambda e, kc=kc, fc=fc, bu_=bu_, Wu_=Wu_, c0=c0, c1=c1, n=n: e.matmul(
                                C.ps[bu_][:, 0:n], lhsT=Wu_[:, kc, fc * 128:(fc + 1) * 128], rhs=hTh[:, kc, c0 - hc0:c1 - hc0], start=(kc == 0), stop=(kc == 7)),
                                reads=hkeys + [wkey], writes=[("ps", bu_)])
                        P.op("act", lambda e, s2=s2, bg_=bg_, n=n: e.activation(out=sil[:, s2, 0:n], in_=C.ps[bg_][:, 0:n], func=AF.Silu),
                             reads=[("ps", bg_)], writes=[("sil", s2)])
                        P.op("dve", lambda e, s2=s2, bu_=bu_, n=n, ffc=2 * ffg + fc, c0=c0, c1=c1: e.tensor_tensor(
                            out=aT[:, ffc, c0 - hc0:c1 - hc0], in0=C.ps[bu_][:, 0:n], in1=sil[:, s2, 0:n], op=ALU.mult),
                            reads=[("ps", bu_), ("sil", s2)], writes=[("aT", 2 * ffg + fc, c0)])
            akeys = {c0: [("aT", f, c0) for f in range(NFF)] for (c0, c1) in groups}
            it = 0
            for fcg in range(4):
                ws = WR.next()
                WR.load(ws, [(0, NFF, 256, I["w_ffn_down"][L][:, 256 * fcg:256 * fcg + 256])])
                Wd_ = WR.view(ws, 0, NFF, 256)
                wkey = WR.key(ws)
                for (c0, c1) in groups:
                    n = c1 - c0
                    for fc in range(2):
                        bk = 4 + (it % 2)
                        it += 1
                        for ffc in range(NFF):
                            P.op("pe", lambda e, ffc=ffc, fc=fc, bk=bk, Wd_=Wd_, c0=c0, c1=c1, n=n: e.matmul(
                                C.ps[bk][:, 0:n], lhsT=Wd_[:, ffc, fc * 128:(fc + 1) * 128], rhs=aT[:, ffc, c0 - hc0:c1 - hc0],
                                start=(ffc == 0), stop=(ffc == NFF - 1)), reads=akeys[c0] + [wkey], writes=[("ps", bk)])
                        P.op("act", lambda e, bk=bk, n=n, f=2 * fcg + fc, c0=c0, c1=c1: e.activation(
                            out=hTh[:, f, c0 - hc0:c1 - hc0], in_=C.ps[bk][:, 0:n], func=AF.Copy),
                            reads=[("ps", bk)] + [("hT", j) for j in tg_tiles(c0, c1)], writes=[("fT", 2 * fcg + fc, c0)])
            P.dma("sp", C.gvec[:, 0, :], I["n_ffn_post"][L].partition_broadcast(128), writes=[("g", 0)])
            ss2 = sb("ss2f", [128, 8], F32, st)
            P.op("dve", lambda e: e.memset(ss2[:], 1.0), writes=["ss2f"])
            junk = sb("junk_f", [128, D], BF16, st)
            tmpx = sb("tmpxf", [128, D], F32, st)
            for j in tiles:
                bank = 6 + (j % 2)
                pv = psb(bank)
                c0g = [c0 for (c0, c1) in groups if c0 <= j * 128 < c1][0]
                lc = j * 128 - hc0
                for f in range(8):
                    P.op("pe", lambda e, f=f, lc=lc, pv=pv: e.transpose(pv[:, f * 128:(f + 1) * 128], hTh[:, f, lc:lc + 128], C.ident_b[:]),
                         reads=[("fT", f, c0g), "ident_b"], writes=[("ps", bank)])
                P.op("act", lambda e, pv=pv: e.activation(out=junk[:], in_=pv, func=AF.Square, accum_out=ss2[:, 0:1]),
                     reads=[("ps", bank)], writes=["junk_f", "ss2f"])
                P.op("dve", lambda e: e.tensor_scalar(out=ss2[:, 2:4], in0=ss2[:, 0:2], scalar1=1.0 / D, scalar2=EPS, op0=ALU.mult, op1=ALU.add),
                     reads=["ss2f"], writes=["ss2f2"])
                P.op("act", lambda e: e.activation(out=ss2[:, 2:4], in_=ss2[:, 2:4], func=AF.Sqrt), reads=["ss2f2"], writes=["ss2f2"])
                P.op("dve", lambda e: e.reciprocal(out=ss2[:, 4:6], in_=ss2[:, 2:4]), reads=["ss2f2"], writes=["ss2g"])
                P.op("act", lambda e, pv=pv: e.activation(out=tmpx[:], in_=pv, func=AF.Copy), reads=[("ps", bank)], writes=["tmpxf"])
                P.op("dve", lambda e: e.scalar_tensor_tensor(out=tmpx[:], in0=tmpx[:], scalar=ss2[:, 4:5], in1=C.gvec[:, 0, :], op0=ALU.mult, op1=ALU.mult),
                     reads=["tmpxf", "ss2g", ("g", 0)], writes=["tmpxf"])
                P.op("pool", lambda e, j=j: e.tensor_tensor(out=C.X[:, j, :], in0=C.X[:, j, :], in1=tmpx[:], op=ALU.add),
                     reads=[("X", j), "tmpxf"], writes=[("X", j)])

        def ple_half(L, H, st, last):
            groups = HALVES[H]
            hc0 = groups[0][0]
            ncol = groups[-1][1] - hc0
            tiles = [j for (c0, c1) in groups for j in tg_tiles(c0, c1)]
            WR = WRing(st, 2, 6144, "P")
            wsl = []
            for nh in range(2):
                ws = WR.next()
                pieces = [(0, 8, 512, I["w_ple_gate"][L][:, 512 * nh:512 * nh + 512])]
                if nh == 0:
                    pieces.append((4096, 2, 1024, I["w_ple"][L]))
                WR.load(ws, pieces)
                wsl.append((WR.view(ws, 0, 8, 512), WR.key(ws)))
            Wp = WR.view(0, 4096, 2, 1024)
            wpk = WR.key(0)
            xb = sb("xb", [128, 2, D], BF16, st)
            xT = sb("xT", [128, 2, 8, 128], BF16, st)
            pt = sb("pt", [128, 2, 256], BF16, st)
            pT = sb("pT", [128, 2, 2, 128], BF16, st)
            sgp = sb("sgp", [128, 2, 512], F32, st)
            tmpp = sb("tmpp", [128, 2, 512], F32, st)
            for j in tiles:
                s2 = j % 2
                P.op("act", lambda e, j=j, s2=s2: e.activation(out=xb[:, s2, :], in_=C.X[:, j, :], func=AF.Copy), reads=[("X", j)], writes=[("xb", s2)])
                pv = psb(6 + s2)
                for c in range(8):
                    P.op("pe", lambda e, c=c, s2=s2, pv=pv: e.transpose(pv[:, c * 128:(c + 1) * 128], xb[:, s2, c * 128:(c + 1) * 128], C.ident_b[:]),
                         reads=[("xb", s2), "ident_b"], writes=[("ps", 6 + s2)])
                P.op("act", lambda e, s2=s2, pv=pv: e.activation(out=xT[:, s2], in_=pv.rearrange("p (c t) -> p c t", c=8), func=AF.Copy),
                     reads=[("ps", 6 + s2)], writes=[("xT", s2)])
                psrc = I["pp"][L, 128 * j:128 * (j + 1), :] if j < 16 else I["psm"][L]
                P.dma("pool", pt[:, s2, :], psrc, writes=[("pt", s2)])
                pv2 = psb(4 + s2)
                for c in range(2):
                    P.op("pe", lambda e, c=c, s2=s2, pv2=pv2: e.transpose(pv2[:, c * 128:(c + 1) * 128], pt[:, s2, c * 128:(c + 1) * 128], C.ident_b[:]),
                         reads=[("pt", s2), "ident_b"], writes=[("ps", 4 + s2)])
                P.op("act", lambda e, s2=s2, pv2=pv2: e.activation(out=pT[:, s2], in_=pv2[:, 0:256].rearrange("p (c t) -> p c t", c=2), func=AF.Copy),
                     reads=[("ps", 4 + s2)], writes=[("pT", s2)])
                for nh in range(2):
                    Wv, wk = wsl[nh]
                    bg_, bp_ = (0, 1) if nh == 0 else (2, 3)
                    for kc in range(8):
                        P.op("pe", lambda e, kc=kc, s2=s2, bg_=bg_, Wv=Wv: e.matmul(C.ps[bg_][:, :], lhsT=xT[:, s2, kc, :], rhs=Wv[:, kc, :], start=(kc == 0), stop=(kc == 7)),
                             reads=[("xT", s2), wk], writes=[("ps", bg_)])
                    for kc in range(2):
                        P.op("pe", lambda e, kc=kc, s2=s2, bp_=bp_, nh=nh: e.matmul(C.ps[bp_][:, :], lhsT=pT[:, s2, kc, :], rhs=Wp[:, kc, 512 * nh:512 * nh + 512],
                                                                                start=(kc == 0), stop=(kc == 1)),
                             reads=[("pT", s2), wpk], writes=[("ps", bp_)])
                    P.op("act", lambda e, nh=nh, bg_=bg_: e.activation(out=sgp[:, nh, :], in_=C.ps[bg_][:, :], func=AF.Sigmoid), reads=[("ps", bg_)], writes=[("sgp", nh)])
                    P.op("dve", lambda e, nh=nh, bp_=bp_: e.tensor_tensor(out=tmpp[:, nh, :], in0=C.ps[bp_][:, :], in1=sgp[:, nh, :], op=ALU.mult),
                         reads=[("ps", bp_), ("sgp", nh)], writes=[("tmpp", nh)])
                    P.op("pool", lambda e, j=j, nh=nh: e.tensor_tensor(out=C.X[:, j, 512 * nh:512 * nh + 512], in0=C.X[:, j, 512 * nh:512 * nh + 512],
                                                                     in1=tmpp[:, nh, :], op=ALU.add),
                         reads=[("X", j), ("tmpp", nh)], writes=[("X", j)])
                if last:
                    if j < 16:
                        P.dma("sp", O["yp"][128 * j:128 * (j + 1), :], C.X[:, j, :], reads=[("X", j)])
                    else:
                        P.dma("sp", O["ys"], C.X[:, j, :], reads=[("X", j)])

        stages = []
        for L in range(depth):
            with ExitStack() as lay:
                C.hT = sb("hT", [128, 8, NTOK], BF16, lay)
                C.odT = sb("odT", [128, 2, NTOK], BF16, lay)
                P.mark("norm1_%d" % L)
                with ExitStack() as st:
                    norm_to_hT(L, "n_mix_pre", list(range(NT)), st, C.hT, 0)
                barrier()
                dump("hT_%d" % L, C.hT[:], [128, 8, NTOK], [("hT", j) for j in range(NT)])
                if stop_after == ("norm", L):
                    break
                P.mark("D_%d" % L)
                with ExitStack() as st:
                    mixer_D(L, st)
                barrier()
                dump("odT_%d" % L, C.odT[:], [128, 2, NTOK], [("odT", v, c0) for v in range(2) for (c0, c1) in TG])
                if stop_after == ("D", L):
                    break
                P.mark("A_%d" % L)
                C.numA = sb("numA", [128, 2, NTOK], BF16, lay)
                with ExitStack() as att:
                    load_att_consts(att)
                    with ExitStack() as st:
                        C.denA = sb("denA", [128, 2, NTOK], BF16, st)
                        for g in range(3):
                            if g not in DBG_A_GROUPS:
                                continue
                            with ExitStack() as st2:
                                mixer_A(L, g, st2)
                            barrier()
                        dump("numA_%d" % L, C.numA[:], [128, 2, NTOK], [("numA", j) for j in range(NT)])
                        dump("denA_%d" % L, C.denA[:], [128, 2, NTOK], [("denA", j) for j in range(NT)])
                        finalize_A(st)
                    barrier()
                dump("oaT_%d" % L, C.numA[:], [128, 2, NTOK], [("numA", j) for j in range(NT)])
                if stop_after == ("A", L):
                    break
                P.mark("B_%d" % L)
                C.obT = sb("obT", [128, 4, NTOK], BF16, lay)
                with ExitStack() as att:
                    load_att_consts(att)
                    with ExitStack() as st:
                        mixer_B(L, st)
                    barrier()
                dump("obT_%d" % L, C.obT[:], [128, 4, NTOK], [("obT", j) for j in range(NT)])
                dump("oaTb_%d" % L, C.numA[:], [128, 2, NTOK], [("numA", j) for j in range(NT)])
                if stop_after == ("B", L):
                    break
                P.mark("C_%d" % L)
                C.ocT = sb("ocT", [128, 2, NTOK], BF16, lay)
                with ExitStack() as st:
                    mixer_C(L, st)
                barrier()
                dump("ocT_%d" % L, C.ocT[:], [128, 2, NTOK], [("ocT", ch, c0) for ch in range(2) for (c0, c1) in TG])
                dump("oaTc_%d" % L, C.numA[:], [128, 2, NTOK], [("numA", j) for j in range(NT)])
                if stop_after == ("C", L):
                    break
                P.op("dve", lambda e: e.engine_nop(), reads=[("numA", j) for j in range(NT)] + [("obT", j) for j in range(NT)]
                     + [("ocT", ch, c0) for ch in range(2) for (c0, c1) in TG] + [("odT", v, c0) for v in range(2) for (c0, c1) in TG], writes=["mix_out"])
                P.mark("merge_%d" % L)
                for H in range(2):
                    with ExitStack() as st:
                        merge_half(L, H, st)
                    barrier()
                dump("x1_%d" % L, C.X[:], [128, NT, D], [("X", j) for j in range(NT)])
            barrier()
            if stop_after == ("merge", L):
                break
            P.mark("ffn_%d" % L)
            for H in range(2):
                with ExitStack() as st:
                    ffn_half(L, H, st)
                barrier()
            dump("x2_%d" % L, C.X[:], [128, NT, D], [("X", j) for j in range(NT)])
            if stop_after == ("ffn", L):
                break
            P.mark("ple_%d" % L)
            for H in range(2):
                with ExitStack() as st:
                    ple_half(L, H, st, last=(L == depth - 1))
                barrier()
        P.mark('end')
        P.emit(es)
    P.min_rem = C.min_rem
    return nc, dump_specs, P


_CACHE = {}


def _get_program(depth=DEPTH, stop_after=None, dumps=()):
    key = (depth, stop_after, tuple(dumps))
    if key not in _CACHE:
        _CACHE[key] = build_program(depth, stop_after, dumps)
    return _CACHE[key]


def _core_inputs(c, a, consts, w_b, w_brb):
    f = np.ascontiguousarray
    sl = slice(16 * c, 16 * c + 16)
    m = dict(
        xp=f(a["x_prompt"][c]), xs=f(a["x_sample"][sl].reshape(128, D)),
        pp=f(a["p_prompt"][:, c]), psm=f(a["p_sample"][:, sl].reshape(DEPTH, 128, 256)),
        ca1k=f(a["cache_a1_k"][:, sl].reshape(DEPTH, 16, 128, 256)), ca1v=f(a["cache_a1_v"][:, sl].reshape(DEPTH, 16, 128, 256)),
        ca2k=f(a["cache_a2_k"][:, sl].reshape(DEPTH, 16, 512, 256)), ca2v=f(a["cache_a2_v"][:, sl].reshape(DEPTH, 16, 512, 256)),
        ca3k=f(a["cache_a3_k"][:, sl].reshape(DEPTH, 16, 2048, 256)), ca3v=f(a["cache_a3_v"][:, sl].reshape(DEPTH, 16, 2048, 256)),
        cbk=f(a["cache_b_k"][:, sl].reshape(DEPTH, 16, 128, 128)), cbv=f(a["cache_b_v"][:, sl].reshape(DEPTH, 16, 128, 128)),
        sconv=f(a["state_c_conv"][:, sl].reshape(DEPTH, 32, 256)),
        sdre=f(a["state_d_re"][:, sl].reshape(DEPTH, 16, 1024)), sdim=f(a["state_d_im"][:, sl].reshape(DEPTH, 16, 1024)),
        n_mix_pre=a["norm_mix_pre"], n_mix_post=a["norm_mix_post"], n_ffn_pre=a["norm_ffn_pre"], n_ffn_post=a["norm_ffn_post"],
        w_in=a["w_in"], w_b=w_b, sinks=a["attn_sinks"], conv_w=a["conv_c_w"],
        lam_re=a["ssm_lam_re"].reshape(DEPTH, 8, 128), lam_im=a["ssm_lam_im"].reshape(DEPTH, 8, 128), log_dt=a["ssm_log_dt"].reshape(DEPTH, 8, 2),
        b_re=a["ssm_b_re"].reshape(DEPTH, 1024, 16), b_im=a["ssm_b_im"].reshape(DEPTH, 1024, 16),
        c_re=a["ssm_c_re"].reshape(DEPTH, 256, 64), c_im=a["ssm_c_im"].reshape(DEPTH, 256, 64),
        ssm_d=a["ssm_d"].reshape(DEPTH, 2, 128), w_d_glu=a["w_d_glu"],
        w_br_a=a["w_br_a"], w_br_b=w_brb, w_br_c=a["w_br_c"], w_br_d=a["w_br_d"],
        w_out=a["w_out"], w_ffn_gate=a["w_ffn_gate"], w_ffn_up=a["w_ffn_up"], w_ffn_down=a["w_ffn_down"],
        w_ple=a["w_ple"], w_ple_gate=a["w_ple_gate"],
    )
    m.update(consts)
    return {k: np.ascontiguousarray(v, dtype=np.float32) for k, v in m.items()}


def _run(inputs, depth=DEPTH, stop_after=None, dumps=(), cores=NCORES):
    a = {k: np.asarray(v) for k, v in inputs.items()}
    nc, dump_specs, _ = _get_program(depth, stop_after, dumps)
    consts = _host_consts()
    w_b, w_brb = _host_weights(a["w_in"], a["w_br_b"])
    in_maps = [_core_inputs(c, a, consts, w_b, w_brb) for c in range(cores)]
    res = run_bass_kernel_spmd(nc, in_maps, core_ids=list(range(cores)))
    return res.results


def kernel(**inputs):
    r = _run(inputs)
    n = NCORES

    def stack_p(name, shape_tail):
        return np.stack([r[c][name] for c in range(n)], axis=1).reshape((DEPTH, n) + shape_tail)

    def cat_s(name, shape_tail):
        return np.concatenate([r[c][name] for c in range(n)], axis=1).reshape((DEPTH, 16 * n) + shape_tail)

    yp = np.stack([r[c]["yp"] for c in range(n)], axis=0)
    ys = np.concatenate([r[c]["ys"] for c in range(n)], axis=0).reshape(16 * n, 8, D)
    outs = [yp, ys]
    for g, w in ((1, 128), (2, 512), (3, 2048)):
        for kv in ("k", "v"):
            outs.append(stack_p("a%d%s_p" % (g, kv), (w, 4, 64)))
            outs.append(cat_s("a%d%s_s" % (g, kv), (w, 4, 64)))
    for kv in ("k", "v"):
        outs.append(stack_p("b%s_p" % kv, (128, 2, 64)))
        outs.append(cat_s("b%s_s" % kv, (128, 2, 64)))
    outs.append(stack_p("conv_p", (2, 256)))
    outs.append(cat_s("conv_s", (2, 256)))
    for nm in ("dre", "dim"):
        outs.append(stack_p(nm + "_p", (16, 64)))
        outs.append(cat_s(nm + "_s", (16, 64)))
    return tuple(np.ascontiguousarray(o, dtype=np.float32) for o in outs)
```

```python
import types
import numpy as np
from contextlib import ExitStack
import concourse.bass as bass
import concourse.mybir as mybir
from concourse.bass_utils import run_bass_kernel_spmd

F32 = mybir.dt.float32
BF16 = mybir.dt.bfloat16
ALU = mybir.AluOpType
AF = mybir.ActivationFunctionType
AX = mybir.AxisListType

NCORES = 8
D = 1024
SEQ = 2048
NT = 17
NTOK = NT * 128
DEPTH = 2
DFF = 2816
NFF = DFF // 128
EPS = 1e-6
PAST = 16384
A_W = (128, 512, 2048)
GATE_OFF = 4096

import os
DBG_A_GROUPS = [int(x) for x in os.environ.get("DBG_A_GROUPS", "0,1,2").split(",")]
DBG_A_NB = int(os.environ.get("DBG_A_NB", "16"))
DBG_A_NT = int(os.environ.get("DBG_A_NT", str(NT)))
DBG_B_NT = int(os.environ.get("DBG_B_NT", str(NT)))
DBG_M_BR = [int(x) for x in os.environ.get("DBG_M_BR", "0,1,2,3").split(",")]

M_C1, M_P1, M_C4, M_E4, M_P4, M_C16, M_E16, M_BD0, M_BD1, M_BD2 = range(10)
NMASK = 10


class _Rec:
    def __init__(self):
        self.call = None

    def __getattr__(self, name):
        def f(*a, **k):
            self.call = (name, a, k)
            return self
        return f

    @staticmethod
    def _rnd(n):
        return 32 if n <= 32 else (64 if n <= 64 else 128)

    def pe_class(self):
        name, a, k = self.call
        out = a[0] if a else k["out"]
        if name == "matmul":
            lhsT = k["lhsT"] if "lhsT" in k else a[1]
        elif name == "transpose":
            lhsT = a[1] if len(a) > 1 else k["in_"]
        else:
            raise ValueError(name)
        return (self._rnd(lhsT.partition_size()), lhsT.base_partition(), self._rnd(lhsT.free_size()), out.base_partition())


class Prog:
    ENG = ("pe", "act", "dve", "pool", "sp")

    def __init__(self, nc):
        self.nc = nc
        self.ops = []
        self.by_eng = {e: [] for e in self.ENG}
        self.writers = {}
        self.readers = {}
        self.self_sync = {"pool", "dve", "act"}
        self.serial = {"pool"}
        self.eng_w = {e: {} for e in self.ENG}
        self.eng_r = {e: {} for e in self.ENG}
        self.pending_bar = {}
        self.dma_since_bar = []
        self.pe_last = {}
        self.log = None
        self.marks = []

    def _add(self, eng, fn, reads, writes, is_dma):
        oid = len(self.ops)
        cls = None
        if not is_dma:
            r = _Rec()
            fn(r)
            name_, a_, k_ = r.call
            if eng == "pe":
                cls = r.pe_class()

            def fn(e, name_=name_, a_=a_, k_=k_):
                return getattr(e, name_)(*a_, **k_)
            auto_out, auto_in = [], []
            if eng in self.self_sync:
                def _isap(x):
                    return hasattr(x, "tensor") and hasattr(x, "ap")
                items = [("arg%d" % i, v) for i, v in enumerate(a_)] + list(k_.items())
                has_out_kw = "out" in k_
                for nm, v in items:
                    if not _isap(v):
                        continue
                    is_out = (nm in ("out", "accum_out")) or (nm == "arg0" and not has_out_kw)
                    (auto_out if is_out else auto_in).append(v.name)
        sub = ("dma", oid) if is_dma else ((eng, cls) if eng == "pe" else (eng,))
        deps = set()
        for r in reads:
            deps.update(self.writers.get(r, {}).values())
        for w_ in writes:
            deps.update(self.writers.get(w_, {}).values())
            deps.update(self.readers.get(w_, ()))
        if not is_dma and eng in self.self_sync:
            ew, er = self.eng_w[eng], self.eng_r[eng]
            for t in auto_in:
                if t in ew:
                    deps.add(ew[t])
            for t in auto_out:
                if t in ew:
                    deps.add(ew[t])
                deps.update(er.get(t, ()))
            for t in auto_in:
                er.setdefault(t, []).append(oid)
            for t in auto_out:
                ew[t] = oid
                er[t] = []
        pb = self.pending_bar.pop(eng, None)
        if pb:
            deps |= pb
        deps.discard(oid)
        op = dict(id=oid, eng=eng, fn=fn, deps=deps, dma=is_dma, need_inc=False, cls=cls)
        self.ops.append(op)
        self.by_eng[eng].append(op)
        if is_dma:
            self.dma_since_bar.append(oid)
        if cls is not None:
            self.pe_last[cls] = oid
        for r in reads:
            self.readers.setdefault(r, []).append(oid)
        for w_ in writes:
            if self.readers.get(w_):
                self.writers[w_] = {sub: oid}
                self.readers[w_] = []
            else:
                self.writers.setdefault(w_, {})[sub] = oid
                self.readers.setdefault(w_, [])
        return op

    @property
    def last_w(self):
        return self.writers

    @staticmethod
    def _snap(fn):
        if getattr(fn, "__closure__", None) is None:
            return fn
        cells = []
        for c in fn.__closure__:
            try:
                cells.append(types.CellType(c.cell_contents))
            except ValueError:
                cells.append(c)
        return types.FunctionType(fn.__code__, fn.__globals__, fn.__name__, fn.__defaults__, tuple(cells))

    def op(self, eng, fn, reads=(), writes=()):
        return self._add(eng, fn, tuple(reads), tuple(writes), False)

    def mark(self, name):
        self.marks.append((name, {e: len(v) for e, v in self.by_eng.items()}))

    def barrier(self):
        d = set(self.dma_since_bar)
        for e in self.ENG:
            if e == "pe":
                d.update(self.pe_last.values())
                continue
            for op in reversed(self.by_eng[e]):
                if not op["dma"]:
                    d.add(op["id"])
                    break
        for e in self.ENG:
            self.pending_bar[e] = set(d) | self.pending_bar.get(e, set())
        self.dma_since_bar = []

    def dma(self, queue, out, in_, reads=(), writes=(), **kw):
        def fn(e, out=out, in_=in_, kw=kw):
            return e.dma_start(out=out, in_=in_, **kw)
        return self._add(queue, fn, tuple(reads), tuple(writes), True)

    def emit(self, es):
        nc = self.nc
        ops = self.ops
        for e in self.serial:
            prev = None
            for op in self.by_eng[e]:
                if prev is not None and not op["dma"]:
                    op["deps"].add(prev)
                if not op["dma"]:
                    prev = op["id"]
        for op in ops:
            kept = set()
            for d in op["deps"]:
                p = ops[d]
                same = (p["eng"] == op["eng"]) and (not p["dma"]) and (not op["dma"]) and (p["eng"] not in self.self_sync)
                if not same:
                    kept.add(d)
                    p["need_inc"] = True
            op["kept"] = kept
        NDS = 40
        esem = {e: es.enter_context(nc.semaphore("sem_" + e)) for e in self.ENG if e != "pe"}
        classes = sorted({op["cls"] for op in ops if op["cls"] is not None})
        pesem = {c: es.enter_context(nc.semaphore("sem_pe%d" % i)) for i, c in enumerate(classes)}
        self.pe_classes = classes
        dsem = [es.enter_context(nc.semaphore("dsem%d" % i)) for i in range(NDS)]
        dcount = [0] * NDS
        cnt = {}
        nd = 0
        for op in ops:
            if op["dma"]:
                s = nd % NDS
                nd += 1
                op["prev_ev"] = (dsem[s], dcount[s]) if dcount[s] > 0 else None
                dcount[s] += 16
                op["ev"] = (dsem[s], dcount[s])
            elif op["need_inc"]:
                sem = pesem[op["cls"]] if op["eng"] == "pe" else esem[op["eng"]]
                cnt[id(sem)] = cnt.get(id(sem), 0) + 1
                op["ev"] = (sem, cnt[id(sem)])
            else:
                op["ev"] = None
        final_waits = [(dsem[s], dcount[s]) for s in range(NDS) if dcount[s] > 0]
        block = es.enter_context(nc.Block())
        by_eng = self.by_eng
        self.n_instr = {e: len(by_eng[e]) for e in self.ENG}

        def run(e, eng):
            seen = {}

            def wait(ev):
                sem, val = ev
                k = id(sem)
                if seen.get(k, 0) >= val:
                    return
                eng.wait_ge(sem, val)
                seen[k] = val
            for op in by_eng[e]:
                wl = []
                for d in sorted(op["kept"]):
                    wait(ops[d]["ev"])
                    wl.append((ops[d]["ev"][0].name, ops[d]["ev"][1]))
                if op["dma"] and op["prev_ev"] is not None:
                    wait(op["prev_ev"])
                ins = op["fn"](eng)
                if op["ev"] is not None:
                    sem, val = op["ev"]
                    ins.then_inc(sem, 16 if op["dma"] else 1)
                if self.log is not None:
                    self.log.append((op["id"], e, wl, (op["ev"][0].name, op["ev"][1]) if op["ev"] else None, str(ins)[:150]))
            if e == "sp":
                for ev in final_waits:
                    wait(ev)

        @block.tensor
        def _(eng):
            run("pe", eng)

        @block.scalar
        def _(eng):
            run("act", eng)

        @block.vector
        def _(eng):
            run("dve", eng)

        @block.gpsimd
        def _(eng):
            run("pool", eng)

        @block.sync
        def _(eng):
            run("sp", eng)


class Ctx:
    pass


def _host_consts():
    kk = np.arange(128)[:, None]
    qq = np.arange(128)[None, :]
    m = np.zeros((128, NMASK, 128), np.float32)
    m[:, M_C1] = kk <= qq
    m[:, M_P1] = kk >= qq
    m[:, M_C4] = (kk <= qq) & ((qq - kk) % 4 == 0)
    m[:, M_E4] = ((qq - kk) % 4 == 0)
    m[:, M_P4] = (kk >= qq) & ((qq - kk) % 4 == 0)
    m[:, M_C16] = (kk <= qq) & ((qq - kk) % 16 == 0)
    m[:, M_E16] = ((qq - kk) % 16 == 0)
    bk, tk = kk // 8, kk % 8
    bq, tq = qq // 8, qq % 8
    m[:, M_BD0] = (bk == bq) & (tk <= tq)
    m[:, M_BD1] = (bk == bq) & (tk <= tq) & ((tq - tk) % 4 == 0)
    m[:, M_BD2] = (bk == bq) & (tk == tq)
    sm = np.zeros((128, 16), np.float32)
    r = np.arange(128)[:, None]
    sm[:, 0:8] = r >= np.arange(8)[None, :]
    sm[:, 8] = 1.0
    sm[:, 9] = (np.arange(128) >= 1)
    half = 32
    inv = (np.float32(10000.0) ** (-np.arange(half, dtype=np.float32) / np.float32(half))).astype(np.float32)
    pos = np.zeros((128, NT), np.float32)
    for j in range(16):
        pos[:, j] = 128 * j + np.arange(128)
    pos[:, 16] = PAST + (np.arange(128) % 8)
    ang = (pos[:, :, None] * inv[None, None, :]).astype(np.float32)
    cos = np.cos(ang).astype(np.float32)
    sin = np.sin(ang).astype(np.float32)
    cosF = np.concatenate([cos, cos], axis=-1)
    sinS = np.concatenate([-sin, sin], axis=-1)
    iota = np.tile(np.arange(256, dtype=np.float32)[None, :], (128, 1))
    gmask = np.zeros((128, 2), np.float32)
    gmask[0:64, 0] = 1.0
    gmask[64:128, 1] = 1.0
    return dict(c_ident=np.eye(128, dtype=np.float32), c_masks=np.ascontiguousarray(m),
                c_smask=sm, c_cos=np.ascontiguousarray(cosF), c_sin=np.ascontiguousarray(sinS),
                c_iota=iota, c_gmask=gmask)


_B_QPERM = [0, 4, 1, 5, 2, 6, 3, 7]


def _host_weights(w_in, w_br_b):
    qb = w_in[:, :, 2304:2816].reshape(DEPTH, D, 8, 64)[:, :, _B_QPERM, :].reshape(DEPTH, D, 512)
    w_b = np.ascontiguousarray(np.concatenate([qb, w_in[:, :, 2816:3072]], axis=2))
    w_brb = np.ascontiguousarray(w_br_b.reshape(DEPTH, 8, 64, D)[:, _B_QPERM].reshape(DEPTH, 512, D))
    return w_b, w_brb


IN_SPECS = [
    ("xp", [SEQ, D]), ("xs", [128, D]), ("pp", [DEPTH, SEQ, 256]), ("psm", [DEPTH, 128, 256]),
    ("ca1k", [DEPTH, 16, 128, 256]), ("ca1v", [DEPTH, 16, 128, 256]),
    ("ca2k", [DEPTH, 16, 512, 256]), ("ca2v", [DEPTH, 16, 512, 256]),
    ("ca3k", [DEPTH, 16, 2048, 256]), ("ca3v", [DEPTH, 16, 2048, 256]),
    ("cbk", [DEPTH, 16, 128, 128]), ("cbv", [DEPTH, 16, 128, 128]),
    ("sconv", [DEPTH, 32, 256]), ("sdre", [DEPTH, 16, 1024]), ("sdim", [DEPTH, 16, 1024]),
    ("n_mix_pre", [DEPTH, D]), ("n_mix_post", [DEPTH, D]), ("n_ffn_pre", [DEPTH, D]), ("n_ffn_post", [DEPTH, D]),
    ("w_in", [DEPTH, D, 8192]), ("w_b", [DEPTH, D, 768]), ("sinks", [DEPTH, 2, 4]), ("conv_w", [DEPTH, 3, 256]),
    ("lam_re", [DEPTH, 8, 128]), ("lam_im", [DEPTH, 8, 128]), ("log_dt", [DEPTH, 8, 2]),
    ("b_re", [DEPTH, 1024, 16]), ("b_im", [DEPTH, 1024, 16]), ("c_re", [DEPTH, 256, 64]), ("c_im", [DEPTH, 256, 64]),
    ("ssm_d", [DEPTH, 2, 128]), ("w_d_glu", [DEPTH, 256, 512]),
    ("w_br_a", [DEPTH, 256, D]), ("w_br_b", [DEPTH, 512, D]), ("w_br_c", [DEPTH, 256, D]), ("w_br_d", [DEPTH, 256, D]),
    ("w_out", [DEPTH, D, D]), ("w_ffn_gate", [DEPTH, D, DFF]), ("w_ffn_up", [DEPTH, D, DFF]),
    ("w_ffn_down", [DEPTH, DFF, D]), ("w_ple", [DEPTH, 256, D]), ("w_ple_gate", [DEPTH, D, D]),
    ("c_ident", [128, 128]), ("c_masks", [128, NMASK, 128]), ("c_smask", [128, 16]),
    ("c_cos", [128, NT, 64]), ("c_sin", [128, NT, 64]), ("c_iota", [128, 256]), ("c_gmask", [128, 2]),
]
OUT_SPECS = [
    ("yp", [SEQ, D]), ("ys", [128, D]),
    ("a1k_p", [DEPTH, 128, 256]), ("a1k_s", [DEPTH, 16, 128, 256]), ("a1v_p", [DEPTH, 128, 256]), ("a1v_s", [DEPTH, 16, 128, 256]),
    ("a2k_p", [DEPTH, 512, 256]), ("a2k_s", [DEPTH, 16, 512, 256]), ("a2v_p", [DEPTH, 512, 256]), ("a2v_s", [DEPTH, 16, 512, 256]),
    ("a3k_p", [DEPTH, 2048, 256]), ("a3k_s", [DEPTH, 16, 2048, 256]), ("a3v_p", [DEPTH, 2048, 256]), ("a3v_s", [DEPTH, 16, 2048, 256]),
    ("bk_p", [DEPTH, 128, 128]), ("bk_s", [DEPTH, 16, 128, 128]), ("bv_p", [DEPTH, 128, 128]), ("bv_s", [DEPTH, 16, 128, 128]),
    ("conv_p", [DEPTH, 2, 256]), ("conv_s", [DEPTH, 32, 256]),
    ("dre_p", [DEPTH, 1024]), ("dre_s", [DEPTH, 16, 1024]), ("dim_p", [DEPTH, 1024]), ("dim_s", [DEPTH, 16, 1024]),
]


def build_program(depth=DEPTH, stop_after=None, dumps=()):
    nc = bass.Bass("TRN2", target_bir_lowering=False)
    I = {n: nc.dram_tensor(n, s, F32, kind="ExternalInput").ap() for n, s in IN_SPECS}
    O = {n: nc.dram_tensor(n, s, F32, kind="ExternalOutput").ap() for n, s in OUT_SPECS}
    dump_specs = []
    P = Prog(nc)
    if os.environ.get("DBG_LOG"):
        P.log = []
    C = Ctx()
    C.min_rem = (1 << 30, '')
    with ExitStack() as es:
        E = es.enter_context

        uid = [0]

        def sb(name, shape, dt, st=None):
            uid[0] += 1
            t = (st or es).enter_context(nc.sbuf_tensor("%s_%d" % (name, uid[0]), shape, dt))
            rem = nc.sbuf_bytes_remaining
            if rem < C.min_rem[0]:
                C.min_rem = (rem, name)
            if os.environ.get("DBG_ALLOC"):
                nb = int(np.prod(shape[1:])) * (2 if dt == BF16 else 4)
                print("ALLOC %-10s bytes/part=%6d  remaining_after=%d" % (name, nb, rem))
            return t

        C.ps = [E(nc.psum_tensor("ps%d" % i, [128, 512], F32)) for i in range(8)]
        C.X = sb("X", [128, NT, D], F32)
        C.ident_f = sb("ident_f", [128, 128], F32)
        C.ident_b = sb("ident_b", [128, 128], BF16)
        C.ones_b = sb("ones_b", [128, 64], BF16)
        C.iota = sb("iota", [128, 256], F32)
        C.gmask = sb("gmask", [128, 2], F32)
        C.halfpi = sb("halfpi", [128, 1], F32)
        C.gvec = sb("gvec", [128, 1, D], F32)
        C.ss = sb("ss", [128, NT], F32)
        C.rs = sb("rs", [128, NT], F32)
        C.gslot = [0]

        def psb(b):
            return C.ps[b][:].bitcast(BF16)

        def dump(name, ap, shape, reads):
            if name in dumps:
                d = nc.dram_tensor("dbg_" + name, list(shape), ap.dtype, kind="ExternalOutput").ap()
                dump_specs.append(("dbg_" + name, list(shape)))
                P.dma("sp", d, ap, reads=reads)

        barrier = P.barrier

        class WRing:
            def __init__(self, st, nslots, elems, name):
                self.n, self.name, self.cnt = nslots, name, 0
                self.buf = sb("wr_" + name, [128, nslots, elems], BF16, st)

            def next(self):
                s_ = self.cnt % self.n
                self.cnt += 1
                return s_

            def key(self, slot):
                return ("w", self.name, slot)

            def load(self, slot, pieces):
                for (off, kc, ncols, src) in pieces:
                    dst = self.buf[:, slot, off:off + kc * ncols].rearrange("p (k n) -> p k n", k=kc)
                    P.dma("pool", dst, src.rearrange("(k p) n -> p k n", p=128), writes=[self.key(slot)])

            def view(self, slot, off, kc, ncols):
                return self.buf[:, slot, off:off + kc * ncols].rearrange("p (k n) -> p k n", k=kc)

        P.dma("sp", C.ident_f[:], I["c_ident"], writes=["ident_f"])
        P.dma("pool", C.ident_b[:], I["c_ident"], writes=["ident_b"])
        P.dma("sp", C.iota[:], I["c_iota"], writes=["iota"])
        P.dma("sp", C.gmask[:], I["c_gmask"], writes=["gmask"])
        P.op("dve", lambda e: e.memset(C.ones_b[:], 1.0), writes=["ones_b"])
        P.op("dve", lambda e: e.memset(C.halfpi[:], 1.5707963267948966), writes=["halfpi"])
        for j in range(16):
            P.dma("sp", C.X[:, j, :], I["xp"][128 * j:128 * (j + 1), :], writes=[("X", j)])
        P.dma("sp", C.X[:, 16, :], I["xs"], writes=[("X", 16)])

        for L_ in range(depth):
            for g_, w_ in enumerate(A_W):
                for b in range(16):
                    for kv_ in ("k", "v"):
                        P.dma("sp", O["a%d%s_s" % (g_ + 1, kv_)][L_, b, 0:w_ - 8, :], I["ca%d%s" % (g_ + 1, kv_)][L_, b, 8:w_, :])
            for b in range(16):
                for kv_ in ("k", "v"):
                    P.dma("sp", O["b%s_s" % kv_][L_, b, 0:120, :], I["cb%s" % kv_][L_, b, 8:128, :])

        def load_att_consts(st):
            C.masks = sb("masks", [128, NMASK, 128], BF16, st)
            C.smask = sb("smask", [128, 16], BF16, st)
            C.cosF = sb("cosF", [128, NT, 64], F32, st)
            C.sinS = sb("sinS", [128, NT, 64], F32, st)
            P.dma("pool", C.masks[:], I["c_masks"], writes=["masks"])
            P.dma("pool", C.smask[:], I["c_smask"], writes=["smask"])
            P.dma("sp", C.cosF[:], I["c_cos"], writes=["rope"])
            P.dma("sp", C.sinS[:], I["c_sin"], writes=["rope"])

        def norm_to_hT(L, gname, tiles, st, dst, col0):
            gs = 0
            P.dma("sp", C.gvec[:, gs, :], I[gname][L].partition_broadcast(128), writes=[("g", gs)])
            junk = sb("junk_n", [128, D], BF16, st)
            hN = sb("hN", [128, 2, D], BF16, st)
            for j in tiles:
                P.op("act", lambda e, j=j: e.activation(out=junk[:], in_=C.X[:, j, :], func=AF.Square,
                                                         accum_out=C.ss[:, j:j + 1]),
                     reads=[("X", j)], writes=["junk_n", ("ss", j)])
            t0, t1 = tiles[0], tiles[-1] + 1
            P.op("dve", lambda e: e.tensor_scalar(out=C.rs[:, t0:t1], in0=C.ss[:, t0:t1], scalar1=1.0 / D, scalar2=EPS,
                                                  op0=ALU.mult, op1=ALU.add),
                 reads=[("ss", j) for j in tiles], writes=["rs"])
            P.op("act", lambda e: e.activation(out=C.rs[:, t0:t1], in_=C.rs[:, t0:t1], func=AF.Sqrt), reads=["rs"], writes=["rs"])
            P.op("dve", lambda e: e.reciprocal(out=C.rs[:, t0:t1], in_=C.rs[:, t0:t1]), reads=["rs"], writes=["rs"])
            for j in tiles:
                hs = j % 2
                P.op("dve", lambda e, j=j, hs=hs: e.scalar_tensor_tensor(
                    out=hN[:, hs, :], in0=C.X[:, j, :], scalar=C.rs[:, j:j + 1], in1=C.gvec[:, gs, :],
                    op0=ALU.mult, op1=ALU.mult), reads=[("X", j), "rs", ("g", gs)], writes=[("hN", hs)])
                bank = 6 + (j % 2)
                pv = psb(bank)
                for c in range(8):
                    P.op("pe", lambda e, c=c, hs=hs, pv=pv: e.transpose(pv[:, c * 128:(c + 1) * 128], hN[:, hs, c * 128:(c + 1) * 128], C.ident_b[:]),
                         reads=[("hN", hs), "ident_b"], writes=[("ps", bank)])
                lc = j * 128 - col0
                P.op("act", lambda e, lc=lc, pv=pv: e.activation(
                    out=dst[:, :, lc:lc + 128], in_=pv.rearrange("p (c t) -> p c t", c=8), func=AF.Copy),
                    reads=[("ps", bank)], writes=[("hT", j)])

        def rope_ops(ps_ap, nheads, j, out_ap, T1, T2, reads, wkey):
            pin = ps_ap.rearrange("p (h d) -> p h d", h=nheads)
            t1 = T1[:, 0:nheads * 64].rearrange("p (h d) -> p h d", h=nheads)
            t2 = T2[:, 0:nheads * 64].rearrange("p (h d) -> p h d", h=nheads)
            cosb = C.cosF[:, j, :].unsqueeze(1).broadcast_to([128, nheads, 64])
            sin_lo = C.sinS[:, j, 0:32].unsqueeze(1).broadcast_to([128, nheads, 32])
            sin_hi = C.sinS[:, j, 32:64].unsqueeze(1).broadcast_to([128, nheads, 32])
            P.op("dve", lambda e: e.tensor_tensor(out=t1, in0=pin, in1=cosb, op=ALU.mult), reads=reads + ["rope"], writes=["ropeT1"])
            P.op("dve", lambda e: e.tensor_tensor(out=t2[:, :, 0:32], in0=pin[:, :, 32:64], in1=sin_lo, op=ALU.mult),
                 reads=reads + ["rope"], writes=["ropeT2"])
            P.op("dve", lambda e: e.tensor_tensor(out=t2[:, :, 32:64], in0=pin[:, :, 0:32], in1=sin_hi, op=ALU.mult),
                 reads=reads + ["rope"], writes=["ropeT2"])
            P.op("dve", lambda e: e.tensor_tensor(out=out_ap, in0=T1[:, 0:nheads * 64], in1=T2[:, 0:nheads * 64], op=ALU.add),
                 reads=["ropeT1", "ropeT2"], writes=[wkey])

        def mixer_A(L, g, st):
            w, dil = A_W[g], (1, 4, 16)[g]
            ck_in, cv_in = I["ca%dk" % (g + 1)], I["ca%dv" % (g + 1)]
            ok_p, ov_p = O["a%dk_p" % (g + 1)], O["a%dv_p" % (g + 1)]
            ok_s, ov_s = O["a%dk_s" % (g + 1)], O["a%dv_s" % (g + 1)]
            first_out_tile = 16 - w // 128
            nct = (1, 4, 8)[g]
            nsl = min(nct, 4)
            nsub = 2 if g == 2 else 1
            WR = WRing(st, 1, 6144, "A")
            ws = WR.next()
            WR.load(ws, [(0, 8, 256, I["w_in"][L][:, 256 * g:256 * g + 256]),
                         (2048, 8, 256, I["w_in"][L][:, 768 + 256 * g:768 + 256 * g + 256]),
                         (4096, 8, 256, I["w_in"][L][:, 1536 + 256 * g:1536 + 256 * g + 256])])
            Wq, Wk, Wv = WR.view(ws, 0, 8, 256), WR.view(ws, 2048, 8, 256), WR.view(ws, 4096, 8, 256)
            wkey = WR.key(ws)
            KT = sb("KT", [128, 2, NTOK], BF16, st)
            V = sb("Vg", [128, NT, 256], BF16, st)
            qkr = sb("qkr", [128, 2, 512], F32, st)
            V32 = sb("V32", [128, 2, 256], F32, st)
            T1 = sb("ropeT1", [128, 512], F32, st)
            T2 = sb("ropeT2", [128, 512], F32, st)
            qT = sb("qT", [128, 2, 2, 128], BF16, st)
            PT = sb("PT", [128, 3, 512], BF16, st)
            Kc = sb("Kc", [128, 2, nsl, 256], BF16, st)
            Vc = sb("Vc", [128, 2, nsl, 256], BF16, st)
            KcT = sb("KcT", [128, 2, nsl, 2, 128], BF16, st)
            PTs = sb("PTs", [128, 2, 32], BF16, st)
            ptc = [0]
            for j in list(range(min(DBG_A_NT, 16))) + [16]:
                s2 = j % 2
                bqk, bv = 0, 2
                for kc in range(8):
                    P.op("pe", lambda e, kc=kc, j=j, bqk=bqk: e.matmul(C.ps[bqk][:, 0:256], lhsT=C.hT[:, kc, j * 128:(j + 1) * 128], rhs=Wq[:, kc, :],
                                                                       start=(kc == 0), stop=(kc == 7)),
                         reads=[("hT", j), wkey], writes=[("ps", bqk)])
                for kc in range(8):
                    P.op("pe", lambda e, kc=kc, j=j, bqk=bqk: e.matmul(C.ps[bqk][:, 256:512], lhsT=C.hT[:, kc, j * 128:(j + 1) * 128], rhs=Wk[:, kc, :],
                                                                       start=(kc == 0), stop=(kc == 7), skip_group_check=True),
                         reads=[("hT", j), wkey], writes=[("ps", bqk)])
                for kc in range(8):
                    P.op("pe", lambda e, kc=kc, j=j: e.matmul(C.ps[bv][:, 0:256], lhsT=C.hT[:, kc, j * 128:(j + 1) * 128], rhs=Wv[:, kc, :],
                                                              start=(kc == 0), stop=(kc == 7)),
                         reads=[("hT", j), wkey], writes=[("ps", bv)])
                rope_ops(C.ps[bqk][:, 0:512], 8, j, qkr[:, s2, :], T1, T2, [("ps", bqk)], ("qkr", s2))
                need_out = (j >= first_out_tile)
                P.op("act", lambda e, j=j: e.activation(out=V[:, j, :], in_=C.ps[bv][:, 0:256], func=AF.Copy),
                     reads=[("ps", bv)], writes=[("V", j)])
                if need_out:
                    P.op("act", lambda e, s2=s2: e.activation(out=V32[:, s2, :], in_=C.ps[bv][:, 0:256], func=AF.Copy),
                         reads=[("ps", bv)], writes=[("V32", s2)])
                    if j < 16:
                        r0 = (j - first_out_tile) * 128
                        P.dma("sp", ok_p[L, r0:r0 + 128, :], qkr[:, s2, 256:512], reads=[("qkr", s2)])
                        P.dma("sp", ov_p[L, r0:r0 + 128, :], V32[:, s2, :], reads=[("V32", s2)])
                    else:
                        for b in range(16):
                            P.dma("sp", ok_s[L, b, w - 8:w, :], qkr[8 * b:8 * b + 8, s2, 256:512], reads=[("qkr", s2)])
                            P.dma("sp", ov_s[L, b, w - 8:w, :], V32[8 * b:8 * b + 8, s2, :], reads=[("V32", s2)])
                for c in range(4):
                    P.op("pe", lambda e, c=c, s2=s2: e.transpose(C.ps[3][:, c * 128:(c + 1) * 128], qkr[:, s2, c * 128:(c + 1) * 128], C.ident_f[:]),
                         reads=[("qkr", s2), "ident_f"], writes=[("ps", 3)])
                P.op("act", lambda e, s2=s2: e.activation(out=qT[:, s2], in_=C.ps[3][:, 0:256].rearrange("p (c t) -> p c t", c=2), func=AF.Copy),
                     reads=[("ps", 3)], writes=[("qT", s2)])
                P.op("act", lambda e, j=j: e.activation(out=KT[:, :, j * 128:(j + 1) * 128], in_=C.ps[3][:, 256:512].rearrange("p (c t) -> p c t", c=2), func=AF.Copy),
                     reads=[("ps", 3)], writes=[("KT", j)])
                if j < 16:
                    if g == 0:
                        kts = ([(j - 1, M_P1)] if j >= 1 else []) + [(j, M_C1)]
                    elif g == 1:
                        kts = ([(j - 4, M_P4)] if j >= 4 else []) + [(kt, M_E4) for kt in range(max(0, j - 3), j)] + [(j, M_C4)]
                    else:
                        kts = [(kt, M_E16) for kt in range(0, j)] + [(j, M_C16)]
                else:
                    kts = [(16, (M_BD0, M_BD1, M_BD2)[g])]
                acc = 6
                ACC = C.ps[acc]
                SPAIR = ((4, 5), (1, 7))

                def emit_S(idx, kt, mid, ps_):
                    pair = SPAIR[idx % 2]
                    for h in range(4):
                        par, c = h % 2, h // 2
                        po = 64 * par
                        P.op("pe", lambda e: e.matmul(
                            C.ps[pair[par]][:, c * 128:(c + 1) * 128], lhsT=KT[po:po + 64, c, kt * 128:(kt + 1) * 128],
                            rhs=qT[po:po + 64, s2, c, :], start=True, stop=True, skip_group_check=True),
                            reads=[("KT", kt), ("qT", s2)], writes=[("ps", pair[par])])
                    for par in range(2):
                        P.op("act", lambda e: e.activation(out=PT[:, ps_, par * 256:par * 256 + 256], in_=C.ps[pair[par]][:, 0:256], func=AF.Exp, scale=0.125),
                             reads=[("ps", pair[par])], writes=[("PT", ps_)])
                    P.op("pool", lambda e: e.tensor_tensor(
                        out=PT[:, ps_, :].rearrange("p (h q) -> p h q", h=4), in0=PT[:, ps_, :].rearrange("p (h q) -> p h q", h=4),
                        in1=C.masks[:, mid, :].unsqueeze(1).broadcast_to([128, 4, 128]), op=ALU.mult),
                        reads=[("PT", ps_), "masks"], writes=[("PT", ps_)])

                def emit_PV(idx, kt, ps_):
                    for h in range(4):
                        par, c = h % 2, h // 2
                        po = 64 * par
                        P.op("pe", lambda e: e.matmul(
                            ACC[po:po + 64, c * 128:(c + 1) * 128], lhsT=V[:, kt, 64 * h:64 * h + 64], rhs=PT[:, ps_, par * 256 + c * 128:par * 256 + c * 128 + 128],
                            start=(idx == 0 and h < 2), stop=False, skip_group_check=True),
                            reads=[("V", kt), ("PT", ps_)], writes=[("ps", acc)])
                    for par in range(2):
                        P.op("pe", lambda e: e.matmul(
                            ACC[64 * par:64 * par + 64, 256:512], lhsT=C.ones_b[:, :], rhs=PT[:, ps_, par * 256:par * 256 + 256],
                            start=False, stop=False, skip_group_check=True),
                            reads=[("PT", ps_), "ones_b"], writes=[("ps", acc)])

                slots = []
                for idx, (kt, mid) in enumerate(kts):
                    ps_ = ptc[0] % 3
                    ptc[0] += 1
                    slots.append(ps_)
                    emit_S(idx, kt, mid, ps_)
                    if idx >= 1:
                        emit_PV(idx - 1, kts[idx - 1][0], slots[idx - 1])
                emit_PV(len(kts) - 1, kts[-1][0], slots[-1])
                if j == 16:
                    step = 0
                    for b in range(DBG_A_NB):
                        for sub in range(nsub):
                            cs = step % 2
                            step += 1
                            if g == 0:
                                ksrc, vsrc = ck_in[L, b].unsqueeze(1), cv_in[L, b].unsqueeze(1)
                                tl = list(range(8))
                            elif g == 1:
                                ksrc = ck_in[L, b].rearrange("(m s) f -> m s f", s=4)
                                vsrc = cv_in[L, b].rearrange("(m s) f -> m s f", s=4)
                                tl = list(range(8))
                            else:
                                ksrc = ck_in[L, b].rearrange("(m s) f -> m s f", s=16)[:, 4 * sub:4 * sub + 4, :]
                                vsrc = cv_in[L, b].rearrange("(m s) f -> m s f", s=16)[:, 4 * sub:4 * sub + 4, :]
                                tl = list(range(4 * sub, 4 * sub + 4))
                            t0, nt_ = tl[0], len(tl)
                            P.dma("pool", Kc[:, cs], ksrc, writes=[("Kc", cs)])
                            P.dma("pool", Vc[:, cs], vsrc, writes=[("Vc", cs)])
                            pv = psb(3)
                            nblk = nsl * 2
                            for ct in range(nsl):
                                for hp in range(2):
                                    bi = ct * 2 + hp
                                    P.op("pe", lambda e, bi=bi, ct=ct, hp=hp, cs=cs, pv=pv: e.transpose(
                                        pv[:, bi * 128:(bi + 1) * 128], Kc[:, cs, ct, hp * 128:(hp + 1) * 128], C.ident_b[:]),
                                        reads=[("Kc", cs), "ident_b"], writes=[("ps", 3)])
                            P.op("act", lambda e, pv=pv, cs=cs, nblk=nblk: e.activation(
                                out=KcT[:, cs].rearrange("p a b m -> p (a b) m"),
                                in_=pv[:, 0:nblk * 128].rearrange("p (a m) -> p a m", a=nblk), func=AF.Copy),
                                reads=[("ps", 3)], writes=[("KcT", cs)])
                            for h in range(4):
                                par, c = h % 2, h // 2
                                po = 64 * par
                                SBk = C.ps[4 + par]
                                if g == 0:
                                    P.op("pe", lambda e, po=po, c=c, cs=cs, b=b, SBk=SBk: e.matmul(
                                        SBk[:, c * 8:c * 8 + 8], lhsT=KcT[po:po + 64, cs, 0, c, :], rhs=qT[po:po + 64, s2, c, 8 * b:8 * b + 8],
                                        start=True, stop=True, skip_group_check=True),
                                        reads=[("KcT", cs), ("qT", s2)], writes=[("ps", 4 + par)])
                                elif g == 1:
                                    for s in range(4):
                                        P.op("pe", lambda e, po=po, c=c, cs=cs, b=b, s=s, SBk=SBk: e.matmul(
                                            SBk[:, c * 8 + s:c * 8 + s + 5:4], lhsT=KcT[po:po + 64, cs, s, c, :],
                                            rhs=qT[po:po + 64, s2, c, 8 * b + s:8 * b + s + 5:4],
                                            start=True, stop=True, skip_group_check=True),
                                            reads=[("KcT", cs), ("qT", s2)], writes=[("ps", 4 + par)])
                                else:
                                    for t in tl:
                                        P.op("pe", lambda e, po=po, c=c, cs=cs, b=b, t=t, t0=t0, SBk=SBk: e.matmul(
                                            SBk[:, c * 8 + t:c * 8 + t + 1], lhsT=KcT[po:po + 64, cs, t - t0, c, :],
                                            rhs=qT[po:po + 64, s2, c, 8 * b + t:8 * b + t + 1],
                                            start=True, stop=True, skip_group_check=True),
                                            reads=[("KcT", cs), ("qT", s2)], writes=[("ps", 4 + par)])
                            PT4 = PTs[:, cs, :].rearrange("p (r c t) -> p r c t", r=2, c=2)
                            for par in range(2):
                                P.op("act", lambda e, par=par, PT4=PT4, t0=t0, nt_=nt_: e.activation(
                                    out=PT4[:, par, :, t0:t0 + nt_], in_=C.ps[4 + par][:, 0:16].rearrange("p (c t) -> p c t", c=2)[:, :, t0:t0 + nt_],
                                    func=AF.Exp, scale=0.125), reads=[("ps", 4 + par)], writes=[("PTs", cs)])
                            PT3 = PTs[:, cs, :].rearrange("p (h t) -> p h t", h=4)
                            if g == 0:
                                P.op("pool", lambda e, PT3=PT3: e.tensor_tensor(
                                    out=PT3, in0=PT3, in1=C.smask[:, 0:8].unsqueeze(1).broadcast_to([128, 4, 8]), op=ALU.mult),
                                    reads=[("PTs", cs), "smask"], writes=[("PTs", cs)])
                            elif g == 1:
                                P.op("pool", lambda e, PT3=PT3: e.tensor_tensor(
                                    out=PT3[:, :, 4:8], in0=PT3[:, :, 4:8],
                                    in1=C.smask[:, 9:10].unsqueeze(1).broadcast_to([128, 4, 4]), op=ALU.mult),
                                    reads=[("PTs", cs), "smask"], writes=[("PTs", cs)])
                            for h in range(4):
                                par, c = h % 2, h // 2
                                po = 64 * par
                                col0 = c * 128 + 8 * b
                                pc0 = par * 16 + c * 8
                                if g == 0:
                                    P.op("pe", lambda e, h=h, po=po, cs=cs, col0=col0, pc0=pc0: e.matmul(
                                        ACC[po:po + 64, col0:col0 + 8], lhsT=Vc[:, cs, 0, 64 * h:64 * h + 64], rhs=PTs[:, cs, pc0:pc0 + 8],
                                        start=False, stop=False, skip_group_check=True),
                                        reads=[("Vc", cs), ("PTs", cs)], writes=[("ps", acc)])
                                elif g == 1:
                                    for s in range(4):
                                        P.op("pe", lambda e, h=h, po=po, cs=cs, col0=col0, pc0=pc0, s=s: e.matmul(
                                            ACC[po:po + 64, col0 + s:col0 + s + 5:4], lhsT=Vc[:, cs, s, 64 * h:64 * h + 64],
                                            rhs=PTs[:, cs, pc0 + s:pc0 + s + 5:4], start=False, stop=False, skip_group_check=True),
                                            reads=[("Vc", cs), ("PTs", cs)], writes=[("ps", acc)])
                                else:
                                    for t in tl:
                                        P.op("pe", lambda e, h=h, po=po, cs=cs, col0=col0, pc0=pc0, t=t, t0=t0: e.matmul(
                                            ACC[po:po + 64, col0 + t:col0 + t + 1], lhsT=Vc[:, cs, t - t0, 64 * h:64 * h + 64],
                                            rhs=PTs[:, cs, pc0 + t:pc0 + t + 1], start=False, stop=False, skip_group_check=True),
                                            reads=[("Vc", cs), ("PTs", cs)], writes=[("ps", acc)])
                            for par in range(2):
                                P.op("pe", lambda e, par=par, cs=cs, b=b, t0=t0, nt_=nt_, PT4=PT4: e.matmul(
                                    ACC[64 * par:64 * par + 64, 256:512].rearrange("p (c q) -> p c q", c=2)[:, :, 8 * b + t0:8 * b + t0 + nt_],
                                    lhsT=C.ones_b[:, :], rhs=PT4[:, par, :, t0:t0 + nt_],
                                    start=False, stop=False, skip_group_check=True),
                                    reads=[("PTs", cs), "ones_b"], writes=[("ps", acc)])
                nv = C.numA[:, :, j * 128:(j + 1) * 128]
                dv = C.denA[:, :, j * 128:(j + 1) * 128]
                pn = ACC[:, 0:256].rearrange("p (c q) -> p c q", c=2)
                pd = ACC[:, 256:512].rearrange("p (c q) -> p c q", c=2)
                if g == 0:
                    P.op("dve", lambda e, nv=nv, pn=pn: e.tensor_copy(out=nv, in_=pn), reads=[("ps", acc)], writes=[("numA", j)])
                    P.op("dve", lambda e, dv=dv, pd=pd: e.tensor_copy(out=dv, in_=pd), reads=[("ps", acc)], writes=[("denA", j)])
                else:
                    P.op("dve", lambda e, nv=nv, pn=pn: e.tensor_tensor(out=nv, in0=pn, in1=nv, op=ALU.add),
                         reads=[("ps", acc), ("numA", j)], writes=[("numA", j)])
                    P.op("dve", lambda e, dv=dv, pd=pd: e.tensor_tensor(out=dv, in0=pd, in1=dv, op=ALU.add),
                         reads=[("ps", acc), ("denA", j)], writes=[("denA", j)])

        def mixer_B(L, st):
            ck_in, cv_in = I["cbk"], I["cbv"]
            ok_p, ov_p, ok_s, ov_s = O["bk_p"], O["bv_p"], O["bk_s"], O["bv_s"]
            WR = WRing(st, 1, 6144, "B")
            ws = WR.next()
            WR.load(ws, [(0, 8, 768, I["w_b"][L])])
            W = WR.view(ws, 0, 8, 768)
            wkey = WR.key(ws)
            KT = sb("KTb", [128, NTOK], BF16, st)
            V = sb("Vb", [128, NT, 128], BF16, st)
            qkr = sb("qkrb", [128, 2, 640], F32, st)
            V32 = sb("V32b", [128, 2, 128], F32, st)
            T1 = sb("ropeT1", [128, 512], F32, st)
            T2 = sb("ropeT2", [128, 512], F32, st)
            qT = sb("qTb", [128, 2, 4, 128], BF16, st)
            PT = sb("PTb", [128, 4, 512], BF16, st)
            Kc = sb("Kcb", [128, 2, 128], BF16, st)
            Vc = sb("Vcb", [128, 2, 128], BF16, st)
            KcT = sb("KcTb", [128, 2, 128], BF16, st)
            PTs = sb("PTsb", [128, 2, 64], BF16, st)
            esink = sb("esink", [128, 4], F32, st)
            dtmp = sb("dtmp", [128, 512], F32, st)
            for kv in range(2):
                P.dma("sp", esink[64 * kv:64 * kv + 64, :], I["sinks"][L, kv].partition_broadcast(64), writes=["esink"])
            P.op("act", lambda e: e.activation(out=esink[:], in_=esink[:], func=AF.Exp), reads=["esink"], writes=["esink"])
            ptc = [0]
            NUM, DEN = C.ps[6], C.ps[7]
            for j in range(DBG_B_NT):
                s2 = j % 2
                bq = (0, 1)[s2]
                for kc in range(8):
                    P.op("pe", lambda e, kc=kc, j=j, bq=bq: e.matmul(C.ps[bq][:, 0:512], lhsT=C.hT[:, kc, j * 128:(j + 1) * 128], rhs=W[:, kc, 0:512],
                                                                     start=(kc == 0), stop=(kc == 7)),
                         reads=[("hT", j), wkey], writes=[("ps", bq)])
                for kc in range(8):
                    P.op("pe", lambda e, kc=kc, j=j: e.matmul(C.ps[2][:, 0:256], lhsT=C.hT[:, kc, j * 128:(j + 1) * 128], rhs=W[:, kc, 512:768],
                                                              start=(kc == 0), stop=(kc == 7)),
                         reads=[("hT", j), wkey], writes=[("ps", 2)])
                rope_ops(C.ps[bq][:, 0:512], 8, j, qkr[:, s2, 0:512], T1, T2, [("ps", bq)], ("qkrb", s2))
                rope_ops(C.ps[2][:, 0:128], 2, j, qkr[:, s2, 512:640], T1, T2, [("ps", 2)], ("qkrb", s2))
                P.op("act", lambda e, j=j: e.activation(out=V[:, j, :], in_=C.ps[2][:, 128:256], func=AF.Copy),
                     reads=[("ps", 2)], writes=[("Vb", j)])
                if j >= 15:
                    P.op("act", lambda e, s2=s2: e.activation(out=V32[:, s2, :], in_=C.ps[2][:, 128:256], func=AF.Copy),
                         reads=[("ps", 2)], writes=[("V32b", s2)])
                    if j == 15:
                        P.dma("sp", ok_p[L], qkr[:, s2, 512:640], reads=[("qkrb", s2)])
                        P.dma("sp", ov_p[L], V32[:, s2, :], reads=[("V32b", s2)])
                    else:
                        for b in range(16):
                            P.dma("sp", ok_s[L, b, 120:128, :], qkr[8 * b:8 * b + 8, s2, 512:640], reads=[("qkrb", s2)])
                            P.dma("sp", ov_s[L, b, 120:128, :], V32[8 * b:8 * b + 8, s2, :], reads=[("V32b", s2)])
                for c in range(4):
                    P.op("pe", lambda e, c=c, s2=s2: e.transpose(C.ps[3][:, c * 128:(c + 1) * 128], qkr[:, s2, c * 128:(c + 1) * 128], C.ident_f[:]),
                         reads=[("qkrb", s2), "ident_f"], writes=[("ps", 3)])
                P.op("act", lambda e, s2=s2: e.activation(out=qT[:, s2], in_=C.ps[3][:, 0:512].rearrange("p (c t) -> p c t", c=4), func=AF.Copy),
                     reads=[("ps", 3)], writes=[("qTb", s2)])
                P.op("pe", lambda e, s2=s2: e.transpose(C.ps[3][:, 0:128], qkr[:, s2, 512:640], C.ident_f[:]),
                     reads=[("qkrb", s2), "ident_f"], writes=[("ps", 3)])
                P.op("act", lambda e, j=j: e.activation(out=KT[:, j * 128:(j + 1) * 128], in_=C.ps[3][:, 0:128], func=AF.Copy),
                     reads=[("ps", 3)], writes=[("KTb", j)])
                if j < 16:
                    kts = ([(j - 1, M_P1)] if j >= 1 else []) + [(j, M_C1)]
                else:
                    kts = [(16, M_BD0)]
                for ki, (kt, mid) in enumerate(kts):
                    pss = []
                    for kv in range(2):
                        sbk = 4 + kv
                        ps_ = ptc[0] % 4
                        ptc[0] += 1
                        pss.append(ps_)
                        P.op("pe", lambda e: e.matmul(
                            C.ps[sbk][:, :], lhsT=KT[64 * kv:64 * kv + 64, kt * 128:(kt + 1) * 128],
                            rhs=qT[64 * kv:64 * kv + 64, s2, :, :], start=True, stop=True),
                            reads=[("KTb", kt), ("qTb", s2)], writes=[("ps", sbk)])
                        P.op("act", lambda e: e.activation(out=PT[:, ps_, :], in_=C.ps[sbk][:, :], func=AF.Exp, scale=0.125),
                             reads=[("ps", sbk)], writes=[("PTb", ps_)])
                        P.op("pool", lambda e: e.tensor_tensor(
                            out=PT[:, ps_, :].rearrange("p (h q) -> p h q", h=4), in0=PT[:, ps_, :].rearrange("p (h q) -> p h q", h=4),
                            in1=C.masks[:, mid, :].unsqueeze(1).broadcast_to([128, 4, 128]), op=ALU.mult),
                            reads=[("PTb", ps_), "masks"], writes=[("PTb", ps_)])
                    for kv in range(2):
                        ps_ = pss[kv]
                        P.op("pe", lambda e: e.matmul(
                            NUM[64 * kv:64 * kv + 64, :], lhsT=V[:, kt, 64 * kv:64 * kv + 64], rhs=PT[:, ps_, :],
                            start=(ki == 0), stop=False, skip_group_check=True),
                            reads=[("Vb", kt), ("PTb", ps_)], writes=[("ps", 6)])
                        P.op("pe", lambda e: e.matmul(
                            DEN[64 * kv:64 * kv + 64, :], lhsT=C.ones_b[:, :], rhs=PT[:, ps_, :],
                            start=(ki == 0), stop=False, skip_group_check=True),
                            reads=[("PTb", ps_), "ones_b"], writes=[("ps", 7)])
                if j == 16:
                    for b in range(16):
                        cs = b % 2
                        P.dma("pool", Kc[:, cs, :], ck_in[L, b], writes=[("Kcb", cs)])
                        P.dma("pool", Vc[:, cs, :], cv_in[L, b], writes=[("Vcb", cs)])
                        pv = psb(3)
                        P.op("pe", lambda e, cs=cs, pv=pv: e.transpose(pv[:, 0:128], Kc[:, cs, :], C.ident_b[:]),
                             reads=[("Kcb", cs), "ident_b"], writes=[("ps", 3)])
                        P.op("act", lambda e, cs=cs, pv=pv: e.activation(out=KcT[:, cs, :], in_=pv[:, 0:128], func=AF.Copy),
                             reads=[("ps", 3)], writes=[("KcTb", cs)])
                        for kv in range(2):
                            P.op("pe", lambda e, kv=kv, cs=cs, b=b: e.matmul(
                                C.ps[4 + kv][:, 0:32].rearrange("p (c t) -> p c t", c=4), lhsT=KcT[64 * kv:64 * kv + 64, cs, :],
                                rhs=qT[64 * kv:64 * kv + 64, s2, :, 8 * b:8 * b + 8], start=True, stop=True, skip_group_check=True),
                                reads=[("KcTb", cs), ("qTb", s2)], writes=[("ps", 4 + kv)])
                            P.op("act", lambda e, cs=cs, kv=kv: e.activation(out=PTs[:, cs, kv * 32:kv * 32 + 32], in_=C.ps[4 + kv][:, 0:32], func=AF.Exp, scale=0.125),
                                 reads=[("ps", 4 + kv)], writes=[("PTsb", cs)])
                        P.op("pool", lambda e, cs=cs: e.tensor_tensor(
                            out=PTs[:, cs, :].rearrange("p (h t) -> p h t", h=8), in0=PTs[:, cs, :].rearrange("p (h t) -> p h t", h=8),
                            in1=C.smask[:, 0:8].unsqueeze(1).broadcast_to([128, 8, 8]), op=ALU.mult),
                            reads=[("PTsb", cs), "smask"], writes=[("PTsb", cs)])
                        for kv in range(2):
                            P.op("pe", lambda e, kv=kv, cs=cs, b=b: e.matmul(
                                NUM[64 * kv:64 * kv + 64, :].rearrange("p (c q) -> p c q", c=4)[:, :, 8 * b:8 * b + 8],
                                lhsT=Vc[:, cs, 64 * kv:64 * kv + 64], rhs=PTs[:, cs, kv * 32:kv * 32 + 32].rearrange("p (c t) -> p c t", c=4),
                                start=False, stop=False, skip_group_check=True),
                                reads=[("Vcb", cs), ("PTsb", cs)], writes=[("ps", 6)])
                            P.op("pe", lambda e, kv=kv, cs=cs, b=b: e.matmul(
                                DEN[64 * kv:64 * kv + 64, :].rearrange("p (c q) -> p c q", c=4)[:, :, 8 * b:8 * b + 8],
                                lhsT=C.ones_b[:, :], rhs=PTs[:, cs, kv * 32:kv * 32 + 32].rearrange("p (c t) -> p c t", c=4),
                                start=False, stop=False, skip_group_check=True),
                                reads=[("PTsb", cs), "ones_b"], writes=[("ps", 7)])
                P.op("dve", lambda e: e.tensor_tensor(out=dtmp[:].rearrange("p (c q) -> p c q", c=4), in0=DEN[:, :].rearrange("p (c q) -> p c q", c=4),
                                                      in1=esink[:, :].unsqueeze(2).broadcast_to([128, 4, 128]), op=ALU.add),
                     reads=[("ps", 7), "esink"], writes=["dtmp"])
                P.op("dve", lambda e: e.reciprocal(out=dtmp[:], in_=dtmp[:]), reads=["dtmp"], writes=["dtmp"])
                P.op("dve", lambda e, j=j: e.tensor_tensor(out=C.obT[:, :, j * 128:(j + 1) * 128], in0=NUM[:, :].rearrange("p (c q) -> p c q", c=4),
                                                           in1=dtmp[:].rearrange("p (c q) -> p c q", c=4), op=ALU.mult),
                     reads=[("ps", 6), "dtmp"], writes=[("obT", j)])

        TG = [(0, 512), (512, 1024), (1024, 1536), (1536, 2048), (2048, 2176)]

        def tg_tiles(c0, c1):
            return list(range(c0 // 128, c1 // 128))

        def mixer_C(L, st):
            WR = WRing(st, 1, 6144, "C")
            ws = WR.next()
            base = 3072
            WR.load(ws, [(0, 8, 256, I["w_in"][L][:, base:base + 256]),
                         (2048, 8, 256, I["w_in"][L][:, base + 256:base + 512]),
                         (4096, 8, 256, I["w_in"][L][:, base + 512:base + 768])])
            Wh, Wb, Wc = WR.view(ws, 0, 8, 256), WR.view(ws, 2048, 8, 256), WR.view(ws, 4096, 8, 256)
            wkey = WR.key(ws)
            cw = sb("cw", [128, 2, 3], F32, st)
            for k in range(3):
                for ch in range(2):
                    P.dma("sp", cw[:, ch, k:k + 1], I["conv_w"][L, k, 128 * ch:128 * ch + 128].rearrange("(p o) -> p o", o=1), writes=["cw"])
            ZC = sb("ZC", [128, 2, 2 + SEQ], F32, st)
            ZCs = sb("ZCs", [128, 2, 16, 10], F32, st)
            hcS = sb("hcS", [128, 2, 512], F32, st)
            ycv = sb("ycv", [128, 2, 512], F32, st)
            st32 = sb("st32", [32, 256], F32, st)
            cstp = sb("cstp", [2, 256], F32, st)
            csts = sb("csts", [32, 256], F32, st)
            ztmp = sb("ztmp", [128, 32], F32, st)
            P.op("dve", lambda e: e.memset(ZC[:, :, 0:2], 0.0), writes=["ZC0"])
            P.dma("sp", st32[:], I["sconv"][L], writes=["st32"])
            for ch in range(2):
                P.op("pe", lambda e, ch=ch: e.transpose(C.ps[3][:, 0:32], st32[:, ch * 128:(ch + 1) * 128], C.ident_f[0:32, 0:32]),
                     reads=["st32", "ident_f"], writes=[("ps", 3)])
                P.op("dve", lambda e, ch=ch: e.tensor_copy(out=ZCs[:, ch, :, 0:2], in_=C.ps[3][:, 0:32].rearrange("p (b r) -> p b r", b=16)),
                     reads=[("ps", 3)], writes=["ZCs0"])
            it = 0
            for (c0, c1) in TG:
                n = c1 - c0
                tiles = [("hT", j) for j in tg_tiles(c0, c1)]
                for ch in range(2):
                    s2 = it % 2
                    it += 1
                    banks = (0, 1, 2) if s2 == 0 else (4, 5, 6)
                    for bi, Wx in enumerate((Wh, Wb, Wc)):
                        for kc in range(8):
                            P.op("pe", lambda e, kc=kc, Wx=Wx, b_=banks[bi], ch=ch: e.matmul(
                                C.ps[b_][:, 0:n], lhsT=Wx[:, kc, ch * 128:(ch + 1) * 128], rhs=C.hT[:, kc, c0:c1],
                                start=(kc == 0), stop=(kc == 7)), reads=tiles + [wkey], writes=[("ps", banks[bi])])
                    P.op("act", lambda e, s2=s2, b_=banks[0]: e.activation(out=hcS[:, s2, 0:n], in_=C.ps[b_][:, 0:n], func=AF.Copy),
                         reads=[("ps", banks[0])], writes=[("hcS", s2)])
                    if c0 < SEQ:
                        zc = ZC[:, ch, 2 + c0:2 + c1]
                        z1 = ZC[:, ch, 1 + c0:1 + c1]
                        z0 = ZC[:, ch, c0:c1]
                        yv = ycv[:, s2, 0:n]
                        hv = hcS[:, s2, 0:n]
                        pc = C.ps[banks[2]][:, 0:n]
                        pb = C.ps[banks[1]][:, 0:n]
                        ov = C.ocT[:, ch, c0:c1]
                    else:
                        zc = ZCs[:, ch, :, 2:10]
                        z1 = ZCs[:, ch, :, 1:9]
                        z0 = ZCs[:, ch, :, 0:8]
                        yv = ycv[:, s2, 0:128].rearrange("p (b t) -> p b t", b=16)
                        hv = hcS[:, s2, 0:128].rearrange("p (b t) -> p b t", b=16)
                        pc = C.ps[banks[2]][:, 0:128].rearrange("p (b t) -> p b t", b=16)
                        pb = C.ps[banks[1]][:, 0:128].rearrange("p (b t) -> p b t", b=16)
                        ov = C.ocT[:, ch, c0:c1].rearrange("p (b t) -> p b t", b=16)
                    zk = ("ZC", ch)
                    P.op("dve", lambda e, zc=zc, pc=pc, hv=hv: e.tensor_tensor(out=zc, in0=pc, in1=hv, op=ALU.mult),
                         reads=[("ps", banks[2]), ("hcS", s2)], writes=[zk])
                    P.op("dve", lambda e, yv=yv, zc=zc, ch=ch: e.tensor_scalar(out=yv, in0=zc, scalar1=cw[:, ch, 2:3], scalar2=None, op0=ALU.mult),
                         reads=[zk, "cw"], writes=[("ycv", s2)])
                    P.op("dve", lambda e, yv=yv, z1=z1, ch=ch: e.scalar_tensor_tensor(out=yv, in0=z1, scalar=cw[:, ch, 1:2], in1=yv, op0=ALU.mult, op1=ALU.add),
                         reads=[zk, "cw", "ZC0", "ZCs0"], writes=[("ycv", s2)])
                    P.op("dve", lambda e, yv=yv, z0=z0, ch=ch: e.scalar_tensor_tensor(out=yv, in0=z0, scalar=cw[:, ch, 0:1], in1=yv, op0=ALU.mult, op1=ALU.add),
                         reads=[zk, "cw", "ZC0", "ZCs0"], writes=[("ycv", s2)])
                    P.op("dve", lambda e, ov=ov, pb=pb, yv=yv: e.tensor_tensor(out=ov, in0=pb, in1=yv, op=ALU.mult),
                         reads=[("ps", banks[1]), ("ycv", s2)], writes=[("ocT", ch, c0)])
            for ch in range(2):
                P.op("pe", lambda e, ch=ch: e.transpose(C.ps[3][0:2, 0:128], ZC[:, ch, SEQ:SEQ + 2], C.ident_f[:]),
                     reads=[("ZC", ch), "ident_f"], writes=[("ps", 3)])
                P.op("dve", lambda e, ch=ch: e.tensor_copy(out=cstp[:, ch * 128:(ch + 1) * 128], in_=C.ps[3][0:2, 0:128]),
                     reads=[("ps", 3)], writes=["cstp"])
                P.op("dve", lambda e, ch=ch: e.tensor_copy(out=ztmp[:].rearrange("p (b r) -> p b r", b=16), in_=ZCs[:, ch, :, 8:10]),
                     reads=[("ZC", ch)], writes=["ztmp"])
                P.op("pe", lambda e, ch=ch: e.transpose(C.ps[3][0:32, 128:256], ztmp[:], C.ident_f[:]),
                     reads=["ztmp", "ident_f"], writes=[("ps", 3)])
                P.op("dve", lambda e, ch=ch: e.tensor_copy(out=csts[:, ch * 128:(ch + 1) * 128], in_=C.ps[3][0:32, 128:256]),
                     reads=[("ps", 3)], writes=["csts"])
            P.dma("sp", O["conv_p"][L], cstp[:], reads=["cstp"])
            P.dma("sp", O["conv_s"][L], csts[:], reads=["csts"])

        TWO_PI = 6.283185307179586
        MAGIC = 12582912.0

        def range_sincos(src, out_sin, out_cos, tmp, keyp, reads):
            P.op("dve", lambda e: e.tensor_scalar(out=tmp, in0=src, scalar1=1.0 / TWO_PI, scalar2=MAGIC, op0=ALU.mult, op1=ALU.add),
                 reads=reads, writes=[keyp + "t"])
            P.op("dve", lambda e: e.tensor_scalar(out=tmp, in0=tmp, scalar1=-MAGIC, scalar2=None, op0=ALU.add), reads=[keyp + "t"], writes=[keyp + "t"])
            P.op("dve", lambda e: e.scalar_tensor_tensor(out=tmp, in0=tmp, scalar=-TWO_PI, in1=src, op0=ALU.mult, op1=ALU.add),
                 reads=[keyp + "t"] + reads, writes=[keyp + "t"])
            P.op("dve", lambda e: e.tensor_scalar(out=tmp, in0=tmp, scalar1=3.141592, scalar2=-3.141592, op0=ALU.min, op1=ALU.max),
                 reads=[keyp + "t"], writes=[keyp + "t"])
            P.op("act", lambda e: e.activation(out=out_sin, in_=tmp, func=AF.Sin), reads=[keyp + "t"], writes=[keyp + "s"])
            P.op("dve", lambda e: e.scalar_tensor_tensor(out=tmp, in0=tmp, scalar=-1.0, in1=tmp, op0=ALU.mult, op1=ALU.max),
                 reads=[keyp + "t", keyp + "s"], writes=[keyp + "t"])
            P.op("act", lambda e: e.activation(out=out_cos, in_=tmp, func=AF.Sin, scale=-1.0, bias=C.halfpi[:, 0:1]),
                 reads=[keyp + "t", "halfpi"], writes=[keyp + "c"])

        def mixer_D(L, st):
            WR = WRing(st, 1, 3072, "D")
            ws = WR.next()
            WR.load(ws, [(0, 8, 256, I["w_in"][L][:, 3840:4096]), (2048, 2, 512, I["w_d_glu"][L])])
            Wu, Wg = WR.view(ws, 0, 8, 256), WR.view(ws, 2048, 2, 512)
            wkey = WR.key(ws)
            udT = C.odT
            ygT = sb("ygT", [128, 2, NTOK], BF16, st)
            it = 0
            for (c0, c1) in TG:
                n = c1 - c0
                tiles = [("hT", j) for j in tg_tiles(c0, c1)]
                for hf in range(2):
                    bk = it % 2
                    it += 1
                    for kc in range(8):
                        P.op("pe", lambda e, kc=kc, hf=hf, bk=bk: e.matmul(C.ps[bk][:, 0:n], lhsT=Wu[:, kc, hf * 128:(hf + 1) * 128], rhs=C.hT[:, kc, c0:c1],
                                                                          start=(kc == 0), stop=(kc == 7)), reads=tiles + [wkey], writes=[("ps", bk)])
                    P.op("act", lambda e, hf=hf, bk=bk: e.activation(out=udT[:, hf, c0:c1], in_=C.ps[bk][:, 0:n], func=AF.Copy),
                         reads=[("ps", bk)], writes=["udT"])
            ps8 = sb("ps8", [8, 3, 128], F32, st)
            ldt = sb("ldt", [8, 2], F32, st)
            LP = sb("LP", [128, 3, 8], F32, st)
            P.dma("sp", ps8[:, 0, :], I["lam_re"][L], writes=["ps8"])
            P.dma("sp", ps8[:, 1, :], I["lam_im"][L], writes=["ps8"])
            P.dma("sp", ldt[:], I["log_dt"][L], writes=["ldt"])
            P.op("dve", lambda e: e.tensor_copy(out=ps8[:, 2, :].rearrange("p (g n) -> p g n", g=2), in_=ldt[:, :].unsqueeze(2).broadcast_to([8, 2, 64])),
                 reads=["ldt"], writes=["ps8"])
            for k in range(3):
                P.op("pe", lambda e, k=k: e.transpose(C.ps[3][:, 8 * k:8 * k + 8], ps8[:, k, :], C.ident_f[0:8, 0:8]),
                     reads=["ps8", "ident_f"], writes=[("ps", 3)])
            P.op("dve", lambda e: e.tensor_copy(out=LP[:].rearrange("p k a -> p (k a)"), in_=C.ps[3][:, 0:24]), reads=[("ps", 3)], writes=["LP"])
            dtv = sb("dtv", [128, 8], F32, st)
            arv = sb("arv", [128, 8], F32, st)
            thv = sb("thv", [128, 8], F32, st)
            P.op("act", lambda e: e.activation(out=dtv[:], in_=LP[:, 2, :], func=AF.Exp), reads=["LP"], writes=["dtv"])
            P.op("dve", lambda e: e.tensor_tensor(out=arv[:], in0=LP[:, 0, :], in1=dtv[:], op=ALU.mult), reads=["LP", "dtv"], writes=["arv"])
            P.op("dve", lambda e: e.tensor_tensor(out=thv[:], in0=LP[:, 1, :], in1=dtv[:], op=ALU.mult), reads=["LP", "dtv"], writes=["thv"])
            kar = sb("kar", [128, 8, 9], F32, st)
            kth = sb("kth", [128, 8, 9], F32, st)
            ktm = sb("ktm", [128, 8, 9], F32, st)
            ksn = sb("ksn", [128, 8, 9], F32, st)
            kcs = sb("kcs", [128, 8, 9], F32, st)
            APW = sb("APW", [128, 2, 8, 9], F32, st)
            kiota = C.iota[:, 0:9].unsqueeze(1).broadcast_to([128, 8, 9])
            P.op("dve", lambda e: e.tensor_tensor(out=kar[:], in0=arv[:, :].unsqueeze(2).broadcast_to([128, 8, 9]), in1=kiota, op=ALU.mult),
                 reads=["arv", "iota"], writes=["kar"])
            P.op("dve", lambda e: e.tensor_tensor(out=kth[:], in0=thv[:, :].unsqueeze(2).broadcast_to([128, 8, 9]), in1=kiota, op=ALU.mult),
                 reads=["thv", "iota"], writes=["kth"])
            P.op("act", lambda e: e.activation(out=kar[:], in_=kar[:], func=AF.Exp), reads=["kar"], writes=["kar"])
            range_sincos(kth[:], ksn[:], kcs[:], ktm[:], "kk", ["kth"])
            P.op("dve", lambda e: e.tensor_tensor(out=APW[:, 0], in0=kar[:], in1=kcs[:], op=ALU.mult), reads=["kar", "kkc"], writes=["APW"])
            P.op("dve", lambda e: e.tensor_tensor(out=APW[:, 1], in0=kar[:], in1=ksn[:], op=ALU.mult), reads=["kar", "kks"], writes=["APW"])
            dump("ssm_APW", APW[:], [128, 2, 8, 9], ["APW"])
            dump("ssm_udT", udT[:], [128, 2, NTOK], ["udT"])
            na8i = sb("na8i", [128, 8], F32, st)
            phi = sb("phi", [128, 8], F32, st)
            P.op("dve", lambda e: e.tensor_scalar(out=na8i[:], in0=APW[:, 1, :, 8], scalar1=-1.0, scalar2=None, op0=ALU.mult), reads=["APW"], writes=["na8i"])
            P.op("dve", lambda e: e.tensor_scalar(out=phi[:], in0=thv[:], scalar1=8.0, scalar2=None, op0=ALU.mult), reads=["thv"], writes=["phi"])
            cf = sb("cf", [128, 6, 8], F32, st)
            lr, li = LP[:, 0, :], LP[:, 1, :]
            P.op("dve", lambda e: e.tensor_scalar(out=cf[:, 0], in0=APW[:, 0, :, 1], scalar1=-1.0, scalar2=None, op0=ALU.add), reads=["APW"], writes=["cf"])
            P.op("dve", lambda e: e.tensor_copy(out=cf[:, 1], in_=APW[:, 1, :, 1]), reads=["APW"], writes=["cf"])
            P.op("dve", lambda e: e.tensor_tensor(out=cf[:, 2], in0=lr, in1=lr, op=ALU.mult), reads=["LP"], writes=["cf"])
            P.op("dve", lambda e: e.tensor_tensor(out=cf[:, 5], in0=li, in1=li, op=ALU.mult), reads=["LP"], writes=["cf"])
            P.op("dve", lambda e: e.tensor_tensor(out=cf[:, 2], in0=cf[:, 2], in1=cf[:, 5], op=ALU.add), writes=["cf"])
            P.op("dve", lambda e: e.reciprocal(out=cf[:, 2], in_=cf[:, 2]), writes=["cf"])
            P.op("dve", lambda e: e.tensor_tensor(out=cf[:, 3], in0=cf[:, 0], in1=lr, op=ALU.mult), writes=["cf"])
            P.op("dve", lambda e: e.tensor_tensor(out=cf[:, 5], in0=cf[:, 1], in1=li, op=ALU.mult), writes=["cf"])
            P.op("dve", lambda e: e.tensor_tensor(out=cf[:, 3], in0=cf[:, 3], in1=cf[:, 5], op=ALU.add), writes=["cf"])
            P.op("dve", lambda e: e.tensor_tensor(out=cf[:, 3], in0=cf[:, 3], in1=cf[:, 2], op=ALU.mult), writes=["cf"])
            P.op("dve", lambda e: e.tensor_tensor(out=cf[:, 4], in0=cf[:, 1], in1=lr, op=ALU.mult), writes=["cf"])
            P.op("dve", lambda e: e.tensor_tensor(out=cf[:, 5], in0=cf[:, 0], in1=li, op=ALU.mult), writes=["cf"])
            P.op("dve", lambda e: e.tensor_tensor(out=cf[:, 4], in0=cf[:, 4], in1=cf[:, 5], op=ALU.subtract), writes=["cf"])
            P.op("dve", lambda e: e.tensor_tensor(out=cf[:, 4], in0=cf[:, 4], in1=cf[:, 2], op=ALU.mult), writes=["cf"])
            Bm = sb("Bm", [128, 2, 8, 16], F32, st)
            bb = sb("bb", [128, 2, 8, 16], F32, st)
            btmp = sb("btmp", [128, 8, 16], F32, st)
            P.dma("sp", Bm[:, 0], I["b_re"][L].rearrange("(a p) j -> p a j", p=128), writes=["Bm"])
            P.dma("sp", Bm[:, 1], I["b_im"][L].rearrange("(a p) j -> p a j", p=128), writes=["Bm"])
            cre = cf[:, 3].unsqueeze(2).broadcast_to([128, 8, 16])
            cim = cf[:, 4].unsqueeze(2).broadcast_to([128, 8, 16])
            P.op("dve", lambda e: e.tensor_tensor(out=bb[:, 0], in0=Bm[:, 0], in1=cre, op=ALU.mult), reads=["Bm", "cf"], writes=["bb"])
            P.op("dve", lambda e: e.tensor_tensor(out=btmp[:], in0=Bm[:, 1], in1=cim, op=ALU.mult), reads=["Bm", "cf"], writes=["btmp"])
            P.op("dve", lambda e: e.tensor_tensor(out=bb[:, 0], in0=bb[:, 0], in1=btmp[:], op=ALU.subtract), writes=["bb"])
            P.op("dve", lambda e: e.tensor_tensor(out=bb[:, 1], in0=Bm[:, 1], in1=cre, op=ALU.mult), reads=["Bm", "cf"], writes=["bb"])
            P.op("dve", lambda e: e.tensor_tensor(out=btmp[:], in0=Bm[:, 0], in1=cim, op=ALU.mult), writes=["btmp"])
            P.op("dve", lambda e: e.tensor_tensor(out=bb[:, 1], in0=bb[:, 1], in1=btmp[:], op=ALU.add), writes=["bb"])
            dump("ssm_bb", bb[:], [128, 2, 8, 16], ["bb"])
            dump("ssm_cf", cf[:], [128, 6, 8], ["cf", "bb"])
            dump("ssm_Bm", Bm[:], [128, 2, 8, 16], ["Bm", "bb"])
            dump("ssm_LP", LP[:], [128, 3, 8], ["LP", "bb"])
            Cst = sb("Cst", [128, 2, 2, 64], F32, st)
            CT = sb("CT", [128, 2, 8, 16], F32, st)
            P.dma("sp", Cst[:, 0], I["c_re"][L].rearrange("(h p) n -> p h n", p=128), writes=["Cst"])
            P.dma("sp", Cst[:, 1], I["c_im"][L].rearrange("(h p) n -> p h n", p=128), writes=["Cst"])
            for ri in range(2):
                for hf in range(2):
                    P.op("pe", lambda e, ri=ri, hf=hf: e.transpose(C.ps[3][0:64, 256 * ri + 128 * hf:256 * ri + 128 * hf + 128], Cst[:, ri, hf, :], C.ident_f[:]),
                         reads=["Cst", "ident_f"], writes=[("ps", 3)])
            for ri in range(2):
                for hf in range(2):
                    for g2 in range(2):
                        src = C.ps[3][0:64, 256 * ri + 128 * hf:256 * ri + 128 * hf + 128].rearrange("p (a g i) -> p a g i", a=4, g=2)[:, :, g2, :]
                        P.op("dve", lambda e, ri=ri, hf=hf, g2=g2, src=src: e.tensor_copy(out=CT[64 * g2:64 * g2 + 64, ri, 4 * hf:4 * hf + 4, :], in_=src),
                             reads=[("ps", 3)], writes=["CT"])
            dump("ssm_CT", CT[:], [128, 2, 8, 16], ["CT"])
            dcol = sb("dcol", [128, 2], F32, st)
            for hf in range(2):
                P.dma("sp", dcol[:, hf:hf + 1], I["ssm_d"][L, hf].rearrange("(p o) -> p o", o=1), writes=["dcol"])
            x0s = sb("x0s", [16, 1024], F32, st)
            X0 = sb("X0", [128, 2, 8, 16], F32, st)
            for ri in range(2):
                P.dma("sp", x0s[:, :], I["sdre" if ri == 0 else "sdim"][L], writes=["x0s"])
                for pr in range(8):
                    P.op("pe", lambda e, ri=ri, pr=pr: e.transpose(C.ps[2][:, (ri * 8 + pr) * 16:(ri * 8 + pr) * 16 + 16], x0s[:, pr * 128:(pr + 1) * 128], C.ident_f[0:16, 0:16]),
                         reads=["x0s", "ident_f"], writes=[("ps", 2)])
            P.op("dve", lambda e: e.tensor_copy(out=X0[:].rearrange("p r a b -> p (r a b)"), in_=C.ps[2][:, 0:256]), reads=[("ps", 2)], writes=["X0"])
            Mx = sb("Mx", [128, 8, 128], BF16, st)
            CAx = sb("CAx", [128, 4, 9, 2, 64], BF16, st)
            Xprev = sb("Xprev", [128, 2, 4, 272], BF16, st)
            XF = sb("XF", [128, 2, 8], F32, st)
            XFs = sb("XFs", [128, 2, 8, 16], F32, st)
            P.op("pool", lambda e: e.memset(Xprev[:, :, :, 0:1], 0.0), writes=["Xprev0"])
            BA = sb("BA", [128, 2, 8, 16], F32, st)
            BAx = sb("BAx", [128, 2, 8, 2, 64], BF16, st)
            SMt = sb("SMt", [128, 16, 128], BF16, st)
            CA = sb("CA", [128, 2, 9, 16], F32, st)
            tmpa = sb("tmpa", [128, 9, 16], F32, st)
            EC = sb("EC", [128, 256], F32, st)
            ES = sb("ES", [128, 256], F32, st)
            ang = sb("ang", [128, 256], F32, st)
            atm = sb("atm", [128, 256], F32, st)
            SR = sb("SR", [128, 2, 256], F32, st)
            Wt = sb("Wt", [128, 2, 256], F32, st)
            XS = sb("XS", [128, 2, 272], F32, st)
            t256 = sb("t256", [128, 256], F32, st)
            P.op("pool", lambda e: e.memset(BAx[:], 0.0), writes=["BAx0"])
            ity = 0
            for pr in range(8):
                hf, pl = pr // 4, pr % 4
                if pl == 0:
                    ykeys = [("ps", 4), ("ps", 5)]
                    P.op("pool", lambda e: e.memset(Mx[:], 0.0), writes=["Mx", "Mxd"] + [("Mxp", k) for k in range(4)])
                    P.op("pool", lambda e: e.memset(CAx[:], 0.0), writes=["CAx"] + [("CAx", k) for k in range(4)])
                quad, q = pl // 2, pl % 2
                bs = pr % 2
                po = 64 * quad
                apr = APW[:, 0, pr, 0:8].unsqueeze(2).broadcast_to([128, 8, 16])
                api = APW[:, 1, pr, 0:8].unsqueeze(2).broadcast_to([128, 8, 16])
                bbr = bb[:, 0, pr, :].unsqueeze(1).broadcast_to([128, 8, 16])
                bbi = bb[:, 1, pr, :].unsqueeze(1).broadcast_to([128, 8, 16])
                t8 = tmpa[:, 0:8, :]
                P.op("dve", lambda e, apr=apr, bbr=bbr: e.tensor_tensor(out=BA[:, 0], in0=apr, in1=bbr, op=ALU.mult), reads=["APW", "bb"], writes=["BA"])
                P.op("dve", lambda e, api=api, bbi=bbi, t8=t8: e.tensor_tensor(out=t8, in0=api, in1=bbi, op=ALU.mult), reads=["APW", "bb"], writes=["tmpa"])
                P.op("dve", lambda e, t8=t8: e.tensor_tensor(out=BA[:, 0], in0=BA[:, 0], in1=t8, op=ALU.subtract), writes=["BA"])
                P.op("dve", lambda e, apr=apr, bbi=bbi: e.tensor_tensor(out=BA[:, 1], in0=apr, in1=bbi, op=ALU.mult), reads=["APW", "bb"], writes=["BA"])
                P.op("dve", lambda e, api=api, bbr=bbr, t8=t8: e.tensor_tensor(out=t8, in0=api, in1=bbr, op=ALU.mult), writes=["tmpa"])
                P.op("dve", lambda e, t8=t8: e.tensor_tensor(out=BA[:, 1], in0=BA[:, 1], in1=t8, op=ALU.add), writes=["BA"])
                for ri in range(2):
                    for g2 in range(2):
                        P.op("dve", lambda e, ri=ri, g2=g2, bs=bs, q=q: e.tensor_scalar(
                            out=BAx[:, bs, :, ri, 32 * q + 16 * g2:32 * q + 16 * g2 + 16], in0=BA[:, ri], scalar1=C.gmask[:, g2:g2 + 1], scalar2=None, op0=ALU.mult),
                            reads=["BA", "gmask", "BAx0", ("BAx", bs)], writes=[("BAx", bs)])
                for hb in range(2):
                    pv = psb(4 + hb)
                    for k8 in range(8):
                        idx = hb * 8 + k8
                        s_, ri = idx // 2, idx % 2
                        P.op("pe", lambda e, pv=pv, k8=k8, s_=s_, ri=ri, bs=bs, po=po: e.transpose(
                            pv[po:po + 64, k8 * 128:(k8 + 1) * 128], BAx[:, bs, s_, ri, :], C.ident_b[:]),
                            reads=[("BAx", bs), "ident_b"], writes=[("ps", 4 + hb)])
                    P.op("act", lambda e, pv=pv, hb=hb, bs=bs, po=po: e.activation(
                        out=SMt[po:po + 64, hb * 8:hb * 8 + 8, :], in_=pv[po:po + 64, :].rearrange("p (a m) -> p a m", a=8), func=AF.Copy),
                        reads=[("ps", 4 + hb)], writes=["SMt"])
                udv = udT[:, hf, :].rearrange("p (c s) -> p c s", s=8)
                for ri in range(2):
                    for s in range(8):
                        P.op("pe", lambda e, ri=ri, s=s, bs=bs, po=po, udv=udv: e.matmul(
                            C.ps[ri][:, 0:272], lhsT=SMt[po:po + 64, (7 - s) * 2 + ri, :], rhs=udv[po:po + 64, :, s],
                            start=(s == 0), stop=(s == 7)), reads=["SMt", "udT"], writes=[("ps", ri)])
                P.op("dve", lambda e, pr=pr: e.tensor_scalar(out=ang[:], in0=C.iota[:, 0:256], scalar1=phi[:, pr:pr + 1], scalar2=None, op0=ALU.mult),
                     reads=["iota", "phi"], writes=["ang"])
                range_sincos(ang[:], ES[:], EC[:], atm[:], "rot", ["ang"])
                Sre, Sim = C.ps[0][:, 0:256], C.ps[1][:, 0:256]
                P.op("dve", lambda e: e.tensor_tensor(out=SR[:, 0], in0=Sre, in1=EC[:], op=ALU.mult), reads=[("ps", 0), "rotc"], writes=["SR"])
                P.op("dve", lambda e: e.tensor_tensor(out=t256[:], in0=Sim, in1=ES[:], op=ALU.mult), reads=[("ps", 1), "rots"], writes=["t256"])
                P.op("dve", lambda e: e.tensor_tensor(out=SR[:, 0], in0=SR[:, 0], in1=t256[:], op=ALU.add), writes=["SR"])
                P.op("dve", lambda e: e.tensor_tensor(out=SR[:, 1], in0=Sim, in1=EC[:], op=ALU.mult), reads=[("ps", 1), "rotc"], writes=["SR"])
                P.op("dve", lambda e: e.tensor_tensor(out=t256[:], in0=Sre, in1=ES[:], op=ALU.mult), reads=[("ps", 0), "rots"], writes=["t256"])
                P.op("dve", lambda e: e.tensor_tensor(out=SR[:, 1], in0=SR[:, 1], in1=t256[:], op=ALU.subtract), writes=["SR"])
                rho8 = APW[:, 0, pr, 8:9]
                for ri in range(2):
                    P.op("dve", lambda e, ri=ri, pr=pr: e.tensor_tensor_scan(out=Wt[:, ri], data0=kar[:, pr, 8:9].broadcast_to([128, 256]), data1=SR[:, ri],
                                                                           initial=0.0, op0=ALU.mult, op1=ALU.add), reads=["SR", "kar"], writes=["Wt"])
                P.op("dve", lambda e: e.tensor_tensor(out=XS[:, 0, 0:256], in0=Wt[:, 0], in1=EC[:], op=ALU.mult), reads=["Wt", "rotc"], writes=["XS"])
                P.op("dve", lambda e: e.tensor_tensor(out=t256[:], in0=Wt[:, 1], in1=ES[:], op=ALU.mult), reads=["Wt", "rots"], writes=["t256"])
                P.op("dve", lambda e: e.tensor_tensor(out=XS[:, 0, 0:256], in0=XS[:, 0, 0:256], in1=t256[:], op=ALU.subtract), writes=["XS"])
                P.op("dve", lambda e: e.tensor_tensor(out=XS[:, 1, 0:256], in0=Wt[:, 1], in1=EC[:], op=ALU.mult), reads=["Wt", "rotc"], writes=["XS"])
                P.op("dve", lambda e: e.tensor_tensor(out=t256[:], in0=Wt[:, 0], in1=ES[:], op=ALU.mult), writes=["t256"])
                P.op("dve", lambda e: e.tensor_tensor(out=XS[:, 1, 0:256], in0=XS[:, 1, 0:256], in1=t256[:], op=ALU.add), writes=["XS"])
                a8r, a8i = APW[:, 0, pr, 8:9], APW[:, 1, pr, 8:9]
                P.op("dve", lambda e, pr=pr, a8r=a8r: e.tensor_scalar(out=XS[:, 0, 256:272], in0=X0[:, 0, pr, :], scalar1=a8r, scalar2=None, op0=ALU.mult),
                     reads=["X0", "APW"], writes=["XS"])
                P.op("dve", lambda e, pr=pr: e.scalar_tensor_tensor(out=XS[:, 0, 256:272], in0=X0[:, 1, pr, :], scalar=na8i[:, pr:pr + 1], in1=XS[:, 0, 256:272],
                                                                    op0=ALU.mult, op1=ALU.add), reads=["X0", "na8i"], writes=["XS"])
                P.op("dve", lambda e: e.tensor_tensor(out=XS[:, 0, 256:272], in0=C.ps[0][:, 256:272], in1=XS[:, 0, 256:272], op=ALU.add), reads=[("ps", 0)], writes=["XS"])
                P.op("dve", lambda e, pr=pr, a8r=a8r: e.tensor_scalar(out=XS[:, 1, 256:272], in0=X0[:, 1, pr, :], scalar1=a8r, scalar2=None, op0=ALU.mult),
                     reads=["X0", "APW"], writes=["XS"])
                P.op("dve", lambda e, pr=pr, a8i=a8i: e.scalar_tensor_tensor(out=XS[:, 1, 256:272], in0=X0[:, 0, pr, :], scalar=a8i, in1=XS[:, 1, 256:272],
                                                                             op0=ALU.mult, op1=ALU.add), reads=["X0", "APW"], writes=["XS"])
                P.op("dve", lambda e: e.tensor_tensor(out=XS[:, 1, 256:272], in0=C.ps[1][:, 256:272], in1=XS[:, 1, 256:272], op=ALU.add), reads=[("ps", 1)], writes=["XS"])
                for ri in range(2):
                    P.op("act", lambda e, ri=ri, pl=pl: e.activation(out=Xprev[:, ri, pl, 1:256], in_=XS[:, ri, 0:255], func=AF.Copy), reads=["XS"], writes=[("Xprev", pl)])
                    P.op("act", lambda e, ri=ri, pr=pr, pl=pl: e.activation(out=Xprev[:, ri, pl, 256:272], in_=X0[:, ri, pr, :], func=AF.Copy), reads=["X0"], writes=[("Xprev", pl)])
                    P.op("act", lambda e, ri=ri, pr=pr: e.activation(out=XF[:, ri, pr:pr + 1], in_=XS[:, ri, 255:256], func=AF.Copy), reads=["XS"], writes=["XF"])
                    P.op("act", lambda e, ri=ri, pr=pr: e.activation(out=XFs[:, ri, pr, :], in_=XS[:, ri, 256:272], func=AF.Copy), reads=["XS"], writes=["XFs"])
                if pr == 0:
                    dump("ssm_XS0", XS[:], [128, 2, 272], ["XS"])
                    dump("ssm_SR0", SR[:], [128, 2, 256], ["SR"])
                    dump("ssm_EC0", EC[:], [128, 256], ["rotc"])
                    dump("ssm_ES0", ES[:], [128, 256], ["rots"])
                    dump("ssm_SMt0", SMt[:], [128, 16, 128], ["SMt"])
                apr9 = APW[:, 0, pr, :].unsqueeze(2).broadcast_to([128, 9, 16])
                api9 = APW[:, 1, pr, :].unsqueeze(2).broadcast_to([128, 9, 16])
                ctr = CT[:, 0, pr, :].unsqueeze(1).broadcast_to([128, 9, 16])
                cti = CT[:, 1, pr, :].unsqueeze(1).broadcast_to([128, 9, 16])
                P.op("dve", lambda e, apr9=apr9, ctr=ctr: e.tensor_tensor(out=CA[:, 0], in0=apr9, in1=ctr, op=ALU.mult), reads=["APW", "CT"], writes=["CA"])
                P.op("dve", lambda e, api9=api9, cti=cti: e.tensor_tensor(out=tmpa[:], in0=api9, in1=cti, op=ALU.mult), reads=["APW", "CT"], writes=["tmpa"])
                P.op("dve", lambda e: e.tensor_tensor(out=CA[:, 0], in0=CA[:, 0], in1=tmpa[:], op=ALU.subtract), writes=["CA"])
                P.op("dve", lambda e, api9=api9, ctr=ctr: e.tensor_tensor(out=CA[:, 1], in0=api9, in1=ctr, op=ALU.mult), reads=["APW", "CT"], writes=["CA"])
                P.op("dve", lambda e, apr9=apr9, cti=cti: e.tensor_tensor(out=tmpa[:], in0=apr9, in1=cti, op=ALU.mult), writes=["tmpa"])
                P.op("dve", lambda e: e.tensor_tensor(out=CA[:, 1], in0=CA[:, 1], in1=tmpa[:], op=ALU.add), writes=["CA"])
                for ri in range(2):
                    for g2 in range(2):
                        P.op("dve", lambda e, ri=ri, g2=g2, pl=pl, q=q: e.tensor_scalar(
                            out=CAx[:, pl, :, ri, 32 * q + 16 * g2:32 * q + 16 * g2 + 16], in0=CA[:, ri], scalar1=C.gmask[:, g2:g2 + 1],
                            scalar2=(1.0 if ri == 0 else -1.0), op0=ALU.mult, op1=ALU.mult),
                            reads=["CA", "gmask", "CAx"], writes=[("CAx", pl)])
                for ri in range(2):
                    P.op("pe", lambda e, ri=ri, bs=bs, pl=pl, q=q, po=po: e.matmul(
                        C.ps[2][po:po + 64, 0:256].rearrange("p (t c) -> p t c", t=8), lhsT=BAx[:, bs, 0, ri, :],
                        rhs=CAx[:, pl, 0:8, ri, 32 * q:32 * q + 32], start=(ri == 0), stop=(ri == 1)),
                        reads=[("BAx", bs), ("CAx", pl)], writes=[("ps", 2)])
                P.op("act", lambda e, pl=pl, po=po: e.activation(
                    out=Mx[po:po + 64, :, 32 * pl:32 * pl + 32], in_=C.ps[2][po:po + 64, 0:256].rearrange("p (t c) -> p t c", t=8), func=AF.Copy),
                    reads=[("ps", 2), "Mx"], writes=[("Mxp", pl)])
                if pl < 3:
                    continue
                P.op("dve", lambda e, hf=hf: e.scalar_tensor_tensor(out=Mx[:, 0, :], in0=C.ident_f[:], scalar=dcol[:, hf:hf + 1], in1=Mx[:, 0, :],
                                                                    op0=ALU.mult, op1=ALU.add),
                     reads=["ident_f", "dcol"] + [("Mxp", k) for k in range(4)], writes=["Mxd"])
                if hf == 0:
                    dump("ssm_Mx0", Mx[:], [128, 8, 128], ["Mxd"])
                    dump("ssm_Xprev0", Xprev[:], [128, 2, 4, 272], [("Xprev", k) for k in range(4)] + ["Xprev0"])
                udv = udT[:, hf, :].rearrange("p (c s) -> p c s", s=8)
                ygv = ygT[:, hf, :].rearrange("p (c s) -> p c s", s=8)
                for t in range(8):
                    bk = 4 + (ity % 2)
                    ity += 1
                    Y = C.ps[bk]
                    for s in range(t + 1):
                        P.op("pe", lambda e, Y=Y, t=t, s=s, udv=udv: e.matmul(
                            Y[:, 0:272], lhsT=Mx[:, t - s, :], rhs=udv[:, :, s], start=(s == 0), stop=False, skip_group_check=True),
                            reads=["Mxd", "udT"], writes=[("ps", bk)])
                    for pl2 in range(4):
                        po2 = 64 * (pl2 // 2)
                        for ri in range(2):
                            P.op("pe", lambda e, Y=Y, pl2=pl2, po2=po2, ri=ri, t=t: e.matmul(
                                Y[po2:po2 + 64, 0:272], lhsT=CAx[:, pl2, t + 1, ri, :], rhs=Xprev[:, ri, pl2, :], start=False, stop=False, skip_group_check=True),
                                reads=[("CAx", pl2), ("Xprev", pl2), "Xprev0"], writes=[("ps", bk)])
                    P.op("act", lambda e, Y=Y, ygv=ygv, t=t: e.activation(out=ygv[:, :, t], in_=Y[:, 0:272], func=AF.Gelu_apprx_tanh),
                         reads=[("ps", bk)], writes=["ygT"])
            dump("ssm_ygT", ygT[:], [128, 2, NTOK], ["ygT"])
            sg = sb("sg", [128, 2, 512], BF16, st)
            it = 0
            for (c0, c1) in TG:
                n = c1 - c0
                for vc in range(2):
                    s2 = it % 2
                    it += 1
                    bv_, bg_ = (0, 1) if s2 == 0 else (2, 3)
                    for kc in range(2):
                        P.op("pe", lambda e, kc=kc, vc=vc, bv_=bv_: e.matmul(C.ps[bv_][:, 0:n], lhsT=Wg[:, kc, vc * 128:(vc + 1) * 128], rhs=ygT[:, kc, c0:c1],
                                                                           start=(kc == 0), stop=(kc == 1)), reads=["ygT", wkey], writes=[("ps", bv_)])
                    for kc in range(2):
                        P.op("pe", lambda e, kc=kc, vc=vc, bg_=bg_: e.matmul(C.ps[bg_][:, 0:n], lhsT=Wg[:, kc, 256 + vc * 128:256 + (vc + 1) * 128], rhs=ygT[:, kc, c0:c1],
                                                                           start=(kc == 0), stop=(kc == 1)), reads=["ygT", wkey], writes=[("ps", bg_)])
                    P.op("act", lambda e, s2=s2, bg_=bg_: e.activation(out=sg[:, s2, 0:n], in_=C.ps[bg_][:, 0:n], func=AF.Sigmoid),
                         reads=[("ps", bg_)], writes=[("sg", s2)])
                    P.op("dve", lambda e, s2=s2, bv_=bv_, vc=vc: e.tensor_tensor(out=C.odT[:, vc, c0:c1], in0=C.ps[bv_][:, 0:n], in1=sg[:, s2, 0:n], op=ALU.mult),
                         reads=[("ps", bv_), ("sg", s2)], writes=[("odT", vc, c0), "udT"])
            xfo = sb("xfo", [8, 2, 128], F32, st)
            xfs = sb("xfs", [16, 1024], F32, st)
            for ri in range(2):
                P.op("pe", lambda e, ri=ri: e.transpose(C.ps[6][0:8, 128 * ri:128 * ri + 128], XF[:, ri, :], C.ident_f[:]),
                     reads=["XF", "ident_f"], writes=[("ps", 6)])
            P.op("dve", lambda e: e.tensor_copy(out=xfo[:].rearrange("p r n -> p (r n)"), in_=C.ps[6][0:8, 0:256]), reads=[("ps", 6)], writes=["xfo"])
            P.dma("sp", O["dre_p"][L].rearrange("(a n) -> a n", n=128), xfo[:, 0, :], reads=["xfo"])
            P.dma("sp", O["dim_p"][L].rearrange("(a n) -> a n", n=128), xfo[:, 1, :], reads=["xfo"])
            for ri in range(2):
                for hb in range(2):
                    for k4 in range(4):
                        pr = hb * 4 + k4
                        P.op("pe", lambda e, ri=ri, pr=pr, k4=k4: e.transpose(C.ps[7][0:16, 128 * k4:128 * k4 + 128], XFs[:, ri, pr, :], C.ident_f[:]),
                             reads=["XFs", "ident_f"], writes=[("ps", 7)])
                    P.op("dve", lambda e, ri=ri, hb=hb: e.tensor_copy(out=xfs[:, 512 * hb:512 * hb + 512], in_=C.ps[7][0:16, 0:512]),
                         reads=[("ps", 7)], writes=["xfs"])
                P.dma("sp", O["dre_s" if ri == 0 else "dim_s"][L], xfs[:, :], reads=["xfs"])

        HALVES = [[(0, 512), (512, 1024)], [(1024, 1536), (1536, 2048), (2048, 2176)]]

        def finalize_A(st):
            rd = sb("rdA", [128, 2, 128], F32, st)
            nf = sb("nfA", [128, 2, 128], F32, st)
            for j in range(NT):
                nv = C.numA[:, :, j * 128:(j + 1) * 128]
                dv = C.denA[:, :, j * 128:(j + 1) * 128]
                P.op("dve", lambda e, dv=dv: e.tensor_copy(out=rd[:], in_=dv), reads=[("denA", j)], writes=["rdA"])
                P.op("dve", lambda e: e.reciprocal(out=rd[:], in_=rd[:]), reads=["rdA"], writes=["rdA"])
                P.op("dve", lambda e, nv=nv: e.tensor_copy(out=nf[:], in_=nv), reads=[("numA", j)], writes=["nfA"])
                P.op("dve", lambda e: e.tensor_tensor(out=nf[:], in0=nf[:], in1=rd[:], op=ALU.mult), reads=["nfA", "rdA"], writes=["nfA"])
                P.op("dve", lambda e, nv=nv: e.tensor_copy(out=nv, in_=nf[:]), reads=["nfA"], writes=[("numA", j)])

        def merge_half(L, H, st):
            groups = HALVES[H]
            hc0 = groups[0][0]
            ncol = groups[-1][1] - hc0
            mT = sb("mergedT", [128, 8, ncol], BF16, st)
            sgm = sb("sgm", [128, 2, 512], BF16, st)
            tmpm = sb("tmpm", [128, 2, 512], BF16, st)
            WR = WRing(st, 2, 6144, "M")
            srcs = [(C.numA, 2, "numA"), (C.obT, 4, "obT"), (C.ocT, 2, "ocT"), (C.odT, 2, "odT")]
            brw = ["w_br_a", "w_br_b", "w_br_c", "w_br_d"]
            it = 0
            for fcg in range(2):
                for bi_, i in enumerate(DBG_M_BR):
                    oT, kci, _ = srcs[i]
                    ws = WR.next()
                    WR.load(ws, [(0, 8, 512, I["w_in"][L][:, GATE_OFF + 1024 * i + 512 * fcg:GATE_OFF + 1024 * i + 512 * fcg + 512]),
                                 (4096, kci, 512, I[brw[i]][L][:, 512 * fcg:512 * fcg + 512])])
                    Wg_, Wb_ = WR.view(ws, 0, 8, 512), WR.view(ws, 4096, kci, 512)
                    wkey = WR.key(ws)
                    for (c0, c1) in groups:
                        n = c1 - c0
                        tl = tg_tiles(c0, c1)
                        hkeys = [("hT", j) for j in tl]
                        for fc in range(4):
                            s2 = it % 2
                            it += 1
                            bg_, bp_ = (0, 1) if s2 == 0 else (2, 3)
                            for kc in range(8):
                                P.op("pe", lambda e, kc=kc, fc=fc, bg_=bg_, Wg_=Wg_, c0=c0, c1=c1, n=n: e.matmul(
                                    C.ps[bg_][:, 0:n], lhsT=Wg_[:, kc, fc * 128:(fc + 1) * 128], rhs=C.hT[:, kc, c0:c1], start=(kc == 0), stop=(kc == 7)),
                                    reads=hkeys + [wkey], writes=[("ps", bg_)])
                            for kc in range(kci):
                                P.op("pe", lambda e, kc=kc, fc=fc, bp_=bp_, Wb_=Wb_, oT=oT, c0=c0, c1=c1, n=n, kci=kci: e.matmul(
                                    C.ps[bp_][:, 0:n], lhsT=Wb_[:, kc, fc * 128:(fc + 1) * 128], rhs=oT[:, kc, c0:c1], start=(kc == 0), stop=(kc == kci - 1)),
                                    reads=["mix_out", wkey], writes=[("ps", bp_)])
                            P.op("act", lambda e, s2=s2, bg_=bg_, n=n: e.activation(out=sgm[:, s2, 0:n], in_=C.ps[bg_][:, 0:n], func=AF.Sigmoid),
                                 reads=[("ps", bg_)], writes=[("sgm", s2)])
                            mv = mT[:, 4 * fcg + fc, c0 - hc0:c1 - hc0]
                            mk = ("mT", 4 * fcg + fc, c0)
                            if bi_ == 0:
                                P.op("dve", lambda e, mv=mv, bp_=bp_, s2=s2, n=n: e.tensor_tensor(out=mv, in0=C.ps[bp_][:, 0:n], in1=sgm[:, s2, 0:n], op=ALU.mult),
                                     reads=[("ps", bp_), ("sgm", s2)], writes=[mk])
                            else:
                                P.op("dve", lambda e, bp_=bp_, s2=s2, n=n: e.tensor_tensor(out=tmpm[:, s2, 0:n], in0=C.ps[bp_][:, 0:n], in1=sgm[:, s2, 0:n], op=ALU.mult),
                                     reads=[("ps", bp_), ("sgm", s2)], writes=[("tmpm", s2)])
                                P.op("dve", lambda e, mv=mv, s2=s2, n=n: e.tensor_tensor(out=mv, in0=tmpm[:, s2, 0:n], in1=mv, op=ALU.add),
                                     reads=[mk, ("tmpm", s2)], writes=[mk])
            if H == 0:
                dump("mT_%d" % L, mT[:], [128, 8, ncol], [("mT", f, c0) for f in range(8) for (c0, c1) in groups])
            P.dma("sp", C.gvec[:, 0, :], I["n_mix_post"][L].partition_broadcast(128), writes=[("g", 0)])
            wsl = []
            for nh in range(2):
                ws = WR.next()
                WR.load(ws, [(0, 8, 512, I["w_out"][L][:, 512 * nh:512 * nh + 512])])
                wsl.append((WR.view(ws, 0, 8, 512), WR.key(ws)))
            mkeys = [("mT", f, c0) for f in range(8) for (c0, c1) in groups]
            tiles = [j for (c0, c1) in groups for j in tg_tiles(c0, c1)]
            post_norm_residual(tiles, lambda kc, j: mT[:, kc, j * 128 - hc0:j * 128 - hc0 + 128], 8, wsl, mkeys, st)

        def post_norm_residual(tiles, lhs_fn, nk, wsl, in_keys, st):
            ss2 = sb("ss2", [128, 8], F32, st)
            junk = sb("junk_p", [128, 512], BF16, st)
            tmpx = sb("tmpx", [128, 2, 512], F32, st)
            for j in tiles:
                s2 = j % 2
                banks = (0, 1) if s2 == 0 else (2, 3)
                for nh in range(2):
                    Wv, wk = wsl[nh]
                    for kc in range(nk):
                        P.op("pe", lambda e, kc=kc, j=j, b_=banks[nh], Wv=Wv: e.matmul(
                            C.ps[b_][:, :], lhsT=lhs_fn(kc, j), rhs=Wv[:, kc, :], start=(kc == 0), stop=(kc == nk - 1)),
                            reads=in_keys + [wk], writes=[("ps", banks[nh])])
                    P.op("act", lambda e, b_=banks[nh], nh=nh: e.activation(out=junk[:], in_=C.ps[b_][:, :], func=AF.Square, accum_out=ss2[:, nh:nh + 1]),
                         reads=[("ps", banks[nh])], writes=["junk_p", "ss2"])
                P.op("dve", lambda e: e.tensor_scalar(out=ss2[:, 2:4], in0=ss2[:, 0:2], scalar1=ss2[:, 1:2], scalar2=None, op0=ALU.add), reads=["ss2"], writes=["ss2b"])
                P.op("dve", lambda e: e.tensor_scalar(out=ss2[:, 2:4], in0=ss2[:, 2:4], scalar1=1.0 / D, scalar2=EPS, op0=ALU.mult, op1=ALU.add),
                     reads=["ss2b"], writes=["ss2b"])
                P.op("act", lambda e: e.activation(out=ss2[:, 2:4], in_=ss2[:, 2:4], func=AF.Sqrt), reads=["ss2b"], writes=["ss2b"])
                P.op("dve", lambda e: e.reciprocal(out=ss2[:, 4:6], in_=ss2[:, 2:4]), reads=["ss2b"], writes=["ss2c"])
                for nh in range(2):
                    P.op("dve", lambda e, b_=banks[nh], nh=nh: e.scalar_tensor_tensor(
                        out=tmpx[:, nh, :], in0=C.ps[b_][:, :], scalar=ss2[:, 4:5], in1=C.gvec[:, 0, 512 * nh:512 * nh + 512], op0=ALU.mult, op1=ALU.mult),
                        reads=[("ps", banks[nh]), "ss2c", ("g", 0)], writes=[("tmpx", nh)])
                    P.op("pool", lambda e, j=j, nh=nh: e.tensor_tensor(out=C.X[:, j, 512 * nh:512 * nh + 512], in0=C.X[:, j, 512 * nh:512 * nh + 512],
                                                                     in1=tmpx[:, nh, :], op=ALU.add),
                         reads=[("X", j), ("tmpx", nh)], writes=[("X", j)])

        def ffn_half(L, H, st):
            groups = HALVES[H]
            hc0 = groups[0][0]
            ncol = groups[-1][1] - hc0
            tiles = [j for (c0, c1) in groups for j in tg_tiles(c0, c1)]
            hTh = sb("hTh", [128, 8, ncol], BF16, st)
            aT = sb("aT", [128, NFF, ncol], BF16, st)
            sil = sb("sil", [128, 2, 512], BF16, st)
            norm_to_hT(L, "n_ffn_pre", tiles, st, hTh, hc0)
            WR = WRing(st, 2, 6144, "F")
            it = 0
            for ffg in range(NFF // 2):
                ws = WR.next()
                WR.load(ws, [(0, 8, 256, I["w_ffn_gate"][L][:, 256 * ffg:256 * ffg + 256]),
                             (2048, 8, 256, I["w_ffn_up"][L][:, 256 * ffg:256 * ffg + 256])])
                Wg_, Wu_ = WR.view(ws, 0, 8, 256), WR.view(ws, 2048, 8, 256)
                wkey = WR.key(ws)
                for (c0, c1) in groups:
                    n = c1 - c0
                    hkeys = [("hT", j) for j in tg_tiles(c0, c1)]
                    for fc in range(2):
                        s2 = it % 2
                        it += 1
                        bg_, bu_ = (0, 1) if s2 == 0 else (2, 3)
                        for kc in range(8):
                            P.op("pe", lambda e, kc=kc, fc=fc, bg_=bg_, Wg_=Wg_, c0=c0, c1=c1, n=n: e.matmul(
                                C.ps[bg_][:, 0:n], lhsT=Wg_[:, kc, fc * 128:(fc + 1) * 128], rhs=hTh[:, kc, c0 - hc0:c1 - hc0], start=(kc == 0), stop=(kc == 7)),
                                reads=hkeys + [wkey], writes=[("ps", bg_)])
                        for kc in range(8):
                            P.op("pe", lambda e, kc=kc, fc=fc, bu_=bu_, Wu_=Wu_, c0=c0, c1=c1, n=n: e.matmul(
                                C.ps[bu_][:, 0:n], lhsT=Wu_[:, kc, fc * 128:(fc + 1) * 128], rhs=hTh[:, kc, c0 - hc0:c1 - hc0], start=(kc == 0), stop=(kc == 7)),
                                reads=hkeys + [wkey], writes=[("ps", bu_)])
                        P.op("act", lambda e, s2=s2, bg_=bg_, n=n: e.activation(out=sil[:, s2, 0:n], in_=C.ps[bg_][:, 0:n], func=AF.Silu),
                             reads=[("ps", bg_)], writes=[("sil", s2)])
                        P.op("dve", lambda e, s2=s2, bu_=bu_, n=n, ffc=2 * ffg + fc, c0=c0, c1=c1: e.tensor_tensor(
                            out=aT[:, ffc, c0 - hc0:c1 - hc0], in0=C.ps[bu_][:, 0:n], in1=sil[:, s2, 0:n], op=ALU.mult),
                            reads=[("ps", bu_), ("sil", s2)], writes=[("aT", 2 * ffg + fc, c0)])
            akeys = {c0: [("aT", f, c0) for f in range(NFF)] for (c0, c1) in groups}
            it = 0
            for fcg in range(4):
                ws = WR.next()
                WR.load(ws, [(0, NFF, 256, I["w_ffn_down"][L][:, 256 * fcg:256 * fcg + 256])])
                Wd_ = WR.view(ws, 0, NFF, 256)
                wkey = WR.key(ws)
                for (c0, c1) in groups:
                    n = c1 - c0
                    for fc in range(2):
                        bk = 4 + (it % 2)
                        it += 1
                        for ffc in range(NFF):
                            P.op("pe", lambda e, ffc=ffc, fc=fc, bk=bk, Wd_=Wd_, c0=c0, c1=c1, n=n: e.matmul(
                                C.ps[bk][:, 0:n], lhsT=Wd_[:, ffc, fc * 128:(fc + 1) * 128], rhs=aT[:, ffc, c0 - hc0:c1 - hc0],
                                start=(ffc == 0), stop=(ffc == NFF - 1)), reads=akeys[c0] + [wkey], writes=[("ps", bk)])
                        P.op("act", lambda e, bk=bk, n=n, f=2 * fcg + fc, c0=c0, c1=c1: e.activation(
                            out=hTh[:, f, c0 - hc0:c1 - hc0], in_=C.ps[bk][:, 0:n], func=AF.Copy),
                            reads=[("ps", bk)] + [("hT", j) for j in tg_tiles(c0, c1)], writes=[("fT", 2 * fcg + fc, c0)])
            P.dma("sp", C.gvec[:, 0, :], I["n_ffn_post"][L].partition_broadcast(128), writes=[("g", 0)])
            ss2 = sb("ss2f", [128, 8], F32, st)
            P.op("dve", lambda e: e.memset(ss2[:], 1.0), writes=["ss2f"])
            junk = sb("junk_f", [128, D], BF16, st)
            tmpx = sb("tmpxf", [128, D], F32, st)
            for j in tiles:
                bank = 6 + (j % 2)
                pv = psb(bank)
                c0g = [c0 for (c0, c1) in groups if c0 <= j * 128 < c1][0]
                lc = j * 128 - hc0
                for f in range(8):
                    P.op("pe", lambda e, f=f, lc=lc, pv=pv: e.transpose(pv[:, f * 128:(f + 1) * 128], hTh[:, f, lc:lc + 128], C.ident_b[:]),
                         reads=[("fT", f, c0g), "ident_b"], writes=[("ps", bank)])
                P.op("act", lambda e, pv=pv: e.activation(out=junk[:], in_=pv, func=AF.Square, accum_out=ss2[:, 0:1]),
                     reads=[("ps", bank)], writes=["junk_f", "ss2f"])
                P.op("dve", lambda e: e.tensor_scalar(out=ss2[:, 2:4], in0=ss2[:, 0:2], scalar1=1.0 / D, scalar2=EPS, op0=ALU.mult, op1=ALU.add),
                     reads=["ss2f"], writes=["ss2f2"])
                P.op("act", lambda e: e.activation(out=ss2[:, 2:4], in_=ss2[:, 2:4], func=AF.Sqrt), reads=["ss2f2"], writes=["ss2f2"])
                P.op("dve", lambda e: e.reciprocal(out=ss2[:, 4:6], in_=ss2[:, 2:4]), reads=["ss2f2"], writes=["ss2g"])
                P.op("act", lambda e, pv=pv: e.activation(out=tmpx[:], in_=pv, func=AF.Copy), reads=[("ps", bank)], writes=["tmpxf"])
                P.op("dve", lambda e: e.scalar_tensor_tensor(out=tmpx[:], in0=tmpx[:], scalar=ss2[:, 4:5], in1=C.gvec[:, 0, :], op0=ALU.mult, op1=ALU.mult),
                     reads=["tmpxf", "ss2g", ("g", 0)], writes=["tmpxf"])
                P.op("pool", lambda e, j=j: e.tensor_tensor(out=C.X[:, j, :], in0=C.X[:, j, :], in1=tmpx[:], op=ALU.add),
                     reads=[("X", j), "tmpxf"], writes=[("X", j)])

        def ple_half(L, H, st, last):
            groups = HALVES[H]
            hc0 = groups[0][0]
            ncol = groups[-1][1] - hc0
            tiles = [j for (c0, c1) in groups for j in tg_tiles(c0, c1)]
            WR = WRing(st, 2, 6144, "P")
            wsl = []
            for nh in range(2):
                ws = WR.next()
                pieces = [(0, 8, 512, I["w_ple_gate"][L][:, 512 * nh:512 * nh + 512])]
                if nh == 0:
                    pieces.append((4096, 2, 1024, I["w_ple"][L]))
                WR.load(ws, pieces)
                wsl.append((WR.view(ws, 0, 8, 512), WR.key(ws)))
            Wp = WR.view(0, 4096, 2, 1024)
            wpk = WR.key(0)
            xb = sb("xb", [128, 2, D], BF16, st)
            xT = sb("xT", [128, 2, 8, 128], BF16, st)
            pt = sb("pt", [128, 2, 256], BF16, st)
            pT = sb("pT", [128, 2, 2, 128], BF16, st)
            sgp = sb("sgp", [128, 2, 512], F32, st)
            tmpp = sb("tmpp", [128, 2, 512], F32, st)
            for j in tiles:
                s2 = j % 2
                P.op("act", lambda e, j=j, s2=s2: e.activation(out=xb[:, s2, :], in_=C.X[:, j, :], func=AF.Copy), reads=[("X", j)], writes=[("xb", s2)])
                pv = psb(6 + s2)
                for c in range(8):
                    P.op("pe", lambda e, c=c, s2=s2, pv=pv: e.transpose(pv[:, c * 128:(c + 1) * 128], xb[:, s2, c * 128:(c + 1) * 128], C.ident_b[:]),
                         reads=[("xb", s2), "ident_b"], writes=[("ps", 6 + s2)])
                P.op("act", lambda e, s2=s2, pv=pv: e.activation(out=xT[:, s2], in_=pv.rearrange("p (c t) -> p c t", c=8), func=AF.Copy),
                     reads=[("ps", 6 + s2)], writes=[("xT", s2)])
                psrc = I["pp"][L, 128 * j:128 * (j + 1), :] if j < 16 else I["psm"][L]
                P.dma("pool", pt[:, s2, :], psrc, writes=[("pt", s2)])
                pv2 = psb(4 + s2)
                for c in range(2):
                    P.op("pe", lambda e, c=c, s2=s2, pv2=pv2: e.transpose(pv2[:, c * 128:(c + 1) * 128], pt[:, s2, c * 128:(c + 1) * 128], C.ident_b[:]),
                         reads=[("pt", s2), "ident_b"], writes=[("ps", 4 + s2)])
                P.op("act", lambda e, s2=s2, pv2=pv2: e.activation(out=pT[:, s2], in_=pv2[:, 0:256].rearrange("p (c t) -> p c t", c=2), func=AF.Copy),
                     reads=[("ps", 4 + s2)], writes=[("pT", s2)])
                for nh in range(2):
                    Wv, wk = wsl[nh]
                    bg_, bp_ = (0, 1) if nh == 0 else (2, 3)
                    for kc in range(8):
                        P.op("pe", lambda e, kc=kc, s2=s2, bg_=bg_, Wv=Wv: e.matmul(C.ps[bg_][:, :], lhsT=xT[:, s2, kc, :], rhs=Wv[:, kc, :], start=(kc == 0), stop=(kc == 7)),
                             reads=[("xT", s2), wk], writes=[("ps", bg_)])
                    for kc in range(2):
                        P.op("pe", lambda e, kc=kc, s2=s2, bp_=bp_, nh=nh: e.matmul(C.ps[bp_][:, :], lhsT=pT[:, s2, kc, :], rhs=Wp[:, kc, 512 * nh:512 * nh + 512],
                                                                                start=(kc == 0), stop=(kc == 1)),
                             reads=[("pT", s2), wpk], writes=[("ps", bp_)])
                    P.op("act", lambda e, nh=nh, bg_=bg_: e.activation(out=sgp[:, nh, :], in_=C.ps[bg_][:, :], func=AF.Sigmoid), reads=[("ps", bg_)], writes=[("sgp", nh)])
                    P.op("dve", lambda e, nh=nh, bp_=bp_: e.tensor_tensor(out=tmpp[:, nh, :], in0=C.ps[bp_][:, :], in1=sgp[:, nh, :], op=ALU.mult),
                         reads=[("ps", bp_), ("sgp", nh)], writes=[("tmpp", nh)])
                    P.op("pool", lambda e, j=j, nh=nh: e.tensor_tensor(out=C.X[:, j, 512 * nh:512 * nh + 512], in0=C.X[:, j, 512 * nh:512 * nh + 512],
                                                                     in1=tmpp[:, nh, :], op=ALU.add),
                         reads=[("X", j), ("tmpp", nh)], writes=[("X", j)])
                if last:
                    if j < 16:
                        P.dma("sp", O["yp"][128 * j:128 * (j + 1), :], C.X[:, j, :], reads=[("X", j)])
                    else:
                        P.dma("sp", O["ys"], C.X[:, j, :], reads=[("X", j)])

        stages = []
        for L in range(depth):
            with ExitStack() as lay:
                C.hT = sb("hT", [128, 8, NTOK], BF16, lay)
                C.odT = sb("odT", [128, 2, NTOK], BF16, lay)
                P.mark("norm1_%d" % L)
                with ExitStack() as st:
                    norm_to_hT(L, "n_mix_pre", list(range(NT)), st, C.hT, 0)
                barrier()
                dump("hT_%d" % L, C.hT[:], [128, 8, NTOK], [("hT", j) for j in range(NT)])
                if stop_after == ("norm", L):
                    break
                P.mark("D_%d" % L)
                with ExitStack() as st:
                    mixer_D(L, st)
                barrier()
                dump("odT_%d" % L, C.odT[:], [128, 2, NTOK], [("odT", v, c0) for v in range(2) for (c0, c1) in TG])
                if stop_after == ("D", L):
                    break
                P.mark("A_%d" % L)
                C.numA = sb("numA", [128, 2, NTOK], BF16, lay)
                with ExitStack() as att:
                    load_att_consts(att)
                    with ExitStack() as st:
                        C.denA = sb("denA", [128, 2, NTOK], BF16, st)
                        for g in range(3):
                            if g not in DBG_A_GROUPS:
                                continue
                            with ExitStack() as st2:
                                mixer_A(L, g, st2)
                            barrier()
                        dump("numA_%d" % L, C.numA[:], [128, 2, NTOK], [("numA", j) for j in range(NT)])
                        dump("denA_%d" % L, C.denA[:], [128, 2, NTOK], [("denA", j) for j in range(NT)])
                        finalize_A(st)
                    barrier()
                dump("oaT_%d" % L, C.numA[:], [128, 2, NTOK], [("numA", j) for j in range(NT)])
                if stop_after == ("A", L):
                    break
                P.mark("B_%d" % L)
                C.obT = sb("obT", [128, 4, NTOK], BF16, lay)
                with ExitStack() as att:
                    load_att_consts(att)
                    with ExitStack() as st:
                        mixer_B(L, st)
                    barrier()
                dump("obT_%d" % L, C.obT[:], [128, 4, NTOK], [("obT", j) for j in range(NT)])
                dump("oaTb_%d" % L, C.numA[:], [128, 2, NTOK], [("numA", j) for j in range(NT)])
                if stop_after == ("B", L):
                    break
                P.mark("C_%d" % L)
                C.ocT = sb("ocT", [128, 2, NTOK], BF16, lay)
                with ExitStack() as st:
                    mixer_C(L, st)
                barrier()
                dump("ocT_%d" % L, C.ocT[:], [128, 2, NTOK], [("ocT", ch, c0) for ch in range(2) for (c0, c1) in TG])
                dump("oaTc_%d" % L, C.numA[:], [128, 2, NTOK], [("numA", j) for j in range(NT)])
                if stop_after == ("C", L):
                    break
                P.op("dve", lambda e: e.engine_nop(), reads=[("numA", j) for j in range(NT)] + [("obT", j) for j in range(NT)]
                     + [("ocT", ch, c0) for ch in range(2) for (c0, c1) in TG] + [("odT", v, c0) for v in range(2) for (c0, c1) in TG], writes=["mix_out"])
                P.mark("merge_%d" % L)
                for H in range(2):
                    with ExitStack() as st:
                        merge_half(L, H, st)
                    barrier()
                dump("x1_%d" % L, C.X[:], [128, NT, D], [("X", j) for j in range(NT)])
            barrier()
            if stop_after == ("merge", L):
                break
            P.mark("ffn_%d" % L)
            for H in range(2):
                with ExitStack() as st:
                    ffn_half(L, H, st)
                barrier()
            dump("x2_%d" % L, C.X[:], [128, NT, D], [("X", j) for j in range(NT)])
            if stop_after == ("ffn", L):
                break
            P.mark("ple_%d" % L)
            for H in range(2):
                with ExitStack() as st:
                    ple_half(L, H, st, last=(L == depth - 1))
                barrier()
        P.mark('end')
        P.emit(es)
    P.min_rem = C.min_rem
    return nc, dump_specs, P


_CACHE = {}


def _get_program(depth=DEPTH, stop_after=None, dumps=()):
    key = (depth, stop_after, tuple(dumps))
    if key not in _CACHE:
        _CACHE[key] = build_program(depth, stop_after, dumps)
    return _CACHE[key]


def _core_inputs(c, a, consts, w_b, w_brb):
    f = np.ascontiguousarray
    sl = slice(16 * c, 16 * c + 16)
    m = dict(
        xp=f(a["x_prompt"][c]), xs=f(a["x_sample"][sl].reshape(128, D)),
        pp=f(a["p_prompt"][:, c]), psm=f(a["p_sample"][:, sl].reshape(DEPTH, 128, 256)),
        ca1k=f(a["cache_a1_k"][:, sl].reshape(DEPTH, 16, 128, 256)), ca1v=f(a["cache_a1_v"][:, sl].reshape(DEPTH, 16, 128, 256)),
        ca2k=f(a["cache_a2_k"][:, sl].reshape(DEPTH, 16, 512, 256)), ca2v=f(a["cache_a2_v"][:, sl].reshape(DEPTH, 16, 512, 256)),
        ca3k=f(a["cache_a3_k"][:, sl].reshape(DEPTH, 16, 2048, 256)), ca3v=f(a["cache_a3_v"][:, sl].reshape(DEPTH, 16, 2048, 256)),
        cbk=f(a["cache_b_k"][:, sl].reshape(DEPTH, 16, 128, 128)), cbv=f(a["cache_b_v"][:, sl].reshape(DEPTH, 16, 128, 128)),
        sconv=f(a["state_c_conv"][:, sl].reshape(DEPTH, 32, 256)),
        sdre=f(a["state_d_re"][:, sl].reshape(DEPTH, 16, 1024)), sdim=f(a["state_d_im"][:, sl].reshape(DEPTH, 16, 1024)),
        n_mix_pre=a["norm_mix_pre"], n_mix_post=a["norm_mix_post"], n_ffn_pre=a["norm_ffn_pre"], n_ffn_post=a["norm_ffn_post"],
        w_in=a["w_in"], w_b=w_b, sinks=a["attn_sinks"], conv_w=a["conv_c_w"],
        lam_re=a["ssm_lam_re"].reshape(DEPTH, 8, 128), lam_im=a["ssm_lam_im"].reshape(DEPTH, 8, 128), log_dt=a["ssm_log_dt"].reshape(DEPTH, 8, 2),
        b_re=a["ssm_b_re"].reshape(DEPTH, 1024, 16), b_im=a["ssm_b_im"].reshape(DEPTH, 1024, 16),
        c_re=a["ssm_c_re"].reshape(DEPTH, 256, 64), c_im=a["ssm_c_im"].reshape(DEPTH, 256, 64),
        ssm_d=a["ssm_d"].reshape(DEPTH, 2, 128), w_d_glu=a["w_d_glu"],
        w_br_a=a["w_br_a"], w_br_b=w_brb, w_br_c=a["w_br_c"], w_br_d=a["w_br_d"],
        w_out=a["w_out"], w_ffn_gate=a["w_ffn_gate"], w_ffn_up=a["w_ffn_up"], w_ffn_down=a["w_ffn_down"],
        w_ple=a["w_ple"], w_ple_gate=a["w_ple_gate"],
    )
    m.update(consts)
    return {k: np.ascontiguousarray(v, dtype=np.float32) for k, v in m.items()}


def _run(inputs, depth=DEPTH, stop_after=None, dumps=(), cores=NCORES):
    a = {k: np.asarray(v) for k, v in inputs.items()}
    nc, dump_specs, _ = _get_program(depth, stop_after, dumps)
    consts = _host_consts()
    w_b, w_brb = _host_weights(a["w_in"], a["w_br_b"])
    in_maps = [_core_inputs(c, a, consts, w_b, w_brb) for c in range(cores)]
    res = run_bass_kernel_spmd(nc, in_maps, core_ids=list(range(cores)))
    return res.results


def kernel(**inputs):
    r = _run(inputs)
    n = NCORES

    def stack_p(name, shape_tail):
        return np.stack([r[c][name] for c in range(n)], axis=1).reshape((DEPTH, n) + shape_tail)

    def cat_s(name, shape_tail):
        return np.concatenate([r[c][name] for c in range(n)], axis=1).reshape((DEPTH, 16 * n) + shape_tail)

    yp = np.stack([r[c]["yp"] for c in range(n)], axis=0)
    ys = np.concatenate([r[c]["ys"] for c in range(n)], axis=0).reshape(16 * n, 8, D)
    outs = [yp, ys]
    for g, w in ((1, 128), (2, 512), (3, 2048)):
        for kv in ("k", "v"):
            outs.append(stack_p("a%d%s_p" % (g, kv), (w, 4, 64)))
            outs.append(cat_s("a%d%s_s" % (g, kv), (w, 4, 64)))
    for kv in ("k", "v"):
        outs.append(stack_p("b%s_p" % kv, (128, 2, 64)))
        outs.append(cat_s("b%s_s" % kv, (128, 2, 64)))
    outs.append(stack_p("conv_p", (2, 256)))
    outs.append(cat_s("conv_s", (2, 256)))
    for nm in ("dre", "dim"):
        outs.append(stack_p(nm + "_p", (16, 64)))
        outs.append(cat_s(nm + "_s", (16, 64)))
    return tuple(np.ascontiguousarray(o, dtype=np.float32) for o in outs)
```
